# Optimizing a Trainium2 kernel written in Bass

```python
import jax, jax.numpy as jnp
from jax import lax
import numpy as np

D_MODEL = 1024
BATCH = 4
SEQ = 8192
DEPTH = 4

CHUNK = 64
WINDOW = 128
WIN_CHUNKS = WINDOW // CHUNK
BAND = (WIN_CHUNKS + 1) * CHUNK
HEAD_DIM = 64
ATTN_WIDTH = D_MODEL // 2
N_Q_HEADS = ATTN_WIDTH // HEAD_DIM
N_KV_HEADS = N_Q_HEADS // 4
KV_WIDTH = N_KV_HEADS * HEAD_DIM
LRU_WIDTH = D_MODEL
LRU_BLOCKS = 16
LRU_BLOCK = LRU_WIDTH // LRU_BLOCKS
CONV_WIDTH = 4
LRU_C = 8.0
N_MEM = 256
MEM_HEADS = 4
MEM_WIDTH = D_MODEL // 2
MEM_HEAD_DIM = MEM_WIDTH // MEM_HEADS
N_BRANCH = 3
D_FF = 11 * D_MODEL // 4
EPS = 1e-6
IN_WIDTH = ATTN_WIDTH + 2 * KV_WIDTH + 2 * LRU_WIDTH + MEM_WIDTH + N_BRANCH * D_MODEL

kernel_name = 'hybrid_swa_rglru_memory_macaron'


def rmsnorm(x, g):
    x32 = x.astype(jnp.float32)
    y = x32 * lax.rsqrt(jnp.mean(x32 * x32, axis=-1, keepdims=True) + EPS)
    return (y * g.astype(jnp.float32)).astype(x.dtype)


def swiglu(h, w_gate, w_up, w_down):
    return (jax.nn.silu(h @ w_gate) * (h @ w_up)) @ w_down


def alibi_slopes(n):
    return jnp.asarray(np.array([2.0 ** (-8.0 * (i + 1) / n) for i in range(n)], dtype=np.float32))


def window_attention(q, k, v, sinks):
    B, S = q.shape[0], q.shape[1]
    nc = S // CHUNK
    G = N_Q_HEADS // N_KV_HEADS
    pad = WIN_CHUNKS * CHUNK
    kp = jnp.pad(k, ((0, 0), (pad, 0), (0, 0), (0, 0))).reshape(B, nc + WIN_CHUNKS, CHUNK, N_KV_HEADS, HEAD_DIM)
    vp = jnp.pad(v, ((0, 0), (pad, 0), (0, 0), (0, 0))).reshape(B, nc + WIN_CHUNKS, CHUNK, N_KV_HEADS, HEAD_DIM)
    kb = jnp.concatenate([kp[:, j:j + nc] for j in range(WIN_CHUNKS + 1)], axis=2)
    vb = jnp.concatenate([vp[:, j:j + nc] for j in range(WIN_CHUNKS + 1)], axis=2)
    qb = q.reshape(B, nc, CHUNK, N_KV_HEADS, G, HEAD_DIM)
    s = jnp.einsum('bcqkgd,bcskd->bckgqs', qb, kb).astype(jnp.float32) * (HEAD_DIM ** -0.5)
    qi = jnp.arange(CHUNK) + pad
    kj = jnp.arange(BAND)
    dist = jnp.abs(qi[:, None] - kj[None, :]).astype(jnp.float32)
    slopes = alibi_slopes(N_Q_HEADS).reshape(N_KV_HEADS, G)
    s = s - slopes[:, :, None, None] * dist
    key_chunk = jnp.arange(nc)[:, None] - WIN_CHUNKS + kj[None, :] // CHUNK
    valid = key_chunk >= 0
    s = jnp.where(valid[None, :, None, None, None, :], s, -jnp.inf)
    sink = sinks.astype(jnp.float32).reshape(N_KV_HEADS, G)[None, None, :, :, None, None]
    m = jnp.maximum(jnp.max(s, axis=-1, keepdims=True), sink)
    p = jnp.exp(s - m)
    p = p / (jnp.sum(p, axis=-1, keepdims=True) + jnp.exp(sink - m))
    o = jnp.einsum('bckgqs,bcskd->bcqkgd', p.astype(v.dtype), vb)
    return o.reshape(B, S, ATTN_WIDTH)


def rglru_branch(xr, yr, conv_w, conv_b, wa, ba, wx, bx, lam):
    B, S = xr.shape[0], xr.shape[1]
    xc = lax.conv_general_dilated(xr, conv_w[:, None, :].astype(xr.dtype), window_strides=(1,),
                                  padding=[(CONV_WIDTH - 1, 0)],
                                  dimension_numbers=('NWC', 'WIO', 'NWC'),
                                  feature_group_count=LRU_WIDTH) + conv_b
    xb = xc.reshape(B, S, LRU_BLOCKS, LRU_BLOCK)
    r = jax.nn.sigmoid((jnp.einsum('bshi,hij->bshj', xb, wa).reshape(B, S, LRU_WIDTH) + ba).astype(jnp.float32))
    gi = jax.nn.sigmoid((jnp.einsum('bshi,hij->bshj', xb, wx).reshape(B, S, LRU_WIDTH) + bx).astype(jnp.float32))
    log_a = -LRU_C * r * jax.nn.softplus(-lam.astype(jnp.float32))
    a = jnp.exp(log_a)
    mult = jnp.sqrt(jnp.maximum(1.0 - jnp.exp(2.0 * log_a), 0.0))
    mult = jnp.where(jnp.arange(S)[None, :, None] == 0, 1.0, mult)
    b = mult * gi * xc.astype(jnp.float32)

    def combine(left, right):
        al, bl = left
        ar, br = right
        return al * ar, ar * bl + br

    _, h = lax.associative_scan(combine, (a, b), axis=1)
    return h.astype(xr.dtype) * jax.nn.gelu(yr)


def memory_attention(cq, mem_n, w_mem_kv):
    B, S = cq.shape[0], cq.shape[1]
    M = mem_n.shape[1]
    kv = mem_n @ w_mem_kv
    k = kv[..., :MEM_WIDTH].reshape(B, M, MEM_HEADS, MEM_HEAD_DIM)
    v = kv[..., MEM_WIDTH:].reshape(B, M, MEM_HEADS, MEM_HEAD_DIM)
    q = cq.reshape(B, S, MEM_HEADS, MEM_HEAD_DIM)
    s = jnp.einsum('bshd,bmhd->bhsm', q, k).astype(jnp.float32) * (MEM_HEAD_DIM ** -0.5)
    p = jax.nn.softmax(s, axis=-1).astype(v.dtype)
    return jnp.einsum('bhsm,bmhd->bshd', p, v).reshape(B, S, MEM_WIDTH)


def setup_inputs(seed: int = 0) -> dict:
    key = jax.random.key(seed)
    ks = jax.random.split(key, 32)
    f32 = jnp.float32

    def nrm(k, shape, fan_in):
        return jax.random.normal(k, shape, f32) * (fan_in ** -0.5)

    def gain(k, shape):
        return 1.0 + 0.05 * jax.random.normal(k, shape, f32)

    u = jax.random.uniform(ks[20], (DEPTH, LRU_WIDTH), f32, 0.9, 0.999)
    a0 = u ** (1.0 / LRU_C)
    lam = jnp.log(a0) - jnp.log1p(-a0)
    return {
        'x': jax.random.normal(ks[0], (BATCH, SEQ, D_MODEL), f32),
        'mem': jax.random.normal(ks[1], (BATCH, N_MEM, D_MODEL), f32),
        'ffn1_norm': gain(ks[2], (DEPTH, D_MODEL)),
        'ffn1_w_gate': nrm(ks[3], (DEPTH, D_MODEL, D_FF), D_MODEL),
        'ffn1_w_up': nrm(ks[4], (DEPTH, D_MODEL, D_FF), D_MODEL),
        'ffn1_w_down': nrm(ks[5], (DEPTH, D_FF, D_MODEL), D_FF),
        'mix_norm': gain(ks[6], (DEPTH, D_MODEL)),
        'w_in': nrm(ks[7], (DEPTH, D_MODEL, IN_WIDTH), D_MODEL),
        'gate_bias': 0.1 * jax.random.normal(ks[8], (DEPTH, N_BRANCH * D_MODEL), f32),
        'attn_sinks': 0.5 * jax.random.normal(ks[9], (DEPTH, N_Q_HEADS), f32),
        'w_attn_out': nrm(ks[10], (DEPTH, ATTN_WIDTH, D_MODEL), ATTN_WIDTH),
        'conv_w': nrm(ks[11], (DEPTH, CONV_WIDTH, LRU_WIDTH), CONV_WIDTH),
        'conv_b': 0.02 * jax.random.normal(ks[12], (DEPTH, LRU_WIDTH), f32),
        'lru_wa': nrm(ks[13], (DEPTH, LRU_BLOCKS, LRU_BLOCK, LRU_BLOCK), LRU_BLOCK),
        'lru_ba': 0.1 * jax.random.normal(ks[14], (DEPTH, LRU_WIDTH), f32),
        'lru_wx': nrm(ks[15], (DEPTH, LRU_BLOCKS, LRU_BLOCK, LRU_BLOCK), LRU_BLOCK),
        'lru_bx': 0.1 * jax.random.normal(ks[16], (DEPTH, LRU_WIDTH), f32),
        'lru_lambda': lam,
        'w_lru_out': nrm(ks[17], (DEPTH, LRU_WIDTH, D_MODEL), LRU_WIDTH),
        'mem_norm': gain(ks[18], (DEPTH, D_MODEL)),
        'w_mem_kv': nrm(ks[19], (DEPTH, D_MODEL, 2 * MEM_WIDTH), D_MODEL),
        'w_mem_out': nrm(ks[21], (DEPTH, MEM_WIDTH, D_MODEL), MEM_WIDTH),
        'w_out': nrm(ks[22], (DEPTH, D_MODEL, D_MODEL), D_MODEL),
        'ffn2_norm': gain(ks[23], (DEPTH, D_MODEL)),
        'ffn2_w_gate': nrm(ks[24], (DEPTH, D_MODEL, D_FF), D_MODEL),
        'ffn2_w_up': nrm(ks[25], (DEPTH, D_MODEL, D_FF), D_MODEL),
        'ffn2_w_down': nrm(ks[26], (DEPTH, D_FF, D_MODEL), D_FF),
        'final_norm': gain(ks[27], (D_MODEL,)),
    }


def reference(x, mem, ffn1_norm, ffn1_w_gate, ffn1_w_up, ffn1_w_down, mix_norm, w_in, gate_bias,
              attn_sinks, w_attn_out, conv_w, conv_b, lru_wa, lru_ba, lru_wx, lru_bx, lru_lambda,
              w_lru_out, mem_norm, w_mem_kv, w_mem_out, w_out, ffn2_norm, ffn2_w_gate, ffn2_w_up,
              ffn2_w_down, final_norm):
    B, S = x.shape[0], x.shape[1]
    o1 = ATTN_WIDTH
    o2 = o1 + KV_WIDTH
    o3 = o2 + KV_WIDTH
    o4 = o3 + LRU_WIDTH
    o5 = o4 + LRU_WIDTH
    o6 = o5 + MEM_WIDTH
    for l in range(DEPTH):
        x = x + 0.5 * swiglu(rmsnorm(x, ffn1_norm[l]), ffn1_w_gate[l], ffn1_w_up[l], ffn1_w_down[l])
        h = rmsnorm(x, mix_norm[l])
        proj = h @ w_in[l]
        q = proj[..., :o1].reshape(B, S, N_Q_HEADS, HEAD_DIM)
        k = proj[..., o1:o2].reshape(B, S, N_KV_HEADS, HEAD_DIM)
        v = proj[..., o2:o3].reshape(B, S, N_KV_HEADS, HEAD_DIM)
        xr = proj[..., o3:o4]
        yr = proj[..., o4:o5]
        cq = proj[..., o5:o6]
        g = jax.nn.sigmoid((proj[..., o6:] + gate_bias[l]).astype(jnp.float32)).astype(x.dtype)
        g = g.reshape(B, S, N_BRANCH, D_MODEL)
        br_attn = window_attention(q, k, v, attn_sinks[l]) @ w_attn_out[l]
        br_lru = rglru_branch(xr, yr, conv_w[l], conv_b[l], lru_wa[l], lru_ba[l], lru_wx[l], lru_bx[l],
                              lru_lambda[l]) @ w_lru_out[l]
        br_mem = memory_attention(cq, rmsnorm(mem, mem_norm[l]), w_mem_kv[l]) @ w_mem_out[l]
        merged = g[:, :, 0] * br_attn + g[:, :, 1] * br_lru + g[:, :, 2] * br_mem
        x = x + merged @ w_out[l]
        x = x + 0.5 * swiglu(rmsnorm(x, ffn2_norm[l]), ffn2_w_gate[l], ffn2_w_up[l], ffn2_w_down[l])
    return rmsnorm(x, final_norm)
```

```python
import numpy as np
import concourse.bass as bass
import concourse.mybir as mybir
from concourse.bass_utils import run_bass_kernel_spmd

F32 = mybir.dt.float32
BF16 = mybir.dt.bfloat16
AF = mybir.ActivationFunctionType
ALU = mybir.AluOpType

D = 1024
NT = 4096
T = 512
DFF = 2816
INW = 6400
O_Q, O_K, O_V, O_XR, O_YR, O_CQ, O_G = 0, 512, 640, 768, 1792, 2816, 3328
NV = 136
V_G1, V_GM, V_G2, V_CW, V_CB, V_BA, V_BX, V_LAM, V_GB, V_GMEM, V_GFIN, V_F0, V_OMF0, V_MASKA = \
    0, 8, 16, 24, 56, 64, 72, 80, 88, 112, 120, 128, 129, 130
EPS = 1e-6
NEG = -30000.0

ENGS = ("pe", "act", "dve", "pool", "sp")


class Buf:
    __slots__ = ("name", "w", "r")

    def __init__(self, name):
        self.name = name
        self.w = None
        self.r = {}


class Sched:
    def __init__(self, nc):
        self.nc = nc
        self.ops = {e: [] for e in ENGS}
        self.cnt = {e: 0 for e in ENGS}
        self.sem = {e: nc.alloc_semaphore("s_" + e) for e in ENGS}
        self.dsem = {}
        self.dcnt = {}
        self.waited = {}
        self.suffix = ""

    def _need(self, eng, dep, waits):
        if dep is None:
            return
        kind, key, val = dep
        if kind == "eng" and key == "pe" and eng == "pe":
            return
        if self.waited.get((eng, kind, key), 0) >= val:
            return
        if val > waits.get((kind, key), 0):
            waits[(kind, key)] = val

    def _collect(self, eng, reads, writes):
        waits = {}
        for b in reads:
            self._need(eng, b.w, waits)
        for b in writes:
            self._need(eng, b.w, waits)
            for d in b.r.values():
                self._need(eng, d, waits)
        for (kind, key), val in waits.items():
            self.waited[(eng, kind, key)] = val
        return waits

    def _mark(self, me, reads, writes):
        for b in reads:
            b.r[(me[0], me[1])] = me
        for b in writes:
            b.w = me
            b.r = {}

    def _emit_waits(self, e, waits):
        for (kind, key), val in waits.items():
            e.wait_ge(self.sem[key] if kind == "eng" else self.dsem[key], val)

    def op(self, eng, fn, reads=(), writes=()):
        waits = self._collect(eng, reads, writes)
        self.cnt[eng] += 1
        self._mark(("eng", eng, self.cnt[eng]), reads, writes)
        sem = self.sem[eng]

        def run(e, fn=fn, waits=waits, sem=sem):
            self._emit_waits(e, waits)
            fn(e).then_inc(sem, 1)
        self.ops[eng].append(run)

    def coll(self, key, fn, reads=(), writes=()):
        key = key + self.suffix
        if key not in self.dsem:
            self.dsem[key] = self.nc.alloc_semaphore("c_" + key)
            self.dcnt[key] = 0
        waits = self._collect("pool", reads, writes)
        self.dcnt[key] += 1
        self._mark(("dma", key, self.dcnt[key]), reads, writes)
        sem = self.dsem[key]

        def run(e, waits=waits, sem=sem, fn=fn):
            self._emit_waits(e, waits)
            fn(e).then_inc(sem, 1)
        self.ops["pool"].append(run)

    def dma(self, queue, key, out, in_, reads=(), writes=()):
        key = key + self.suffix
        if key not in self.dsem:
            self.dsem[key] = self.nc.alloc_semaphore("d_" + key)
            self.dcnt[key] = 0
        waits = self._collect(queue, reads, writes)
        self.dcnt[key] += 16
        self._mark(("dma", key, self.dcnt[key]), reads, writes)
        sem = self.dsem[key]

        def run(e, waits=waits, sem=sem, out=out, in_=in_):
            self._emit_waits(e, waits)
            e.dma_start(out=out, in_=in_).then_inc(sem, 16)
        self.ops[queue].append(run)

    def finish(self, final_bufs):
        waits = self._collect("sp", final_bufs, ())
        self.ops["sp"].append(lambda e, waits=waits: self._emit_waits(e, waits))

    def emit(self):
        nc = self.nc
        with nc.Block() as block:
            @block.tensor
            def _(e):
                for f in self.ops["pe"]:
                    f(e)

            @block.scalar
            def _(e):
                for f in self.ops["act"]:
                    f(e)

            @block.vector
            def _(e):
                for f in self.ops["dve"]:
                    f(e)

            @block.gpsimd
            def _(e):
                for f in self.ops["pool"]:
                    f(e)

            @block.sync
            def _(e):
                for f in self.ops["sp"]:
                    f(e)


class Rot:
    def __init__(self, nc, name, shape, dtype, n):
        self.items = []
        for i in range(n):
            nm = "%s%d" % (name, i)
            self.items.append((nc.alloc_sbuf_tensor(nm, shape, dtype).ap(), Buf(nm)))
        self.i = 0

    def next(self):
        it = self.items[self.i % len(self.items)]
        self.i += 1
        return it


def build(depth=4, ntiles=NT // T):
    nc = bass.Bass("TRN2", target_bir_lowering=False)
    S = Sched(nc)
    L = {"vo": 0, "p23": False}

    def din(name, shape):
        return nc.dram_tensor(name, shape, F32, kind="ExternalInput").ap()

    def dout(name, shape):
        return nc.dram_tensor(name, shape, F32, kind="ExternalOutput").ap()

    x_in = din("x_in", [128, 8, NT])
    xh4_in = din("xh4_in", [128, 8, 4])
    vecs_in = din("vecs", [128, depth * NV])
    w_in_all = din("w_in", [depth, D, INW])
    wbd_all = din("wbd", [depth, 128, 16, 128])
    f1 = [din("f1_wg", [depth, D, DFF]), din("f1_wu", [depth, D, DFF]), din("f1_wd", [depth, DFF, D])]
    f2 = [din("f2_wg", [depth, D, DFF]), din("f2_wu", [depth, D, DFF]), din("f2_wd", [depth, DFF, D])]
    bias_in = din("biasT", [128, 2, 2, 4, 128])
    sinks_all = din("sinks", [depth, 1, 1024])
    memT_in = din("memT", [128, 8, 256])
    w_attn_all = din("w_attn", [depth, 512, D])
    w_lru_all = din("w_lru", [depth, D, D])
    w_memkv_all = din("w_memkv", [depth, D, D])
    w_memo_all = din("w_memo", [depth, 512, D])
    w_out_all = din("w_out", [depth, D, D])
    y_out = dout("y_out", [128, 8, NT])
    xres = nc.dram_tensor("xres", [128, 8, NT], F32).ap()
    Bxres = [Buf("xres%d" % i) for i in range(ntiles)]
    CW1 = 8 * 128 + 8
    cc1_in = [nc.dram_tensor("cc1_in%d" % l, [128, CW1], F32) for l in range(depth)]
    cc1_out = [nc.dram_tensor("cc1_out%d" % l, [256, CW1], F32) for l in range(depth)]
    cc2_in = [nc.dram_tensor("cc2_in%d" % l, [128, 32], F32) for l in range(depth)]
    cc2_out = [nc.dram_tensor("cc2_out%d" % l, [256, 32], F32) for l in range(depth)]
    RG = [[0, 1], [2, 3], [4, 5], [6, 7]]
    p23 = True

    def sb(name, shape, dt=F32):
        return nc.alloc_sbuf_tensor(name, shape, dt).ap()

    vecs = sb("vecs_s", [128, depth * NV]); Bvecs = Buf("vecs")
    cf = sb("cf", [128, 32]); Bcf = Buf("cf")
    tmpv = sb("tmpv", [128, 4, 8]); Btmpv = Buf("tmpv")
    ones32 = sb("ones32", [128, 128]); Bones32 = Buf("ones32")
    onesb = sb("onesb", [128, 128], BF16); Bonesb = Buf("onesb")
    wbd = sb("wbd_s", [128, 16, 128], BF16); Bwbd = Buf("wbd")
    xt = sb("xt", [128, 8, T]); Bxt = Buf("xt")
    big32 = sb("big32", [128, 8, T]); Bbig = Buf("big32")
    hT = sb("hT", [128, 8, T], BF16); BhT = Buf("hT")
    hist = sb("hist", [128, 8, 4]); Bhist = Buf("hist")
    state = sb("state", [128, 8]); Bstate = Buf("state")
    FH = 11
    actT = sb("actT", [128, FH, T], BF16); Bact = Buf("actT")
    stg = Rot(nc, "stg", [128, 4096], F32, 2)
    wbr = Rot(nc, "wbf", [128, 4096], BF16, 3)
    tA = Rot(nc, "tA", [128, T + 4], F32, 4)
    tB = Rot(nc, "tB", [128, T], F32, 14)
    tC = Rot(nc, "tC", [128, T], BF16, 4)
    psb = [(nc.alloc_psum_tensor("ps%d" % i, [128, 512], F32).ap(), Buf("ps%d" % i)) for i in range(8)]
    psi = [0]

    def psum():
        it = psb[psi[0] % 8]
        psi[0] += 1
        return it

    casti = [0]

    def cast_eng():
        casti[0] += 1
        return "pool" if casti[0] % 2 else "act"

    def copy_op(eng, out, in_):
        if eng == "act":
            return lambda e: e.copy(out=out, in_=in_)
        return lambda e: e.tensor_copy(out=out, in_=in_)

    if p23:
        biasT = sb("biasT_s", [128, 2, 2, 4, 128]); BbiasT = Buf("biasT")
        esink = sb("esink", [64, 512], BF16); Besink = Buf("esink")
        qT = sb("qT", [128, 4, T], BF16); BqT = Buf("qT")
        kT = sb("kT", [128, 128 + T], BF16); BkT = Buf("kT")
        vtm = sb("vtm", [128, 5, 128], BF16); Bvtm = Buf("vtm")
        cqT = sb("cqT", [128, 4, T], BF16); BcqT = Buf("cqT")
        attnT = sb("attnT", [64, 8, T], BF16); BattnT = Buf("attnT")
        lruT = sb("lruT", [128, 8, T], BF16); BlruT = Buf("lruT")
        memoT = sb("memoT", [128, 4, T], BF16); BmemoT = Buf("memoT")
        merged, Bmerged = lruT, BlruT
        kmT = sb("kmT", [128, 4, 256], BF16); BkmT = Buf("kmT")
        vm = sb("vm", [128, 2, 512], BF16); Bvm = Buf("vm")
        pT = Rot(nc, "pT", [128, 2, 512], BF16, 2)

    WSC_COLS = depth * 222000
    wsc = nc.dram_tensor("wsc", [128, WSC_COLS], BF16).ap()
    wcache = {}
    wsc_pos = [0]

    def linear(W, K, c0, ncols, rhs, rhs_bufs, n, consume, kp=128):
        kcs = K // kp
        nch = ncols // 128
        G = max(1, min(nch, 32 // kcs))
        Wv = W.rearrange("(kc p) n -> p kc n", p=kp)
        for g0 in range(0, nch, G):
            gn = min(G, nch - g0)
            wb, wb_b = wbr.next()
            ncol = kcs * gn * 128
            wbv = wb[:kp, :ncol].rearrange("p (k n) -> p k n", k=kcs)
            key = (W.tensor.name, W.offset, K, kp, c0, g0, gn)
            if key in wcache:
                pos, cb = wcache[key]
                S.dma("sp", wb_b.name, wb[:kp, :ncol], wsc[:kp, pos:pos + ncol], reads=[cb], writes=[wb_b])
            else:
                st, st_b = stg.next()
                stv = st[:kp, :ncol].rearrange("p (k n) -> p k n", k=kcs)
                S.dma("sp", st_b.name, stv, Wv[:, :, c0 + g0 * 128:c0 + (g0 + gn) * 128], writes=[st_b])
                ce = cast_eng()
                S.op(ce, copy_op(ce, wb[:kp, :ncol], st[:kp, :ncol]), reads=[st_b], writes=[wb_b])
                pos = wsc_pos[0]
                wsc_pos[0] += ncol
                assert wsc_pos[0] <= WSC_COLS
                cb = Buf("wsc%d" % pos)
                S.dma(ce, "wscw_" + ce, wsc[:kp, pos:pos + ncol], wb[:kp, :ncol], reads=[wb_b], writes=[cb])
                wcache[key] = (pos, cb)
            for j in range(gn):
                ps, ps_b = psum()

                def mm(e, ps=ps, wbv=wbv, j=j):
                    ins = None
                    for kc in range(kcs):
                        ins = e.matmul(ps[:, :n], lhsT=wbv[:, kc, j * 128:(j + 1) * 128], rhs=rhs(kc),
                                       start=(kc == 0), stop=(kc == kcs - 1))
                    return ins
                S.op("pe", mm, reads=[wb_b] + list(rhs_bufs), writes=[ps_b])
                consume(g0 + j, ps, ps_b)

    def rmsnorm(xsrc, Bx, n, gofs, out, Bout):
        vo = L["vo"]
        rstd, Brstd = tB.next()
        for c in range(8):
            if c % 2 == 0:
                S.op("act", lambda e, c=c: e.activation(out=big32[:, c, :n], in_=xsrc[:, c, :n], func=AF.Square),
                     reads=[Bx], writes=[Bbig])
            else:
                S.op("pool", lambda e, c=c: e.tensor_tensor(out=big32[:, c, :n], in0=xsrc[:, c, :n], in1=xsrc[:, c, :n],
                                                            op=ALU.mult), reads=[Bx], writes=[Bbig])
        ps, ps_b = psum()

        def mm(e):
            ins = None
            for c in range(8):
                ins = e.matmul(ps[:, :n], lhsT=ones32, rhs=big32[:, c, :n], start=(c == 0), stop=(c == 7))
            return ins
        S.op("pe", mm, reads=[Bbig, Bones32], writes=[ps_b])
        S.op("act", lambda e: e.activation(out=rstd[:, :n], in_=ps[:, :n], func=AF.Sqrt, bias=EPS, scale=1.0),
             reads=[ps_b], writes=[Brstd])
        S.op("dve", lambda e: e.reciprocal(out=rstd[:, :n], in_=rstd[:, :n]), reads=[Brstd], writes=[Brstd])
        for c in range(8):
            S.op("dve", lambda e, c=c: e.scalar_tensor_tensor(
                out=out[:, c, :n], in0=xsrc[:, c, :n], scalar=vecs[:, vo + gofs + c:vo + gofs + c + 1], in1=rstd[:, :n],
                op0=ALU.mult, op1=ALU.mult), reads=[Bx, Brstd, Bvecs], writes=[Bout])

    def ffn(n, gofs):
        vo = L["vo"]
        rmsnorm(xt, Bxt, n, gofs, hT, BhT)
        for half in range(2):
            j0 = half * FH
            gate_sb = {}

            def cons_gate(j, ps, ps_b):
                t, t_b = tB.next()
                S.op("act", lambda e, t=t, ps=ps: e.activation(out=t[:, :n], in_=ps[:, :n], func=AF.Silu),
                     reads=[ps_b], writes=[t_b])
                gate_sb[j] = (t, t_b)

            def cons_up(j, ps, ps_b):
                t, t_b = gate_sb[j]
                S.op("dve", lambda e, t=t, ps=ps, j=j: e.tensor_tensor(out=actT[:, j, :n], in0=t[:, :n], in1=ps[:, :n],
                                                                       op=ALU.mult),
                     reads=[ps_b, t_b], writes=[Bact])
            for s0 in range(0, FH, 4):
                sn = min(4, FH - s0)
                base = j0 + s0
                linear(L["wg"], D, base * 128, sn * 128, lambda kc: hT[:, kc, :n], [BhT], n,
                       lambda j, ps, ps_b, s0=s0: cons_gate(s0 + j, ps, ps_b))
                linear(L["wu"], D, base * 128, sn * 128, lambda kc: hT[:, kc, :n], [BhT], n,
                       lambda j, ps, ps_b, s0=s0: cons_up(s0 + j, ps, ps_b))

            def cons_down(j, ps, ps_b):
                S.op("dve", lambda e, ps=ps, j=j: e.scalar_tensor_tensor(
                    out=xt[:, j, :n], in0=ps[:, :n], scalar=0.5, in1=xt[:, j, :n], op0=ALU.mult, op1=ALU.add),
                    reads=[ps_b, Bxt], writes=[Bxt])
            linear(L["wd"][j0 * 128:(j0 + FH) * 128, :], FH * 128, 0, D, lambda kc: actT[:, kc, :n], [Bact], n, cons_down)

    NB = 2

    def lru(n, first_tile):
        vo = L["vo"]
        want_out = L["p23"]
        hsrc = lambda kc: hT[:, kc, :n]
        for cb in range(0, 8, NB):
            cs = list(range(cb, min(8, cb + NB)))
            X = {}
            for c in cs:
                xrb, xrb_b = tA.next()
                xc, xc_b = tB.next()
                X[c] = {"xrb": (xrb, xrb_b), "xc": (xc, xc_b)}
                S.op("dve", lambda e, xrb=xrb, c=c: e.tensor_copy(out=xrb[:, 0:3], in_=hist[:, c, 0:3]),
                     reads=[Bhist], writes=[xrb_b])

                def cons_xr(j, ps, ps_b, xrb=xrb, xrb_b=xrb_b):
                    S.op("dve", lambda e: e.tensor_copy(out=xrb[:, 3:3 + n], in_=ps[:, :n]), reads=[ps_b], writes=[xrb_b])
                linear(L["w_in"], D, O_XR + c * 128, 128, hsrc, [BhT], n, cons_xr)
            for c in cs:
                xrb, xrb_b = X[c]["xrb"]
                xc, xc_b = X[c]["xc"]
                S.op("dve", lambda e, xc=xc, xrb=xrb, c=c: e.tensor_scalar(
                    out=xc[:, :n], in0=xrb[:, 0:n], scalar1=vecs[:, vo + V_CW + c:vo + V_CW + c + 1],
                    scalar2=vecs[:, vo + V_CB + c:vo + V_CB + c + 1], op0=ALU.mult, op1=ALU.add),
                    reads=[xrb_b, Bvecs], writes=[xc_b])
                for j in range(1, 4):
                    S.op("dve", lambda e, xc=xc, xrb=xrb, c=c, j=j: e.scalar_tensor_tensor(
                        out=xc[:, :n], in0=xrb[:, j:j + n],
                        scalar=vecs[:, vo + V_CW + j * 8 + c:vo + V_CW + j * 8 + c + 1],
                        in1=xc[:, :n], op0=ALU.mult, op1=ALU.add), reads=[xrb_b, Bvecs, xc_b], writes=[xc_b])
                S.op("dve", lambda e, xrb=xrb, c=c: e.tensor_copy(out=hist[:, c, 0:3], in_=xrb[:, n:n + 3]),
                     reads=[xrb_b], writes=[Bhist])
                xcb, xcb_b = tC.next()
                X[c]["xcb"] = (xcb, xcb_b)
                S.op("pool", lambda e, xcb=xcb, xc=xc: e.tensor_copy(out=xcb[:, :n], in_=xc[:, :n]), reads=[xc_b],
                     writes=[xcb_b])
            for c in cs:
                xcb, xcb_b = X[c]["xcb"]
                psa, psa_b = psum()
                S.op("pe", lambda e, psa=psa, xcb=xcb, c=c: e.matmul(psa[:, :n], lhsT=wbd[:, c, :], rhs=xcb[:, :n],
                                                                  start=True, stop=True),
                     reads=[Bwbd, xcb_b], writes=[psa_b])
                psx, psx_b = psum()
                S.op("pe", lambda e, psx=psx, xcb=xcb, c=c: e.matmul(psx[:, :n], lhsT=wbd[:, 8 + c, :], rhs=xcb[:, :n],
                                                                  start=True, stop=True),
                     reads=[Bwbd, xcb_b], writes=[psx_b])
                X[c]["ps"] = (psa, psa_b, psx, psx_b)
            for c in cs:
                psa, psa_b, psx, psx_b = X[c]["ps"]
                r, r_b = tB.next()
                gi, gi_b = tB.next()
                X[c]["r"] = (r, r_b)
                X[c]["gi"] = (gi, gi_b)
                S.op("act", lambda e, r=r, psa=psa, c=c: e.activation(out=r[:, :n], in_=psa[:, :n], func=AF.Sigmoid,
                                                                   bias=vecs[:, vo + V_BA + c:vo + V_BA + c + 1]),
                     reads=[psa_b, Bvecs], writes=[r_b])
                S.op("act", lambda e, gi=gi, psx=psx, c=c: e.activation(out=gi[:, :n], in_=psx[:, :n], func=AF.Sigmoid,
                                                                     bias=vecs[:, vo + V_BX + c:vo + V_BX + c + 1]),
                     reads=[psx_b, Bvecs], writes=[gi_b])
            for c in cs:
                gi, gi_b = X[c]["gi"]
                xc, xc_b = X[c]["xc"]
                S.op("pool", lambda e, gi=gi, xc=xc: e.tensor_tensor(out=gi[:, :n], in0=gi[:, :n], in1=xc[:, :n],
                                                                    op=ALU.mult), reads=[gi_b, xc_b], writes=[gi_b])
            for c in cs:
                r, r_b = X[c]["r"]
                T1, T1_b = tB.next()
                T2, T2_b = tB.next()
                X[c]["T1"] = (T1, T1_b)
                X[c]["T2"] = (T2, T2_b)
                S.op("act", lambda e, T1=T1, r=r, c=c: e.activation(out=T1[:, :n], in_=r[:, :n], func=AF.Tanh,
                                                                 scale=cf[:, c:c + 1]), reads=[r_b, Bcf], writes=[T1_b])
                S.op("act", lambda e, T2=T2, r=r, c=c: e.activation(out=T2[:, :n], in_=r[:, :n], func=AF.Tanh,
                                                                 scale=cf[:, 24 + c:25 + c]), reads=[r_b, Bcf], writes=[T2_b])
            for c in cs:
                r, r_b = X[c]["r"]
                S.op("act", lambda e, r=r, c=c: e.activation(out=r[:, :n], in_=r[:, :n], func=AF.Exp,
                                                           scale=cf[:, 8 + c:9 + c]), reads=[r_b, Bcf], writes=[r_b])
            for c in cs:
                r, r_b = X[c]["r"]
                xc, xc_b = X[c]["xc"]
                S.op("pool", lambda e, r=r, xc=xc: e.tensor_tensor(out=xc[:, :n], in0=r[:, :n], in1=r[:, :n],
                                                                  op=ALU.mult), reads=[r_b, xc_b], writes=[xc_b])
            for c in cs:
                r, r_b = X[c]["r"]
                T1, T1_b = X[c]["T1"]
                S.op("dve", lambda e, r=r, T1=T1: e.scalar_tensor_tensor(out=r[:, :n], in0=r[:, :n], scalar=1.0,
                                                                      in1=T1[:, :n], op0=ALU.add, op1=ALU.mult),
                     reads=[r_b, T1_b], writes=[r_b])
                S.op("pool", lambda e, r=r: e.tensor_scalar(out=r[:, :n], in0=r[:, :n], scalar1=1.0, scalar2=None,
                                                          op0=ALU.add), reads=[r_b], writes=[r_b])
            for c in cs:
                xc, xc_b = X[c]["xc"]
                T2, T2_b = X[c]["T2"]
                S.op("dve", lambda e, xc=xc, T2=T2: e.scalar_tensor_tensor(out=xc[:, :n], in0=xc[:, :n], scalar=1.0,
                                                                        in1=T2[:, :n], op0=ALU.add, op1=ALU.mult),
                     reads=[xc_b, T2_b], writes=[xc_b])
            for c in cs:
                xc, xc_b = X[c]["xc"]
                gi, gi_b = X[c]["gi"]
                S.op("act", lambda e, xc=xc: e.activation(out=xc[:, :n], in_=xc[:, :n], func=AF.Sqrt), reads=[xc_b],
                     writes=[xc_b])
                if first_tile:
                    S.op("dve", lambda e, xc=xc: e.scalar_tensor_tensor(
                        out=xc[:, 0:1], in0=xc[:, 0:1], scalar=vecs[:, vo + V_OMF0:vo + V_OMF0 + 1],
                        in1=vecs[:, vo + V_F0:vo + V_F0 + 1], op0=ALU.mult, op1=ALU.add), reads=[xc_b, Bvecs],
                        writes=[xc_b])
                S.op("pool", lambda e, xc=xc, gi=gi: e.tensor_tensor(out=xc[:, :n], in0=xc[:, :n], in1=gi[:, :n],
                                                                    op=ALU.mult), reads=[xc_b, gi_b], writes=[xc_b])
            for c in cs:
                r, r_b = X[c]["r"]
                xc, xc_b = X[c]["xc"]
                T2, T2_b = X[c]["T2"]
                S.op("dve", lambda e, T2=T2, r=r, xc=xc, c=c: e.tensor_tensor_scan(
                    out=T2[:, :n], data0=r[:, :n], data1=xc[:, :n], initial=state[:, c:c + 1], op0=ALU.mult,
                    op1=ALU.add), reads=[r_b, xc_b, Bstate, T2_b], writes=[T2_b])
                S.op("dve", lambda e, T2=T2, c=c: e.tensor_copy(out=state[:, c:c + 1], in_=T2[:, n - 1:n]),
                     reads=[T2_b], writes=[Bstate])
            if want_out:
                for c in cs:
                    T1, T1_b = X[c]["T1"]

                    def cons_yr(j, ps, ps_b, T1=T1, T1_b=T1_b):
                        S.op("act", lambda e: e.activation(out=T1[:, :n], in_=ps[:, :n], func=AF.Gelu_apprx_tanh),
                             reads=[ps_b, T1_b], writes=[T1_b])
                    linear(L["w_in"], D, O_YR + c * 128, 128, hsrc, [BhT], n, cons_yr)
                for c in cs:
                    T1, T1_b = X[c]["T1"]
                    T2, T2_b = X[c]["T2"]
                    S.op("dve", lambda e, T2=T2, T1=T1, c=c: e.tensor_tensor(out=lruT[:, c, :n], in0=T2[:, :n],
                                                                          in1=T1[:, :n], op=ALU.mult),
                         reads=[T2_b, T1_b], writes=[BlruT])

    S.dma("sp", "vecs", vecs, vecs_in, writes=[Bvecs])
    S.op("dve", lambda e: e.memset(ones32, 1.0 / D), writes=[Bones32])
    S.op("dve", lambda e: e.memset(onesb, 1.0), writes=[Bonesb])
    S.dma("sp", "biasT", biasT, bias_in, writes=[BbiasT])

    def layer_prelude(l):
        vo = L["vo"]
        st, st_b = stg.next()
        stv = st[:, :2048].rearrange("p (k n) -> p k n", k=16)
        S.dma("sp", st_b.name, stv, wbd_all[l], writes=[st_b])
        S.op("dve", lambda e, stv=stv: e.tensor_copy(out=wbd, in_=stv), reads=[st_b], writes=[Bwbd])
        e_ = tmpv[:, 0, :]
        acc = tmpv[:, 1, :]
        S.op("act", lambda e: e.activation(out=e_, in_=vecs[:, vo + V_LAM:vo + V_LAM + 8], func=AF.Exp, scale=-1.0),
             reads=[Bvecs], writes=[Btmpv])
        S.op("dve", lambda e: e.tensor_scalar(out=acc, in0=e_, scalar1=-1.0 / 6.0, scalar2=1.0 / 5.0, op0=ALU.mult,
                                              op1=ALU.add), reads=[Btmpv], writes=[Btmpv])
        for coef in (-1.0 / 4.0, 1.0 / 3.0, -1.0 / 2.0, 1.0):
            S.op("dve", lambda e: e.tensor_tensor(out=acc, in0=acc, in1=e_, op=ALU.mult), reads=[Btmpv], writes=[Btmpv])
            if coef < 0:
                S.op("dve", lambda e, coef=coef: e.tensor_scalar(out=acc, in0=acc, scalar1=-1.0, scalar2=-coef,
                                                                 op0=ALU.mult, op1=ALU.add), reads=[Btmpv], writes=[Btmpv])
            else:
                S.op("dve", lambda e, coef=coef: e.tensor_scalar(out=acc, in0=acc, scalar1=-1.0, scalar2=coef,
                                                                 op0=ALU.mult, op1=ALU.add), reads=[Btmpv], writes=[Btmpv])
        S.op("dve", lambda e: e.tensor_tensor(out=acc, in0=acc, in1=e_, op=ALU.mult), reads=[Btmpv], writes=[Btmpv])
        for k_, mul_ in enumerate((-4.0, -8.0, -16.0, 8.0)):
            S.op("dve", lambda e, k_=k_, mul_=mul_: e.tensor_scalar(out=cf[:, k_ * 8:(k_ + 1) * 8], in0=acc, scalar1=mul_,
                                                                    scalar2=None, op0=ALU.mult), reads=[Btmpv], writes=[Bcf])

    def inproj_xr_hist(n):
        for c in range(8):
            def cons(j, ps, ps_b, c=c):
                S.op("act", lambda e: e.copy(out=hist[:, c, 0:3], in_=ps[:, n - 3:n]), reads=[ps_b], writes=[Bhist])
            linear(L["w_in"], D, O_XR + c * 128, 128, lambda kc: hT[:, kc, :n], [BhT], n, cons)

    def inproj_v(n, blk0):
        st, st_b = stg.next()
        wb_, wb_b = wbr.next()
        stv = st[:, :1024].rearrange("p (k n) -> p k n", k=8)
        wb3 = wb_[:, :1024].rearrange("p (k n) -> p k n", k=8)
        S.dma("sp", st_b.name, stv, L["w_in"].rearrange("(kc p) n -> p kc n", p=128)[:, :, O_V:O_V + 128], writes=[st_b])
        S.op("pool", lambda e, wb_=wb_, st=st: e.tensor_copy(out=wb_[:, :1024], in_=st[:, :1024]), reads=[st_b], writes=[wb_b])
        for i in range(n // 128):
            ps, ps_b = psum()

            def mmv(e, ps=ps, i=i):
                ins = None
                for kc in range(8):
                    ins = e.matmul(ps[:, :128], lhsT=hT[:, kc, i * 128:(i + 1) * 128], rhs=wb3[:, kc, :],
                                   start=(kc == 0), stop=(kc == 7))
                return ins
            S.op("pe", mmv, reads=[BhT, wb_b], writes=[ps_b])
            S.op("act", lambda e, ps=ps, i=i: e.copy(out=vtm[:, blk0 + i, :], in_=ps[:, :128]), reads=[ps_b],
                 writes=[Bvtm])

    def p23_prelude(l):
        sink32, Bsink32 = tB.next()
        for kh_ in range(2):
            S.dma("sp", "sink32", sink32[kh_ * 32:kh_ * 32 + 1, :], sinks_all[l][0:1, kh_ * 512:(kh_ + 1) * 512],
                  writes=[Bsink32])
        for kh_ in range(2):
            S.op("act", lambda e, kh_=kh_, sink32=sink32: e.activation(
                out=esink[kh_ * 32:kh_ * 32 + 1, :], in_=sink32[kh_ * 32:kh_ * 32 + 1, :], func=AF.Exp),
                reads=[Bsink32], writes=[Besink])
        S.dma("sp", "xt", xt[:, :, :256], memT_in, writes=[Bxt])
        rmsnorm(xt, Bxt, 256, V_GMEM, hT, BhT)

        def cons_km(j, ps, ps_b):
            S.op("act", lambda e: e.copy(out=kmT[:, j, :], in_=ps[:, :256]), reads=[ps_b], writes=[BkmT])
        linear(w_memkv_all[l], D, 0, 512, lambda kc: hT[:, kc, :256], [BhT], 256, cons_km)
        st, st_b = stg.next()
        wbv_, wbv_b = wbr.next()
        stv = st.rearrange("p (k n) -> p k n", k=8)
        wv3 = wbv_.rearrange("p (k n) -> p k n", k=8)
        S.dma("sp", st_b.name, stv, w_memkv_all[l].rearrange("(kc p) n -> p kc n", p=128)[:, :, 512:1024], writes=[st_b])
        S.op("pool", lambda e, wbv_=wbv_, st=st: e.tensor_copy(out=wbv_, in_=st), reads=[st_b], writes=[wbv_b])
        for mb in range(2):
            ps, ps_b = psum()

            def mmv(e, ps=ps, mb=mb):
                ins = None
                for kc in range(8):
                    ins = e.matmul(ps, lhsT=hT[:, kc, mb * 128:(mb + 1) * 128], rhs=wv3[:, kc, :], start=(kc == 0),
                                   stop=(kc == 7))
                return ins
            S.op("pe", mmv, reads=[BhT, wbv_b], writes=[ps_b])
            S.op("act", lambda e, ps=ps, mb=mb: e.copy(out=vm[:, mb, :], in_=ps), reads=[ps_b], writes=[Bvm])


    def final_norm_store(t0):
        vo = L["vo"]
        rstd, Brstd = tB.next()
        for c in range(8):
            S.op("act", lambda e, c=c: e.activation(out=big32[:, c, :], in_=xt[:, c, :], func=AF.Square),
                 reads=[Bxt], writes=[Bbig])
        ps, ps_b = psum()

        def mmf(e, ps=ps):
            ins = None
            for c in range(8):
                ins = e.matmul(ps, lhsT=ones32, rhs=big32[:, c, :], start=(c == 0), stop=(c == 7))
            return ins
        S.op("pe", mmf, reads=[Bbig, Bones32], writes=[ps_b])
        S.op("act", lambda e, ps=ps: e.activation(out=rstd, in_=ps, func=AF.Sqrt, bias=EPS, scale=1.0), reads=[ps_b],
             writes=[Brstd])
        S.op("dve", lambda e: e.reciprocal(out=rstd, in_=rstd), reads=[Brstd], writes=[Brstd])
        for c in range(8):
            S.op("dve", lambda e, c=c: e.scalar_tensor_tensor(
                out=big32[:, c, :], in0=xt[:, c, :], scalar=vecs[:, vo + V_GFIN + c:vo + V_GFIN + c + 1], in1=rstd,
                op0=ALU.mult, op1=ALU.mult), reads=[Bxt, Brstd, Bvecs, Bbig], writes=[Bbig])
        S.dma("sp", "y_out", y_out[:, :, t0:t0 + T], big32, reads=[Bbig], writes=[Byo])

    def p23_phase(l, last):
        vo = L["vo"]
        HN = 128
        S.dma("sp", "xt", xt[:, :, :HN].rearrange("p c n -> p (c n)") if False else xt[:, :, :HN],
              cc1_out[l].ap()[0:128, 0:1024].rearrange("p (c n) -> p c n", c=8), reads=[Bcc1o[l]], writes=[Bxt])
        S.op("dve", lambda e: e.tensor_scalar(out=xt[:, :, :HN], in0=xt[:, :, :HN], scalar1=vecs[:, V_OMF0:V_OMF0 + 1],
                                              scalar2=None, op0=ALU.mult), reads=[Bxt, Bvecs], writes=[Bxt])
        S.dma("sp", "state", state, cc1_out[l].ap()[0:128, 1024:1032], reads=[Bcc1o[l]], writes=[Bstate])
        S.op("dve", lambda e: e.tensor_scalar(out=state, in0=state, scalar1=vecs[:, V_OMF0:V_OMF0 + 1],
                                              scalar2=None, op0=ALU.mult), reads=[Bstate, Bvecs], writes=[Bstate])
        rmsnorm(xt, Bxt, HN, V_GM, hT, BhT)

        def cons_k0(j, ps, ps_b):
            S.op("act", lambda e: e.copy(out=kT[:, 0:128], in_=ps[:, :128]), reads=[ps_b], writes=[BkT])
        linear(L["w_in"], D, O_K, 128, lambda kc: hT[:, kc, :HN], [BhT], HN, cons_k0)
        inproj_v(HN, 0)
        inproj_xr_hist(HN)

        for ti in range(ntiles):
            t0 = ti * T
            n = T
            S.dma("sp", "xt", xt, xres[:, :, t0:t0 + T], reads=[Bxres[ti]], writes=[Bxt])
            rmsnorm(xt, Bxt, T, V_GM, hT, BhT)
            rh = lambda kc: hT[:, kc, :n]

            def cons_q(j, ps, ps_b):
                S.op("act", lambda e: e.copy(out=qT[:, j, :], in_=ps), reads=[ps_b], writes=[BqT])
            linear(L["w_in"], D, O_Q, 512, rh, [BhT], n, cons_q)

            def cons_k(j, ps, ps_b):
                S.op("act", lambda e: e.copy(out=kT[:, 128:128 + T], in_=ps), reads=[ps_b], writes=[BkT])
            linear(L["w_in"], D, O_K, 128, rh, [BhT], n, cons_k)
            inproj_v(T, 1)

            def cons_cq(j, ps, ps_b):
                S.op("act", lambda e: e.copy(out=cqT[:, j, :], in_=ps), reads=[ps_b], writes=[BcqT])
            linear(L["w_in"], D, O_CQ, 512, rh, [BhT], n, cons_cq)

            for bi in range(4):
                for kh in range(2):
                    r0 = kh * 64
                    pt, pt_b = pT.next()
                    for kb in range(2):
                        ps, ps_b = psum()
                        kcol = (bi + kb) * 128
                        S.op("pe", lambda e, ps=ps, kcol=kcol, r0=r0, bi=bi: e.matmul(
                            ps, lhsT=kT[r0:r0 + 64, kcol:kcol + 128], rhs=qT[r0:r0 + 64, :, bi * 128:(bi + 1) * 128],
                            start=True, stop=True), reads=[BkT, BqT], writes=[ps_b])
                        sc, sc_b = tB.next()
                        bsrc, bsrc_b = biasT[:, kh, kb], BbiasT
                        S.op("dve", lambda e, sc=sc, ps=ps, bsrc=bsrc: e.scalar_tensor_tensor(
                            out=sc.rearrange("p (g q) -> p g q", g=4), in0=ps.rearrange("p (g q) -> p g q", g=4), scalar=0.125,
                            in1=bsrc, op0=ALU.mult, op1=ALU.add), reads=[ps_b, bsrc_b], writes=[sc_b])
                        if ti == 0 and bi == 0 and kb == 0:
                            S.op("dve", lambda e, sc=sc: e.tensor_scalar(out=sc, in0=sc, scalar1=vecs[:, V_MASKA:V_MASKA + 1],
                                                                        scalar2=None, op0=ALU.add),
                                 reads=[sc_b, Bvecs], writes=[sc_b])
                        S.op("act", lambda e, sc=sc, pt=pt, kb=kb: e.activation(out=pt[:, kb, :], in_=sc, func=AF.Exp),
                             reads=[sc_b], writes=[pt_b])
                    psn, psn_b = psum()

                    def mmn(e, psn=psn, pt=pt, kh=kh, bi=bi):
                        ins = None
                        for kb in range(2):
                            ins = e.matmul(psn[0:64, :], lhsT=vtm[:, bi + kb, kh * 64:(kh + 1) * 64], rhs=pt[:, kb, :],
                                           start=(kb == 0), stop=(kb == 1))
                        return ins
                    S.op("pe", mmn, reads=[Bvtm, pt_b], writes=[psn_b])
                    psd, psd_b = psum()

                    def mmd(e, psd=psd, pt=pt, kh=kh):
                        for kb in range(2):
                            e.matmul(psd[0:64, :], lhsT=onesb[:, 0:64], rhs=pt[:, kb, :], start=(kb == 0), stop=False)
                        return e.matmul(psd[0:64, :], lhsT=onesb[kh * 32:kh * 32 + 1, 0:64], rhs=esink[kh * 32:kh * 32 + 1, :],
                                        start=False, stop=True)
                    S.op("pe", mmd, reads=[Bonesb, pt_b, Besink], writes=[psd_b])
                    rd, rd_b = tB.next()
                    S.op("dve", lambda e, rd=rd, psd=psd: e.reciprocal(out=rd[0:64, :], in_=psd[0:64, :]), reads=[psd_b],
                         writes=[rd_b])
                    S.op("dve", lambda e, rd=rd, psn=psn, kh=kh, bi=bi: e.tensor_tensor(
                        out=attnT[:, kh * 4:(kh + 1) * 4, bi * 128:(bi + 1) * 128],
                        in0=psn[0:64, :].rearrange("p (g q) -> p g q", g=4), in1=rd[0:64, :].rearrange("p (g q) -> p g q", g=4),
                        op=ALU.mult), reads=[psn_b, rd_b], writes=[BattnT])
            S.op("pool", lambda e: e.tensor_copy(out=kT[:, 0:128], in_=kT[:, T:T + 128]), reads=[BkT], writes=[BkT])
            S.op("pool", lambda e: e.tensor_copy(out=vtm[:, 0, :], in_=vtm[:, 4, :]), reads=[Bvtm], writes=[Bvtm])

            for hd in range(4):
                pt, pt_b = pT.next()
                for mb in range(2):
                    ps, ps_b = psum()
                    S.op("pe", lambda e, ps=ps, hd=hd, mb=mb: e.matmul(ps, lhsT=kmT[:, hd, mb * 128:(mb + 1) * 128],
                                                                    rhs=cqT[:, hd, :], start=True, stop=True),
                         reads=[BkmT, BcqT], writes=[ps_b])
                    S.op("act", lambda e, ps=ps, pt=pt, mb=mb: e.activation(out=pt[:, mb, :], in_=ps, func=AF.Exp,
                                                                         scale=128.0 ** -0.5), reads=[ps_b], writes=[pt_b])
                psn, psn_b = psum()

                def mmn2(e, psn=psn, pt=pt, hd=hd):
                    ins = None
                    for mb in range(2):
                        ins = e.matmul(psn, lhsT=vm[:, mb, hd * 128:(hd + 1) * 128], rhs=pt[:, mb, :], start=(mb == 0),
                                       stop=(mb == 1))
                    return ins
                S.op("pe", mmn2, reads=[Bvm, pt_b], writes=[psn_b])
                psd, psd_b = psum()

                def mmd2(e, psd=psd, pt=pt):
                    ins = None
                    for mb in range(2):
                        ins = e.matmul(psd, lhsT=onesb, rhs=pt[:, mb, :], start=(mb == 0), stop=(mb == 1))
                    return ins
                S.op("pe", mmd2, reads=[Bonesb, pt_b], writes=[psd_b])
                rd, rd_b = tB.next()
                S.op("dve", lambda e, rd=rd, psd=psd: e.reciprocal(out=rd, in_=psd), reads=[psd_b], writes=[rd_b])
                S.op("dve", lambda e, rd=rd, psn=psn, hd=hd: e.tensor_tensor(out=memoT[:, hd, :], in0=psn, in1=rd, op=ALU.mult),
                     reads=[psn_b, rd_b], writes=[BmemoT])

            lru(T, ti == 0)

            branches = [
                (w_attn_all[l], 512, 64, lambda kc: attnT[:, kc, :], BattnT),
                (w_lru_all[l], D, 128, lambda kc: lruT[:, kc, :], BlruT),
                (w_memo_all[l], 512, 128, lambda kc: memoT[:, kc, :], BmemoT),
            ]
            for b, (Wb, Kb, kpb, rb, Brb) in enumerate(branches):
                sg_sb = {}
                for g0 in range(0, 8, 4):
                    def cons_g(j, ps, ps_b, b=b, g0=g0):
                        t, t_b = tB.next()
                        jj = g0 + j
                        S.op("act", lambda e, t=t, ps=ps: e.activation(
                            out=t, in_=ps, func=AF.Sigmoid, bias=vecs[:, vo + V_GB + b * 8 + jj:vo + V_GB + b * 8 + jj + 1]),
                            reads=[ps_b, Bvecs], writes=[t_b])
                        sg_sb[jj] = (t, t_b)
                    linear(L["w_in"], D, O_G + b * 1024 + g0 * 128, 512, rh, [BhT], n, cons_g)

                    def cons_b(j, ps, ps_b, b=b, g0=g0):
                        jj = g0 + j
                        t, t_b = sg_sb[jj]
                        if b == 0:
                            S.op("dve", lambda e, t=t, ps=ps: e.tensor_tensor(out=big32[:, jj, :], in0=t, in1=ps, op=ALU.mult),
                                 reads=[t_b, ps_b], writes=[Bbig])
                        else:
                            S.op("dve", lambda e, t=t, ps=ps: e.tensor_tensor(out=t, in0=t, in1=ps, op=ALU.mult),
                                 reads=[t_b, ps_b], writes=[t_b])
                            if b == 1:
                                S.op("pool", lambda e, t=t: e.tensor_tensor(out=big32[:, jj, :], in0=big32[:, jj, :], in1=t,
                                                                            op=ALU.add), reads=[t_b, Bbig], writes=[Bbig])
                            else:
                                S.op("pool", lambda e, t=t: e.tensor_tensor(out=merged[:, jj, :], in0=big32[:, jj, :], in1=t,
                                                                            op=ALU.add), reads=[t_b, Bbig], writes=[Bmerged])
                    linear(Wb, Kb, g0 * 128, 512, rb, [Brb], n, cons_b, kp=kpb)

            def cons_o(j, ps, ps_b):
                S.op("dve", lambda e, ps=ps: e.tensor_tensor(out=xt[:, j, :], in0=xt[:, j, :], in1=ps, op=ALU.add),
                     reads=[ps_b, Bxt], writes=[Bxt])
            linear(w_out_all[l], D, 0, D, lambda kc: merged[:, kc, :], [Bmerged], n, cons_o)

            ffn(T, V_G2)
            if not last:
                S.dma("sp", "xres", xres[:, :, t0:t0 + T], xt, reads=[Bxt], writes=[Bxres[ti]])
            else:
                final_norm_store(t0)

    Bcc1i = [Buf("cc1i%d" % l) for l in range(depth)]
    Bcc1o = [Buf("cc1o%d" % l) for l in range(depth)]
    Bcc2i = [Buf("cc2i%d" % l) for l in range(depth)]
    Bcc2o = [Buf("cc2o%d" % l) for l in range(depth)]
    Byo = Buf("y_out")

    def p1_phase(l):
        S.op("dve", lambda e: e.memset(state, 0.0), writes=[Bstate])
        if l == 0:
            S.dma("sp", "xt", xt[:, :, :4], xh4_in, writes=[Bxt])
        else:
            S.dma("sp", "xt", xt[:, :, :4], cc2_out[l - 1].ap()[0:128, :].rearrange("p (c n) -> p c n", c=8),
                  reads=[Bcc2o[l - 1]], writes=[Bxt])
            S.op("dve", lambda e: e.tensor_scalar(out=xt[:, :, :4], in0=xt[:, :, :4], scalar1=vecs[:, V_OMF0:V_OMF0 + 1],
                                                  scalar2=None, op0=ALU.mult), reads=[Bxt, Bvecs], writes=[Bxt])
        ffn(4, V_G1)
        rmsnorm(xt, Bxt, 4, V_GM, hT, BhT)
        inproj_xr_hist(4)
        for ti in range(ntiles):
            t0 = ti * T
            if l == 0:
                S.dma("sp", "xt", xt, x_in[:, :, t0:t0 + T], writes=[Bxt])
            else:
                S.dma("sp", "xt", xt, xres[:, :, t0:t0 + T], reads=[Bxres[ti]], writes=[Bxt])
            ffn(T, V_G1)
            S.dma("sp", "xres", xres[:, :, t0:t0 + T], xt, reads=[Bxt], writes=[Bxres[ti]])
            rmsnorm(xt, Bxt, T, V_GM, hT, BhT)
            lru(T, ti == 0)

    def exchange1(l):
        S.dma("sp", "cc1i", cc1_in[l].ap()[:, 0:1024].rearrange("p (c n) -> p c n", c=8), xres[:, :, NT - 128:NT],
              reads=[Bxres[ntiles - 1]], writes=[Bcc1i[l]])
        S.dma("sp", "cc1i", cc1_in[l].ap()[:, 1024:1032], state, reads=[Bstate], writes=[Bcc1i[l]])
        S.coll("cc", lambda e: e.collective_compute("AllGather", ALU.bypass, replica_groups=RG,
                                                    ins=[cc1_in[l].ap().opt()], outs=[cc1_out[l].ap().opt()]),
               reads=[Bcc1i[l]], writes=[Bcc1o[l]])

    def exchange2(l):
        S.dma("sp", "cc2i", cc2_in[l].ap().rearrange("p (c n) -> p c n", c=8), xres[:, :, NT - 4:NT],
              reads=[Bxres[ntiles - 1]], writes=[Bcc2i[l]])
        S.coll("cc", lambda e: e.collective_compute("AllGather", ALU.bypass, replica_groups=RG,
                                                    ins=[cc2_in[l].ap().opt()], outs=[cc2_out[l].ap().opt()]),
               reads=[Bcc2i[l]], writes=[Bcc2o[l]])

    for l in range(depth):
        S.suffix = "_L%d" % l
        L["vo"] = l * NV
        L["w_in"] = w_in_all[l]
        layer_prelude(l)
        L["p23"] = False
        L["wg"], L["wu"], L["wd"] = f1[0][l], f1[1][l], f1[2][l]
        p1_phase(l)
        exchange1(l)
        L["p23"] = True
        L["wg"], L["wu"], L["wd"] = f2[0][l], f2[1][l], f2[2][l]
        p23_prelude(l)
        p23_phase(l, l == depth - 1)
        if l < depth - 1:
            exchange2(l)
    S.finish([Byo])
    S.emit()
    return nc


def _fm(v):
    return np.ascontiguousarray(np.asarray(v, np.float32).reshape(-1, 128).T)


def _to_fm(xtok):
    t = xtok.shape[0]
    return np.ascontiguousarray(xtok.T.reshape(8, 128, t).transpose(1, 0, 2))


def _from_fm(xfm):
    t = xfm.shape[2]
    return np.ascontiguousarray(xfm.transpose(1, 0, 2).reshape(1024, t).T)


def _alibi_tables():
    slopes = np.array([2.0 ** (-8.0 * (i + 1) / 8) for i in range(8)], dtype=np.float32)
    j = np.arange(128)[:, None]
    i = np.arange(128)[None, :]
    tab = np.zeros((128, 2, 2, 4, 128), np.float32)
    for kh in range(2):
        for kb in range(2):
            kpos = kb * 128 + j
            qpos = 128 + i
            dist = np.abs(qpos - kpos).astype(np.float32)
            kch = kpos // 64
            qch = qpos // 64
            valid = (kch >= qch - 2) & (kch <= qch)
            for g in range(4):
                tab[:, kh, kb, g, :] = np.where(valid, -slopes[kh * 4 + g] * dist, np.float32(NEG))
    return tab


_PROGS = {}


def _prog(depth):
    if depth not in _PROGS:
        _PROGS[depth] = build(depth)
    return _PROGS[depth]


def _prep_inputs(x, mem, ffn1_norm, ffn1_w_gate, ffn1_w_up, ffn1_w_down, mix_norm, w_in, gate_bias,
                 attn_sinks, w_attn_out, conv_w, conv_b, lru_wa, lru_ba, lru_wx, lru_bx, lru_lambda,
                 w_lru_out, mem_norm, w_mem_kv, w_mem_out, w_out, ffn2_norm, ffn2_w_gate, ffn2_w_up,
                 ffn2_w_down, final_norm, depth=None):
    f32 = np.float32
    x = np.asarray(x, f32)
    mem = np.asarray(mem, f32)
    if depth is None:
        depth = ffn1_norm.shape[0]
    ncores = 8
    ca = lambda a: np.ascontiguousarray(np.asarray(a, f32)[:depth])
    tab = _alibi_tables()
    tabF_A = tab[:, :, 0].copy()
    tabF_A[...] = NEG
    tabF_B = np.ascontiguousarray(tab[:, :, 0])
    qperm = np.concatenate([np.concatenate([np.arange(j * 64, j * 64 + 64), np.arange((4 + j) * 64, (4 + j) * 64 + 64)])
                            for j in range(4)])
    win = np.array(np.asarray(w_in, f32)[:depth], copy=True)
    win[:, :, :512] = win[:, :, qperm]
    win = np.ascontiguousarray(win)
    wbd = np.zeros((depth, 128, 16, 128), f32)
    for l in range(depth):
        for gi_, wsrc in enumerate((lru_wa[l], lru_wx[l])):
            for c in range(8):
                wbd[l, 0:64, gi_ * 8 + c, 0:64] = wsrc[2 * c]
                wbd[l, 64:128, gi_ * 8 + c, 64:128] = wsrc[2 * c + 1]
    sinks = np.ascontiguousarray(np.repeat(np.asarray(attn_sinks, f32)[:depth], 128, axis=1)[:, None, :])
    shared = {"w_in": win, "wbd": wbd,
              "f1_wg": ca(ffn1_w_gate), "f1_wu": ca(ffn1_w_up), "f1_wd": ca(ffn1_w_down),
              "f2_wg": ca(ffn2_w_gate), "f2_wu": ca(ffn2_w_up), "f2_wd": ca(ffn2_w_down),
              "biasT": tab, "sinks": sinks, "w_attn": ca(w_attn_out), "w_lru": ca(w_lru_out),
              "w_memkv": ca(w_mem_kv), "w_memo": ca(w_mem_out), "w_out": ca(w_out)}
    in_maps = []
    for c in range(ncores):
        b, half = c // 2, c % 2
        v = np.zeros((128, depth * NV), f32)
        for l in range(depth):
            o = l * NV
            v[:, o + V_G1:o + V_G1 + 8] = _fm(ffn1_norm[l])
            v[:, o + V_GM:o + V_GM + 8] = _fm(mix_norm[l])
            v[:, o + V_G2:o + V_G2 + 8] = _fm(ffn2_norm[l])
            for j in range(4):
                v[:, o + V_CW + j * 8:o + V_CW + j * 8 + 8] = _fm(conv_w[l, j])
            v[:, o + V_CB:o + V_CB + 8] = _fm(conv_b[l])
            v[:, o + V_BA:o + V_BA + 8] = _fm(lru_ba[l])
            v[:, o + V_BX:o + V_BX + 8] = _fm(lru_bx[l])
            v[:, o + V_LAM:o + V_LAM + 8] = _fm(lru_lambda[l])
            v[:, o + V_GB:o + V_GB + 24] = _fm(gate_bias[l])
            v[:, o + V_GMEM:o + V_GMEM + 8] = _fm(mem_norm[l])
            v[:, o + V_GFIN:o + V_GFIN + 8] = _fm(final_norm)
            v[:, o + V_F0] = 1.0 if half == 0 else 0.0
            v[:, o + V_OMF0] = 0.0 if half == 0 else 1.0
            v[:, o + V_MASKA] = NEG if half == 0 else 0.0
        xs = _to_fm(x[b, half * NT:(half + 1) * NT, :])
        if half == 0:
            xh4 = np.zeros((128, 8, 4), f32)
        else:
            xh4 = _to_fm(x[b, NT - 4:NT, :])
        m = dict(shared)
        m.update({"x_in": xs, "xh4_in": xh4, "vecs": v,
                  "memT": _to_fm(mem[b])})
        in_maps.append(m)
    return in_maps, depth


def kernel(x, mem, ffn1_norm, ffn1_w_gate, ffn1_w_up, ffn1_w_down, mix_norm, w_in, gate_bias,
           attn_sinks, w_attn_out, conv_w, conv_b, lru_wa, lru_ba, lru_wx, lru_bx, lru_lambda,
           w_lru_out, mem_norm, w_mem_kv, w_mem_out, w_out, ffn2_norm, ffn2_w_gate, ffn2_w_up,
           ffn2_w_down, final_norm):
    in_maps, depth = _prep_inputs(x, mem, ffn1_norm, ffn1_w_gate, ffn1_w_up, ffn1_w_down, mix_norm, w_in,
                                  gate_bias, attn_sinks, w_attn_out, conv_w, conv_b, lru_wa, lru_ba, lru_wx,
                                  lru_bx, lru_lambda, w_lru_out, mem_norm, w_mem_kv, w_mem_out, w_out,
                                  ffn2_norm, ffn2_w_gate, ffn2_w_up, ffn2_w_down, final_norm)
    ncores = 8
    res = run_bass_kernel_spmd(_prog(depth), in_maps, core_ids=list(range(ncores))).results
    B = np.asarray(x).shape[0]
    out = np.empty((B, 2 * NT, D), np.float32)
    for c in range(ncores):
        out[c // 2, (c % 2) * NT:(c % 2 + 1) * NT, :] = _from_fm(np.asarray(res[c]["y_out"], np.float32))
    return out
```

```python
import numpy as np
import concourse.bass as bass
import concourse.mybir as mybir
from concourse.bass_utils import run_bass_kernel_spmd

F32 = mybir.dt.float32
BF16 = mybir.dt.bfloat16
AF = mybir.ActivationFunctionType
ALU = mybir.AluOpType

D = 1024
NT = 4096
T = 512
DFF = 2816
INW = 6400
O_Q, O_K, O_V, O_XR, O_YR, O_CQ, O_G = 0, 512, 640, 768, 1792, 2816, 3328
NV = 136
V_G1, V_GM, V_G2, V_CW, V_CB, V_BA, V_BX, V_LAM, V_GB, V_GMEM, V_GFIN, V_F0, V_OMF0, V_MASKA = \
    0, 8, 16, 24, 56, 64, 72, 80, 88, 112, 120, 128, 129, 130
EPS = 1e-6
NEG = -30000.0

ENGS = ("pe", "act", "dve", "pool", "sp")


class Buf:
    __slots__ = ("name", "w", "r")

    def __init__(self, name):
        self.name = name
        self.w = None
        self.r = {}


class Sched:
    def __init__(self, nc):
        self.nc = nc
        self.ops = {e: [] for e in ENGS}
        self.cnt = {e: 0 for e in ENGS}
        self.sem = {e: nc.alloc_semaphore("s_" + e) for e in ENGS}
        self.dsem = {}
        self.dcnt = {}
        self.waited = {}
        self.suffix = ""

    def _need(self, eng, dep, waits):
        if dep is None:
            return
        kind, key, val = dep
        if kind == "eng" and key == "pe" and eng == "pe":
            return
        if self.waited.get((eng, kind, key), 0) >= val:
            return
        if val > waits.get((kind, key), 0):
            waits[(kind, key)] = val

    def _collect(self, eng, reads, writes):
        waits = {}
        for b in reads:
            self._need(eng, b.w, waits)
        for b in writes:
            self._need(eng, b.w, waits)
            for d in b.r.values():
                self._need(eng, d, waits)
        for (kind, key), val in waits.items():
            self.waited[(eng, kind, key)] = val
        return waits

    def _mark(self, me, reads, writes):
        for b in reads:
            b.r[(me[0], me[1])] = me
        for b in writes:
            b.w = me
            b.r = {}

    def _emit_waits(self, e, waits):
        for (kind, key), val in waits.items():
            e.wait_ge(self.sem[key] if kind == "eng" else self.dsem[key], val)

    def op(self, eng, fn, reads=(), writes=()):
        waits = self._collect(eng, reads, writes)
        self.cnt[eng] += 1
        self._mark(("eng", eng, self.cnt[eng]), reads, writes)
        sem = self.sem[eng]

        def run(e, fn=fn, waits=waits, sem=sem):
            self._emit_waits(e, waits)
            fn(e).then_inc(sem, 1)
        self.ops[eng].append(run)

    def coll(self, key, fn, reads=(), writes=()):
        key = key + self.suffix
        if key not in self.dsem:
            self.dsem[key] = self.nc.alloc_semaphore("c_" + key)
            self.dcnt[key] = 0
        waits = self._collect("pool", reads, writes)
        self.dcnt[key] += 1
        self._mark(("dma", key, self.dcnt[key]), reads, writes)
        sem = self.dsem[key]

        def run(e, waits=waits, sem=sem, fn=fn):
            self._emit_waits(e, waits)
            fn(e).then_inc(sem, 1)
        self.ops["pool"].append(run)

    def dma(self, queue, key, out, in_, reads=(), writes=(), nosuffix=False):
        if not nosuffix:
            key = key + self.suffix
        if key not in self.dsem:
            self.dsem[key] = self.nc.alloc_semaphore("d_" + key)
            self.dcnt[key] = 0
        waits = self._collect(queue, reads, writes)
        self.dcnt[key] += 16
        self._mark(("dma", key, self.dcnt[key]), reads, writes)
        sem = self.dsem[key]

        def run(e, waits=waits, sem=sem, out=out, in_=in_):
            self._emit_waits(e, waits)
            e.dma_start(out=out, in_=in_).then_inc(sem, 16)
        self.ops[queue].append(run)

    def finish(self, final_bufs):
        waits = self._collect("sp", final_bufs, ())
        self.ops["sp"].append(lambda e, waits=waits: self._emit_waits(e, waits))

    def emit(self):
        nc = self.nc
        with nc.Block() as block:
            @block.tensor
            def _(e):
                for f in self.ops["pe"]:
                    f(e)

            @block.scalar
            def _(e):
                for f in self.ops["act"]:
                    f(e)

            @block.vector
            def _(e):
                for f in self.ops["dve"]:
                    f(e)

            @block.gpsimd
            def _(e):
                for f in self.ops["pool"]:
                    f(e)

            @block.sync
            def _(e):
                for f in self.ops["sp"]:
                    f(e)


class Rot:
    def __init__(self, nc, name, shape, dtype, n):
        self.items = []
        for i in range(n):
            nm = "%s%d" % (name, i)
            self.items.append((nc.alloc_sbuf_tensor(nm, shape, dtype).ap(), Buf(nm)))
        self.i = 0

    def next(self):
        it = self.items[self.i % len(self.items)]
        self.i += 1
        return it


def build(depth=4, ntiles=NT // T):
    nc = bass.Bass("TRN2", target_bir_lowering=False)
    S = Sched(nc)
    L = {"vo": 0, "p23": False}

    def din(name, shape):
        return nc.dram_tensor(name, shape, F32, kind="ExternalInput").ap()

    def dout(name, shape):
        return nc.dram_tensor(name, shape, F32, kind="ExternalOutput").ap()

    x_in = din("x_in", [128, 8, NT])
    xh4_in = din("xh4_in", [128, 8, 4])
    vecs_in = din("vecs", [128, depth * NV])
    w_in_all = din("w_in", [depth, D, INW])
    wbd_all = din("wbd", [depth, 128, 16, 128])
    f1 = [din("f1_wg", [depth, D, DFF]), din("f1_wu", [depth, D, DFF]), din("f1_wd", [depth, DFF, D])]
    f2 = [din("f2_wg", [depth, D, DFF]), din("f2_wu", [depth, D, DFF]), din("f2_wd", [depth, DFF, D])]
    bias_in = din("biasT", [128, 2, 2, 4, 128])
    sinks_all = din("sinks", [depth, 1, 1024])
    memT_in = din("memT", [128, 8, 256])
    w_attn_all = din("w_attn", [depth, 512, D])
    w_lru_all = din("w_lru", [depth, D, D])
    w_memkv_all = din("w_memkv", [depth, D, D])
    w_memo_all = din("w_memo", [depth, 512, D])
    w_out_all = din("w_out", [depth, D, D])
    y_out = dout("y_out", [128, 8, NT])
    xres = nc.dram_tensor("xres", [128, 8, NT], F32).ap()
    Bxres = [Buf("xres%d" % i) for i in range(ntiles)]
    CW1 = 8 * 128 + 8
    cc1_in = [nc.dram_tensor("cc1_in%d" % l, [128, CW1], F32) for l in range(depth)]
    cc1_out = [nc.dram_tensor("cc1_out%d" % l, [256, CW1], F32) for l in range(depth)]
    cc2_in = [nc.dram_tensor("cc2_in%d" % l, [128, 32], F32) for l in range(depth)]
    cc2_out = [nc.dram_tensor("cc2_out%d" % l, [256, 32], F32) for l in range(depth)]
    RG = [[0, 1], [2, 3], [4, 5], [6, 7]]
    rg_d = nc.dram_tensor("rg_d", [128, 2, 8, NT], F32).ap()
    Brg = [[[Buf("rg%d_%d_%d" % (i, c, k)) for k in range(2)] for c in range(8)] for i in range(ntiles)]
    p23 = True

    def sb(name, shape, dt=F32):
        return nc.alloc_sbuf_tensor(name, shape, dt).ap()

    vecs = sb("vecs_s", [128, depth * NV]); Bvecs = Buf("vecs")
    cf = sb("cf", [128, 32]); Bcf = Buf("cf")
    tmpv = sb("tmpv", [128, 4, 8]); Btmpv = Buf("tmpv")
    ones32 = sb("ones32", [128, 128]); Bones32 = Buf("ones32")
    onesb = sb("onesb", [128, 128], BF16); Bonesb = Buf("onesb")
    wbd = sb("wbd_s", [128, 16, 128], BF16); Bwbd = Buf("wbd")
    xt = sb("xt", [128, 8, T]); Bxt = Buf("xt")
    big32 = sb("big32", [128, 8, T]); Bbig = Buf("big32")
    hT = sb("hT", [128, 8, T], BF16); BhT = Buf("hT")
    hist = sb("hist", [128, 8, 4]); Bhist = Buf("hist")
    state = sb("state", [128, 8]); Bstate = Buf("state")
    FH = 11
    actT = sb("actT", [128, FH, T], BF16); Bact = Buf("actT")
    stg = Rot(nc, "stg", [128, 4096], F32, 2)
    wbr = Rot(nc, "wbf", [128, 4096], BF16, 2)
    tA = Rot(nc, "tA", [128, T + 4], F32, 4)
    tB = Rot(nc, "tB", [128, T], F32, 20)
    tC = Rot(nc, "tC", [128, T], BF16, 4)
    psb = [(nc.alloc_psum_tensor("ps%d" % i, [128, 512], F32).ap(), Buf("ps%d" % i)) for i in range(8)]
    psi = [0]

    def psum():
        it = psb[psi[0] % 8]
        psi[0] += 1
        return it

    casti = [0]

    def cast_eng():
        casti[0] += 1
        return "pool" if casti[0] % 2 else "act"

    def copy_op(eng, out, in_):
        if eng == "act":
            return lambda e: e.copy(out=out, in_=in_)
        return lambda e: e.tensor_copy(out=out, in_=in_)

    if p23:
        biasT = sb("biasT_s", [128, 2, 2, 4, 128]); BbiasT = Buf("biasT")
        esink = sb("esink", [64, 512], BF16); Besink = Buf("esink")
        qT = sb("qT", [128, 4, T], BF16); BqT = Buf("qT")
        kT = sb("kT", [128, 128 + T], BF16); BkT = Buf("kT")
        vtm = sb("vtm", [128, 5, 128], BF16); Bvtm = Buf("vtm")
        cqT = sb("cqT", [128, 4, T], BF16); BcqT = Buf("cqT")
        attnT = sb("attnT", [64, 8, T], BF16); BattnT = Buf("attnT")
        lruT = sb("lruT", [128, 8, T], BF16); BlruT = Buf("lruT")
        memoT = sb("memoT", [128, 4, T], BF16); BmemoT = Buf("memoT")
        merged, Bmerged = lruT, BlruT
        kmT = sb("kmT", [128, 4, 256], BF16); BkmT = Buf("kmT")
        vm = sb("vm", [128, 2, 512], BF16); Bvm = Buf("vm")
        pT = Rot(nc, "pT", [128, 2, 512], BF16, 2)

    WSC_COLS = depth * 222000
    wsc = nc.dram_tensor("wsc", [128, WSC_COLS], BF16).ap()
    wcache = {}
    wsc_pos = [0]

    def linear(W, K, c0, ncols, rhs, rhs_bufs, n, consume, kp=128):
        kcs = K // kp
        nch = ncols // 128
        G = max(1, min(nch, 32 // kcs))
        Wv = W.rearrange("(kc p) n -> p kc n", p=kp)
        for g0 in range(0, nch, G):
            gn = min(G, nch - g0)
            wb, wb_b = wbr.next()
            ncol = kcs * gn * 128
            wbv = wb[:kp, :ncol].rearrange("p (k n) -> p k n", k=kcs)
            key = (W.tensor.name, W.offset, K, kp, c0, g0, gn)
            if key in wcache:
                pos, cb = wcache[key]
                S.dma("sp", wb_b.name, wb[:kp, :ncol], wsc[:kp, pos:pos + ncol], reads=[cb], writes=[wb_b])
            else:
                st, st_b = stg.next()
                stv = st[:kp, :ncol].rearrange("p (k n) -> p k n", k=kcs)
                S.dma("sp", st_b.name, stv, Wv[:, :, c0 + g0 * 128:c0 + (g0 + gn) * 128], writes=[st_b])
                ce = cast_eng()
                S.op(ce, copy_op(ce, wb[:kp, :ncol], st[:kp, :ncol]), reads=[st_b], writes=[wb_b])
                pos = wsc_pos[0]
                wsc_pos[0] += ncol
                assert wsc_pos[0] <= WSC_COLS
                cb = Buf("wsc%d" % pos)
                S.dma(ce, "wscw_" + ce, wsc[:kp, pos:pos + ncol], wb[:kp, :ncol], reads=[wb_b], writes=[cb])
                wcache[key] = (pos, cb)
            for j in range(gn):
                ps, ps_b = psum()

                def mm(e, ps=ps, wbv=wbv, j=j):
                    ins = None
                    for kc in range(kcs):
                        ins = e.matmul(ps[:, :n], lhsT=wbv[:, kc, j * 128:(j + 1) * 128], rhs=rhs(kc),
                                       start=(kc == 0), stop=(kc == kcs - 1))
                    return ins
                S.op("pe", mm, reads=[wb_b] + list(rhs_bufs), writes=[ps_b])
                consume(g0 + j, ps, ps_b)

    def rmsnorm(xsrc, Bx, n, gofs, out, Bout):
        vo = L["vo"]
        rstd, Brstd = tB.next()
        for c in range(8):
            S.op("act", lambda e, c=c: e.activation(out=big32[:, c, :n], in_=xsrc[:, c, :n], func=AF.Square),
                 reads=[Bx], writes=[Bbig])
        ps, ps_b = psum()

        def mm(e):
            ins = None
            for c in range(8):
                ins = e.matmul(ps[:, :n], lhsT=ones32, rhs=big32[:, c, :n], start=(c == 0), stop=(c == 7))
            return ins
        S.op("pe", mm, reads=[Bbig, Bones32], writes=[ps_b])
        S.op("act", lambda e: e.activation(out=rstd[:, :n], in_=ps[:, :n], func=AF.Sqrt, bias=EPS, scale=1.0),
             reads=[ps_b], writes=[Brstd])
        S.op("dve", lambda e: e.reciprocal(out=rstd[:, :n], in_=rstd[:, :n]), reads=[Brstd], writes=[Brstd])
        for c in range(8):
            S.op("dve", lambda e, c=c: e.scalar_tensor_tensor(
                out=out[:, c, :n], in0=xsrc[:, c, :n], scalar=vecs[:, vo + gofs + c:vo + gofs + c + 1], in1=rstd[:, :n],
                op0=ALU.mult, op1=ALU.mult), reads=[Bx, Brstd, Bvecs], writes=[Bout])

    def ffn(n, gofs):
        vo = L["vo"]
        rmsnorm(xt, Bxt, n, gofs, hT, BhT)
        for half in range(2):
            j0 = half * FH
            gate_sb = {}

            def cons_gate(j, ps, ps_b):
                t, t_b = tB.next()
                S.op("act", lambda e, t=t, ps=ps: e.activation(out=t[:, :n], in_=ps[:, :n], func=AF.Silu),
                     reads=[ps_b], writes=[t_b])
                gate_sb[j] = (t, t_b)

            def cons_up(j, ps, ps_b):
                t, t_b = gate_sb[j]
                S.op("dve", lambda e, t=t, ps=ps, j=j: e.tensor_tensor(out=actT[:, j, :n], in0=t[:, :n], in1=ps[:, :n],
                                                                       op=ALU.mult),
                     reads=[ps_b, t_b], writes=[Bact])
            for s0 in range(0, FH, 4):
                sn = min(4, FH - s0)
                base = j0 + s0
                linear(L["wg"], D, base * 128, sn * 128, lambda kc: hT[:, kc, :n], [BhT], n,
                       lambda j, ps, ps_b, s0=s0: cons_gate(s0 + j, ps, ps_b))
                linear(L["wu"], D, base * 128, sn * 128, lambda kc: hT[:, kc, :n], [BhT], n,
                       lambda j, ps, ps_b, s0=s0: cons_up(s0 + j, ps, ps_b))

            def cons_down(j, ps, ps_b):
                S.op("dve", lambda e, ps=ps, j=j: e.scalar_tensor_tensor(
                    out=xt[:, j, :n], in0=ps[:, :n], scalar=0.5, in1=xt[:, j, :n], op0=ALU.mult, op1=ALU.add),
                    reads=[ps_b, Bxt], writes=[Bxt])
            linear(L["wd"][j0 * 128:(j0 + FH) * 128, :], FH * 128, 0, D, lambda kc: actT[:, kc, :n], [Bact], n, cons_down)

    NB = 4

    def lru(n, first_tile, ti=0):
        vo = L["vo"]
        want_out = L["p23"]
        hsrc = lambda kc: hT[:, kc, :n]
        t0_ = ti * T
        for cb in range(0, 8, NB):
            cs = list(range(cb, min(8, cb + NB)))
            X = {}
            if want_out:
                for c in cs:
                    xc, xc_b = tB.next()
                    r, r_b = tB.next()
                    gi, gi_b = tB.next()
                    X[c] = {"xc": (xc, xc_b), "r": (r, r_b), "gi": (gi, gi_b)}
                    S.dma("pool", "ld_r%d" % (c % NB), r[:, :n], rg_d[:, 0, c, t0_:t0_ + n], reads=[Brg[ti][c][0]],
                          writes=[r_b], nosuffix=True)
                    S.dma("pool", "ld_g%d" % (c % NB), gi[:, :n], rg_d[:, 1, c, t0_:t0_ + n], reads=[Brg[ti][c][1]],
                          writes=[gi_b], nosuffix=True)
            if not want_out:
                    for c in cs:
                        xrb, xrb_b = tA.next()
                        xc, xc_b = tB.next()
                        X[c] = {"xrb": (xrb, xrb_b), "xc": (xc, xc_b)}
                        S.op("dve", lambda e, xrb=xrb, c=c: e.tensor_copy(out=xrb[:, 0:3], in_=hist[:, c, 0:3]),
                             reads=[Bhist], writes=[xrb_b])

                        def cons_xr(j, ps, ps_b, xrb=xrb, xrb_b=xrb_b):
                            S.op("dve", lambda e: e.tensor_copy(out=xrb[:, 3:3 + n], in_=ps[:, :n]), reads=[ps_b], writes=[xrb_b])
                        linear(L["w_in"], D, O_XR + c * 128, 128, hsrc, [BhT], n, cons_xr)
                    for c in cs:
                        xrb, xrb_b = X[c]["xrb"]
                        xc, xc_b = X[c]["xc"]
                        S.op("dve", lambda e, xc=xc, xrb=xrb, c=c: e.tensor_scalar(
                            out=xc[:, :n], in0=xrb[:, 0:n], scalar1=vecs[:, vo + V_CW + c:vo + V_CW + c + 1],
                            scalar2=vecs[:, vo + V_CB + c:vo + V_CB + c + 1], op0=ALU.mult, op1=ALU.add),
                            reads=[xrb_b, Bvecs], writes=[xc_b])
                        for j in range(1, 4):
                            S.op("dve", lambda e, xc=xc, xrb=xrb, c=c, j=j: e.scalar_tensor_tensor(
                                out=xc[:, :n], in0=xrb[:, j:j + n],
                                scalar=vecs[:, vo + V_CW + j * 8 + c:vo + V_CW + j * 8 + c + 1],
                                in1=xc[:, :n], op0=ALU.mult, op1=ALU.add), reads=[xrb_b, Bvecs, xc_b], writes=[xc_b])
                        S.op("dve", lambda e, xrb=xrb, c=c: e.tensor_copy(out=hist[:, c, 0:3], in_=xrb[:, n:n + 3]),
                             reads=[xrb_b], writes=[Bhist])
                        xcb, xcb_b = tC.next()
                        X[c]["xcb"] = (xcb, xcb_b)
                        S.op("pool", lambda e, xcb=xcb, xc=xc: e.tensor_copy(out=xcb[:, :n], in_=xc[:, :n]), reads=[xc_b],
                             writes=[xcb_b])
                    for c in cs:
                        xcb, xcb_b = X[c]["xcb"]
                        psa, psa_b = psum()
                        S.op("pe", lambda e, psa=psa, xcb=xcb, c=c: e.matmul(psa[:, :n], lhsT=wbd[:, c, :], rhs=xcb[:, :n],
                                                                          start=True, stop=True),
                             reads=[Bwbd, xcb_b], writes=[psa_b])
                        psx, psx_b = psum()
                        S.op("pe", lambda e, psx=psx, xcb=xcb, c=c: e.matmul(psx[:, :n], lhsT=wbd[:, 8 + c, :], rhs=xcb[:, :n],
                                                                          start=True, stop=True),
                             reads=[Bwbd, xcb_b], writes=[psx_b])
                        X[c]["ps"] = (psa, psa_b, psx, psx_b)
                    for c in cs:
                        psa, psa_b, psx, psx_b = X[c]["ps"]
                        r, r_b = tB.next()
                        gi, gi_b = tB.next()
                        X[c]["r"] = (r, r_b)
                        X[c]["gi"] = (gi, gi_b)
                        S.op("act", lambda e, r=r, psa=psa, c=c: e.activation(out=r[:, :n], in_=psa[:, :n], func=AF.Sigmoid,
                                                                           bias=vecs[:, vo + V_BA + c:vo + V_BA + c + 1]),
                             reads=[psa_b, Bvecs], writes=[r_b])
                        S.op("act", lambda e, gi=gi, psx=psx, c=c: e.activation(out=gi[:, :n], in_=psx[:, :n], func=AF.Sigmoid,
                                                                             bias=vecs[:, vo + V_BX + c:vo + V_BX + c + 1]),
                             reads=[psx_b, Bvecs], writes=[gi_b])
                    for c in cs:
                        gi, gi_b = X[c]["gi"]
                        xc, xc_b = X[c]["xc"]
                        S.op("pool", lambda e, gi=gi, xc=xc: e.tensor_tensor(out=gi[:, :n], in0=gi[:, :n], in1=xc[:, :n],
                                                                            op=ALU.mult), reads=[gi_b, xc_b], writes=[gi_b])
                    for c in cs:
                        r, r_b = X[c]["r"]
                        gi, gi_b = X[c]["gi"]
                        S.dma("act", "sv_r%d" % (c % NB), rg_d[:, 0, c, t0_:t0_ + n], r[:, :n], reads=[r_b], writes=[Brg[ti][c][0]], nosuffix=True)
                        S.dma("pool", "sv_g%d" % (c % NB), rg_d[:, 1, c, t0_:t0_ + n], gi[:, :n], reads=[gi_b], writes=[Brg[ti][c][1]], nosuffix=True)
            for c in cs:
                r, r_b = X[c]["r"]
                T1, T1_b = tB.next()
                T2, T2_b = tB.next()
                X[c]["T1"] = (T1, T1_b)
                X[c]["T2"] = (T2, T2_b)
                S.op("act", lambda e, T1=T1, r=r, c=c: e.activation(out=T1[:, :n], in_=r[:, :n], func=AF.Tanh,
                                                                 scale=cf[:, c:c + 1]), reads=[r_b, Bcf], writes=[T1_b])
                S.op("act", lambda e, T2=T2, r=r, c=c: e.activation(out=T2[:, :n], in_=r[:, :n], func=AF.Tanh,
                                                                 scale=cf[:, 24 + c:25 + c]), reads=[r_b, Bcf], writes=[T2_b])
            for c in cs:
                r, r_b = X[c]["r"]
                S.op("act", lambda e, r=r, c=c: e.activation(out=r[:, :n], in_=r[:, :n], func=AF.Exp,
                                                           scale=cf[:, 8 + c:9 + c]), reads=[r_b, Bcf], writes=[r_b])
            for c in cs:
                r, r_b = X[c]["r"]
                xc, xc_b = X[c]["xc"]
                S.op("pool", lambda e, r=r, xc=xc: e.tensor_tensor(out=xc[:, :n], in0=r[:, :n], in1=r[:, :n],
                                                                  op=ALU.mult), reads=[r_b, xc_b], writes=[xc_b])
            for c in cs:
                r, r_b = X[c]["r"]
                T1, T1_b = X[c]["T1"]
                S.op("dve", lambda e, r=r, T1=T1: e.scalar_tensor_tensor(out=r[:, :n], in0=r[:, :n], scalar=1.0,
                                                                      in1=T1[:, :n], op0=ALU.add, op1=ALU.mult),
                     reads=[r_b, T1_b], writes=[r_b])
                S.op("pool", lambda e, r=r: e.tensor_scalar(out=r[:, :n], in0=r[:, :n], scalar1=1.0, scalar2=None,
                                                          op0=ALU.add), reads=[r_b], writes=[r_b])
            for c in cs:
                xc, xc_b = X[c]["xc"]
                T2, T2_b = X[c]["T2"]
                S.op("dve", lambda e, xc=xc, T2=T2: e.scalar_tensor_tensor(out=xc[:, :n], in0=xc[:, :n], scalar=1.0,
                                                                        in1=T2[:, :n], op0=ALU.add, op1=ALU.mult),
                     reads=[xc_b, T2_b], writes=[xc_b])
            for c in cs:
                xc, xc_b = X[c]["xc"]
                gi, gi_b = X[c]["gi"]
                S.op("act", lambda e, xc=xc: e.activation(out=xc[:, :n], in_=xc[:, :n], func=AF.Sqrt), reads=[xc_b],
                     writes=[xc_b])
                if first_tile:
                    S.op("dve", lambda e, xc=xc: e.scalar_tensor_tensor(
                        out=xc[:, 0:1], in0=xc[:, 0:1], scalar=vecs[:, vo + V_OMF0:vo + V_OMF0 + 1],
                        in1=vecs[:, vo + V_F0:vo + V_F0 + 1], op0=ALU.mult, op1=ALU.add), reads=[xc_b, Bvecs],
                        writes=[xc_b])
                S.op("pool", lambda e, xc=xc, gi=gi: e.tensor_tensor(out=xc[:, :n], in0=xc[:, :n], in1=gi[:, :n],
                                                                    op=ALU.mult), reads=[xc_b, gi_b], writes=[xc_b])
            for c in cs:
                r, r_b = X[c]["r"]
                xc, xc_b = X[c]["xc"]
                T2, T2_b = X[c]["T2"]
                S.op("dve", lambda e, T2=T2, r=r, xc=xc, c=c: e.tensor_tensor_scan(
                    out=T2[:, :n], data0=r[:, :n], data1=xc[:, :n], initial=state[:, c:c + 1], op0=ALU.mult,
                    op1=ALU.add), reads=[r_b, xc_b, Bstate, T2_b], writes=[T2_b])
                S.op("dve", lambda e, T2=T2, c=c: e.tensor_copy(out=state[:, c:c + 1], in_=T2[:, n - 1:n]),
                     reads=[T2_b], writes=[Bstate])
            if want_out:
                for c in cs:
                    T1, T1_b = X[c]["T1"]

                    def cons_yr(j, ps, ps_b, T1=T1, T1_b=T1_b):
                        S.op("act", lambda e: e.activation(out=T1[:, :n], in_=ps[:, :n], func=AF.Gelu_apprx_tanh),
                             reads=[ps_b, T1_b], writes=[T1_b])
                    linear(L["w_in"], D, O_YR + c * 128, 128, hsrc, [BhT], n, cons_yr)
                for c in cs:
                    T1, T1_b = X[c]["T1"]
                    T2, T2_b = X[c]["T2"]
                    S.op("dve", lambda e, T2=T2, T1=T1, c=c: e.tensor_tensor(out=lruT[:, c, :n], in0=T2[:, :n],
                                                                          in1=T1[:, :n], op=ALU.mult),
                         reads=[T2_b, T1_b], writes=[BlruT])

    S.dma("sp", "vecs", vecs, vecs_in, writes=[Bvecs])
    S.op("dve", lambda e: e.memset(ones32, 1.0 / D), writes=[Bones32])
    S.op("dve", lambda e: e.memset(onesb, 1.0), writes=[Bonesb])
    S.dma("sp", "biasT", biasT, bias_in, writes=[BbiasT])

    def layer_prelude(l):
        vo = L["vo"]
        st, st_b = stg.next()
        stv = st[:, :2048].rearrange("p (k n) -> p k n", k=16)
        S.dma("sp", st_b.name, stv, wbd_all[l], writes=[st_b])
        S.op("dve", lambda e, stv=stv: e.tensor_copy(out=wbd, in_=stv), reads=[st_b], writes=[Bwbd])
        e_ = tmpv[:, 0, :]
        acc = tmpv[:, 1, :]
        S.op("act", lambda e: e.activation(out=e_, in_=vecs[:, vo + V_LAM:vo + V_LAM + 8], func=AF.Exp, scale=-1.0),
             reads=[Bvecs], writes=[Btmpv])
        S.op("dve", lambda e: e.tensor_scalar(out=acc, in0=e_, scalar1=-1.0 / 6.0, scalar2=1.0 / 5.0, op0=ALU.mult,
                                              op1=ALU.add), reads=[Btmpv], writes=[Btmpv])
        for coef in (-1.0 / 4.0, 1.0 / 3.0, -1.0 / 2.0, 1.0):
            S.op("dve", lambda e: e.tensor_tensor(out=acc, in0=acc, in1=e_, op=ALU.mult), reads=[Btmpv], writes=[Btmpv])
            if coef < 0:
                S.op("dve", lambda e, coef=coef: e.tensor_scalar(out=acc, in0=acc, scalar1=-1.0, scalar2=-coef,
                                                                 op0=ALU.mult, op1=ALU.add), reads=[Btmpv], writes=[Btmpv])
            else:
                S.op("dve", lambda e, coef=coef: e.tensor_scalar(out=acc, in0=acc, scalar1=-1.0, scalar2=coef,
                                                                 op0=ALU.mult, op1=ALU.add), reads=[Btmpv], writes=[Btmpv])
        S.op("dve", lambda e: e.tensor_tensor(out=acc, in0=acc, in1=e_, op=ALU.mult), reads=[Btmpv], writes=[Btmpv])
        for k_, mul_ in enumerate((-4.0, -8.0, -16.0, 8.0)):
            S.op("dve", lambda e, k_=k_, mul_=mul_: e.tensor_scalar(out=cf[:, k_ * 8:(k_ + 1) * 8], in0=acc, scalar1=mul_,
                                                                    scalar2=None, op0=ALU.mult), reads=[Btmpv], writes=[Bcf])

    def inproj_xr_hist(n):
        for c in range(8):
            def cons(j, ps, ps_b, c=c):
                S.op("act", lambda e: e.copy(out=hist[:, c, 0:3], in_=ps[:, n - 3:n]), reads=[ps_b], writes=[Bhist])
            linear(L["w_in"], D, O_XR + c * 128, 128, lambda kc: hT[:, kc, :n], [BhT], n, cons)

    def inproj_v(n, blk0):
        st, st_b = stg.next()
        wb_, wb_b = wbr.next()
        stv = st[:, :1024].rearrange("p (k n) -> p k n", k=8)
        wb3 = wb_[:, :1024].rearrange("p (k n) -> p k n", k=8)
        S.dma("sp", st_b.name, stv, L["w_in"].rearrange("(kc p) n -> p kc n", p=128)[:, :, O_V:O_V + 128], writes=[st_b])
        S.op("pool", lambda e, wb_=wb_, st=st: e.tensor_copy(out=wb_[:, :1024], in_=st[:, :1024]), reads=[st_b], writes=[wb_b])
        for i in range(n // 128):
            ps, ps_b = psum()

            def mmv(e, ps=ps, i=i):
                ins = None
                for kc in range(8):
                    ins = e.matmul(ps[:, :128], lhsT=hT[:, kc, i * 128:(i + 1) * 128], rhs=wb3[:, kc, :],
                                   start=(kc == 0), stop=(kc == 7))
                return ins
            S.op("pe", mmv, reads=[BhT, wb_b], writes=[ps_b])
            S.op("act", lambda e, ps=ps, i=i: e.copy(out=vtm[:, blk0 + i, :], in_=ps[:, :128]), reads=[ps_b],
                 writes=[Bvtm])

    def p23_prelude(l):
        sink32, Bsink32 = tB.next()
        for kh_ in range(2):
            S.dma("sp", "sink32", sink32[kh_ * 32:kh_ * 32 + 1, :], sinks_all[l][0:1, kh_ * 512:(kh_ + 1) * 512],
                  writes=[Bsink32])
        for kh_ in range(2):
            S.op("act", lambda e, kh_=kh_, sink32=sink32: e.activation(
                out=esink[kh_ * 32:kh_ * 32 + 1, :], in_=sink32[kh_ * 32:kh_ * 32 + 1, :], func=AF.Exp),
                reads=[Bsink32], writes=[Besink])
        S.dma("sp", "xt", xt[:, :, :256], memT_in, writes=[Bxt])
        rmsnorm(xt, Bxt, 256, V_GMEM, hT, BhT)

        def cons_km(j, ps, ps_b):
            S.op("act", lambda e: e.copy(out=kmT[:, j, :], in_=ps[:, :256]), reads=[ps_b], writes=[BkmT])
        linear(w_memkv_all[l], D, 0, 512, lambda kc: hT[:, kc, :256], [BhT], 256, cons_km)
        st, st_b = stg.next()
        wbv_, wbv_b = wbr.next()
        stv = st.rearrange("p (k n) -> p k n", k=8)
        wv3 = wbv_.rearrange("p (k n) -> p k n", k=8)
        S.dma("sp", st_b.name, stv, w_memkv_all[l].rearrange("(kc p) n -> p kc n", p=128)[:, :, 512:1024], writes=[st_b])
        S.op("pool", lambda e, wbv_=wbv_, st=st: e.tensor_copy(out=wbv_, in_=st), reads=[st_b], writes=[wbv_b])
        for mb in range(2):
            ps, ps_b = psum()

            def mmv(e, ps=ps, mb=mb):
                ins = None
                for kc in range(8):
                    ins = e.matmul(ps, lhsT=hT[:, kc, mb * 128:(mb + 1) * 128], rhs=wv3[:, kc, :], start=(kc == 0),
                                   stop=(kc == 7))
                return ins
            S.op("pe", mmv, reads=[BhT, wbv_b], writes=[ps_b])
            S.op("act", lambda e, ps=ps, mb=mb: e.copy(out=vm[:, mb, :], in_=ps), reads=[ps_b], writes=[Bvm])


    def final_norm_store(t0):
        vo = L["vo"]
        rstd, Brstd = tB.next()
        for c in range(8):
            S.op("act", lambda e, c=c: e.activation(out=big32[:, c, :], in_=xt[:, c, :], func=AF.Square),
                 reads=[Bxt], writes=[Bbig])
        ps, ps_b = psum()

        def mmf(e, ps=ps):
            ins = None
            for c in range(8):
                ins = e.matmul(ps, lhsT=ones32, rhs=big32[:, c, :], start=(c == 0), stop=(c == 7))
            return ins
        S.op("pe", mmf, reads=[Bbig, Bones32], writes=[ps_b])
        S.op("act", lambda e, ps=ps: e.activation(out=rstd, in_=ps, func=AF.Sqrt, bias=EPS, scale=1.0), reads=[ps_b],
             writes=[Brstd])
        S.op("dve", lambda e: e.reciprocal(out=rstd, in_=rstd), reads=[Brstd], writes=[Brstd])
        for c in range(8):
            S.op("dve", lambda e, c=c: e.scalar_tensor_tensor(
                out=big32[:, c, :], in0=xt[:, c, :], scalar=vecs[:, vo + V_GFIN + c:vo + V_GFIN + c + 1], in1=rstd,
                op0=ALU.mult, op1=ALU.mult), reads=[Bxt, Brstd, Bvecs, Bbig], writes=[Bbig])
        S.dma("sp", "y_out", y_out[:, :, t0:t0 + T], big32, reads=[Bbig], writes=[Byo])

    def p23_phase(l, last):
        vo = L["vo"]
        HN = 128
        S.dma("sp", "xt", xt[:, :, :HN].rearrange("p c n -> p (c n)") if False else xt[:, :, :HN],
              cc1_out[l].ap()[0:128, 0:1024].rearrange("p (c n) -> p c n", c=8), reads=[Bcc1o[l]], writes=[Bxt])
        S.op("dve", lambda e: e.tensor_scalar(out=xt[:, :, :HN], in0=xt[:, :, :HN], scalar1=vecs[:, V_OMF0:V_OMF0 + 1],
                                              scalar2=None, op0=ALU.mult), reads=[Bxt, Bvecs], writes=[Bxt])
        S.dma("sp", "state", state, cc1_out[l].ap()[0:128, 1024:1032], reads=[Bcc1o[l]], writes=[Bstate])
        S.op("dve", lambda e: e.tensor_scalar(out=state, in0=state, scalar1=vecs[:, V_OMF0:V_OMF0 + 1],
                                              scalar2=None, op0=ALU.mult), reads=[Bstate, Bvecs], writes=[Bstate])
        rmsnorm(xt, Bxt, HN, V_GM, hT, BhT)

        def cons_k0(j, ps, ps_b):
            S.op("act", lambda e: e.copy(out=kT[:, 0:128], in_=ps[:, :128]), reads=[ps_b], writes=[BkT])
        linear(L["w_in"], D, O_K, 128, lambda kc: hT[:, kc, :HN], [BhT], HN, cons_k0)
        inproj_v(HN, 0)
        inproj_xr_hist(HN)

        for ti in range(ntiles):
            t0 = ti * T
            n = T
            S.dma("sp", "xt", xt, xres[:, :, t0:t0 + T], reads=[Bxres[ti]], writes=[Bxt])
            rmsnorm(xt, Bxt, T, V_GM, hT, BhT)
            rh = lambda kc: hT[:, kc, :n]

            def cons_q(j, ps, ps_b):
                S.op("act", lambda e: e.copy(out=qT[:, j, :], in_=ps), reads=[ps_b], writes=[BqT])
            linear(L["w_in"], D, O_Q, 512, rh, [BhT], n, cons_q)

            def cons_k(j, ps, ps_b):
                S.op("act", lambda e: e.copy(out=kT[:, 128:128 + T], in_=ps), reads=[ps_b], writes=[BkT])
            linear(L["w_in"], D, O_K, 128, rh, [BhT], n, cons_k)
            inproj_v(T, 1)

            def cons_cq(j, ps, ps_b):
                S.op("act", lambda e: e.copy(out=cqT[:, j, :], in_=ps), reads=[ps_b], writes=[BcqT])
            linear(L["w_in"], D, O_CQ, 512, rh, [BhT], n, cons_cq)

            for bi in range(4):
                for kh in range(2):
                    r0 = kh * 64
                    pt, pt_b = pT.next()
                    for kb in range(2):
                        ps, ps_b = psum()
                        kcol = (bi + kb) * 128
                        S.op("pe", lambda e, ps=ps, kcol=kcol, r0=r0, bi=bi: e.matmul(
                            ps, lhsT=kT[r0:r0 + 64, kcol:kcol + 128], rhs=qT[r0:r0 + 64, :, bi * 128:(bi + 1) * 128],
                            start=True, stop=True), reads=[BkT, BqT], writes=[ps_b])
                        sc, sc_b = tB.next()
                        bsrc, bsrc_b = biasT[:, kh, kb], BbiasT
                        S.op("dve", lambda e, sc=sc, ps=ps, bsrc=bsrc: e.scalar_tensor_tensor(
                            out=sc.rearrange("p (g q) -> p g q", g=4), in0=ps.rearrange("p (g q) -> p g q", g=4), scalar=0.125,
                            in1=bsrc, op0=ALU.mult, op1=ALU.add), reads=[ps_b, bsrc_b], writes=[sc_b])
                        if ti == 0 and bi == 0 and kb == 0:
                            S.op("dve", lambda e, sc=sc: e.tensor_scalar(out=sc, in0=sc, scalar1=vecs[:, V_MASKA:V_MASKA + 1],
                                                                        scalar2=None, op0=ALU.add),
                                 reads=[sc_b, Bvecs], writes=[sc_b])
                        S.op("act", lambda e, sc=sc, pt=pt, kb=kb: e.activation(out=pt[:, kb, :], in_=sc, func=AF.Exp),
                             reads=[sc_b], writes=[pt_b])
                    psn, psn_b = psum()

                    def mmn(e, psn=psn, pt=pt, kh=kh, bi=bi):
                        ins = None
                        for kb in range(2):
                            ins = e.matmul(psn[0:64, :], lhsT=vtm[:, bi + kb, kh * 64:(kh + 1) * 64], rhs=pt[:, kb, :],
                                           start=(kb == 0), stop=(kb == 1))
                        return ins
                    S.op("pe", mmn, reads=[Bvtm, pt_b], writes=[psn_b])
                    psd, psd_b = psum()

                    def mmd(e, psd=psd, pt=pt, kh=kh):
                        for kb in range(2):
                            e.matmul(psd[0:64, :], lhsT=onesb[:, 0:64], rhs=pt[:, kb, :], start=(kb == 0), stop=False)
                        return e.matmul(psd[0:64, :], lhsT=onesb[kh * 32:kh * 32 + 1, 0:64], rhs=esink[kh * 32:kh * 32 + 1, :],
                                        start=False, stop=True)
                    S.op("pe", mmd, reads=[Bonesb, pt_b, Besink], writes=[psd_b])
                    rd, rd_b = tB.next()
                    S.op("dve", lambda e, rd=rd, psd=psd: e.reciprocal(out=rd[0:64, :], in_=psd[0:64, :]), reads=[psd_b],
                         writes=[rd_b])
                    S.op("dve", lambda e, rd=rd, psn=psn, kh=kh, bi=bi: e.tensor_tensor(
                        out=attnT[:, kh * 4:(kh + 1) * 4, bi * 128:(bi + 1) * 128],
                        in0=psn[0:64, :].rearrange("p (g q) -> p g q", g=4), in1=rd[0:64, :].rearrange("p (g q) -> p g q", g=4),
                        op=ALU.mult), reads=[psn_b, rd_b], writes=[BattnT])
            S.op("pool", lambda e: e.tensor_copy(out=kT[:, 0:128], in_=kT[:, T:T + 128]), reads=[BkT], writes=[BkT])
            S.op("pool", lambda e: e.tensor_copy(out=vtm[:, 0, :], in_=vtm[:, 4, :]), reads=[Bvtm], writes=[Bvtm])

            for hd in range(4):
                pt, pt_b = pT.next()
                for mb in range(2):
                    ps, ps_b = psum()
                    S.op("pe", lambda e, ps=ps, hd=hd, mb=mb: e.matmul(ps, lhsT=kmT[:, hd, mb * 128:(mb + 1) * 128],
                                                                    rhs=cqT[:, hd, :], start=True, stop=True),
                         reads=[BkmT, BcqT], writes=[ps_b])
                    S.op("act", lambda e, ps=ps, pt=pt, mb=mb: e.activation(out=pt[:, mb, :], in_=ps, func=AF.Exp,
                                                                         scale=128.0 ** -0.5), reads=[ps_b], writes=[pt_b])
                psn, psn_b = psum()

                def mmn2(e, psn=psn, pt=pt, hd=hd):
                    ins = None
                    for mb in range(2):
                        ins = e.matmul(psn, lhsT=vm[:, mb, hd * 128:(hd + 1) * 128], rhs=pt[:, mb, :], start=(mb == 0),
                                       stop=(mb == 1))
                    return ins
                S.op("pe", mmn2, reads=[Bvm, pt_b], writes=[psn_b])
                psd, psd_b = psum()

                def mmd2(e, psd=psd, pt=pt):
                    ins = None
                    for mb in range(2):
                        ins = e.matmul(psd, lhsT=onesb, rhs=pt[:, mb, :], start=(mb == 0), stop=(mb == 1))
                    return ins
                S.op("pe", mmd2, reads=[Bonesb, pt_b], writes=[psd_b])
                rd, rd_b = tB.next()
                S.op("dve", lambda e, rd=rd, psd=psd: e.reciprocal(out=rd, in_=psd), reads=[psd_b], writes=[rd_b])
                S.op("dve", lambda e, rd=rd, psn=psn, hd=hd: e.tensor_tensor(out=memoT[:, hd, :], in0=psn, in1=rd, op=ALU.mult),
                     reads=[psn_b, rd_b], writes=[BmemoT])

            lru(T, ti == 0, ti)

            branches = [
                (w_attn_all[l], 512, 64, lambda kc: attnT[:, kc, :], BattnT),
                (w_lru_all[l], D, 128, lambda kc: lruT[:, kc, :], BlruT),
                (w_memo_all[l], 512, 128, lambda kc: memoT[:, kc, :], BmemoT),
            ]
            for b, (Wb, Kb, kpb, rb, Brb) in enumerate(branches):
                sg_sb = {}
                for g0 in range(0, 8, 4):
                    def cons_g(j, ps, ps_b, b=b, g0=g0):
                        t, t_b = tB.next()
                        jj = g0 + j
                        S.op("act", lambda e, t=t, ps=ps: e.activation(
                            out=t, in_=ps, func=AF.Sigmoid, bias=vecs[:, vo + V_GB + b * 8 + jj:vo + V_GB + b * 8 + jj + 1]),
                            reads=[ps_b, Bvecs], writes=[t_b])
                        sg_sb[jj] = (t, t_b)
                    linear(L["w_in"], D, O_G + b * 1024 + g0 * 128, 512, rh, [BhT], n, cons_g)

                    def cons_b(j, ps, ps_b, b=b, g0=g0):
                        jj = g0 + j
                        t, t_b = sg_sb[jj]
                        if b == 0:
                            S.op("dve", lambda e, t=t, ps=ps: e.tensor_tensor(out=big32[:, jj, :], in0=t, in1=ps, op=ALU.mult),
                                 reads=[t_b, ps_b], writes=[Bbig])
                        else:
                            S.op("dve", lambda e, t=t, ps=ps: e.tensor_tensor(out=t, in0=t, in1=ps, op=ALU.mult),
                                 reads=[t_b, ps_b], writes=[t_b])
                            if b == 1:
                                S.op("pool", lambda e, t=t: e.tensor_tensor(out=big32[:, jj, :], in0=big32[:, jj, :], in1=t,
                                                                            op=ALU.add), reads=[t_b, Bbig], writes=[Bbig])
                            else:
                                S.op("pool", lambda e, t=t: e.tensor_tensor(out=merged[:, jj, :], in0=big32[:, jj, :], in1=t,
                                                                            op=ALU.add), reads=[t_b, Bbig], writes=[Bmerged])
                    linear(Wb, Kb, g0 * 128, 512, rb, [Brb], n, cons_b, kp=kpb)

            def cons_o(j, ps, ps_b):
                S.op("dve", lambda e, ps=ps: e.tensor_tensor(out=xt[:, j, :], in0=xt[:, j, :], in1=ps, op=ALU.add),
                     reads=[ps_b, Bxt], writes=[Bxt])
            linear(w_out_all[l], D, 0, D, lambda kc: merged[:, kc, :], [Bmerged], n, cons_o)

            ffn(T, V_G2)
            if not last:
                S.dma("sp", "xres", xres[:, :, t0:t0 + T], xt, reads=[Bxt], writes=[Bxres[ti]])
            else:
                final_norm_store(t0)

    Bcc1i = [Buf("cc1i%d" % l) for l in range(depth)]
    Bcc1o = [Buf("cc1o%d" % l) for l in range(depth)]
    Bcc2i = [Buf("cc2i%d" % l) for l in range(depth)]
    Bcc2o = [Buf("cc2o%d" % l) for l in range(depth)]
    Byo = Buf("y_out")

    def p1_phase(l):
        S.op("dve", lambda e: e.memset(state, 0.0), writes=[Bstate])
        if l == 0:
            S.dma("sp", "xt", xt[:, :, :4], xh4_in, writes=[Bxt])
        else:
            S.dma("sp", "xt", xt[:, :, :4], cc2_out[l - 1].ap()[0:128, :].rearrange("p (c n) -> p c n", c=8),
                  reads=[Bcc2o[l - 1]], writes=[Bxt])
            S.op("dve", lambda e: e.tensor_scalar(out=xt[:, :, :4], in0=xt[:, :, :4], scalar1=vecs[:, V_OMF0:V_OMF0 + 1],
                                                  scalar2=None, op0=ALU.mult), reads=[Bxt, Bvecs], writes=[Bxt])
        ffn(4, V_G1)
        rmsnorm(xt, Bxt, 4, V_GM, hT, BhT)
        inproj_xr_hist(4)
        for ti in range(ntiles):
            t0 = ti * T
            if l == 0:
                S.dma("sp", "xt", xt, x_in[:, :, t0:t0 + T], writes=[Bxt])
            else:
                S.dma("sp", "xt", xt, xres[:, :, t0:t0 + T], reads=[Bxres[ti]], writes=[Bxt])
            ffn(T, V_G1)
            S.dma("sp", "xres", xres[:, :, t0:t0 + T], xt, reads=[Bxt], writes=[Bxres[ti]])
            rmsnorm(xt, Bxt, T, V_GM, hT, BhT)
            lru(T, ti == 0, ti)

    def exchange1(l):
        S.dma("sp", "cc1i", cc1_in[l].ap()[:, 0:1024].rearrange("p (c n) -> p c n", c=8), xres[:, :, NT - 128:NT],
              reads=[Bxres[ntiles - 1]], writes=[Bcc1i[l]])
        S.dma("sp", "cc1i", cc1_in[l].ap()[:, 1024:1032], state, reads=[Bstate], writes=[Bcc1i[l]])
        S.coll("cc", lambda e: e.collective_compute("AllGather", ALU.bypass, replica_groups=RG,
                                                    ins=[cc1_in[l].ap().opt()], outs=[cc1_out[l].ap().opt()]),
               reads=[Bcc1i[l]], writes=[Bcc1o[l]])

    def exchange2(l):
        S.dma("sp", "cc2i", cc2_in[l].ap().rearrange("p (c n) -> p c n", c=8), xres[:, :, NT - 4:NT],
              reads=[Bxres[ntiles - 1]], writes=[Bcc2i[l]])
        S.coll("cc", lambda e: e.collective_compute("AllGather", ALU.bypass, replica_groups=RG,
                                                    ins=[cc2_in[l].ap().opt()], outs=[cc2_out[l].ap().opt()]),
               reads=[Bcc2i[l]], writes=[Bcc2o[l]])

    for l in range(depth):
        S.suffix = "_L%d" % l
        L["vo"] = l * NV
        L["w_in"] = w_in_all[l]
        layer_prelude(l)
        L["p23"] = False
        L["wg"], L["wu"], L["wd"] = f1[0][l], f1[1][l], f1[2][l]
        p1_phase(l)
        exchange1(l)
        L["p23"] = True
        L["wg"], L["wu"], L["wd"] = f2[0][l], f2[1][l], f2[2][l]
        p23_prelude(l)
        p23_phase(l, l == depth - 1)
        if l < depth - 1:
            exchange2(l)
    S.finish([Byo])
    S.emit()
    return nc


def _fm(v):
    return np.ascontiguousarray(np.asarray(v, np.float32).reshape(-1, 128).T)


def _to_fm(xtok):
    t = xtok.shape[0]
    return np.ascontiguousarray(xtok.T.reshape(8, 128, t).transpose(1, 0, 2))


def _from_fm(xfm):
    t = xfm.shape[2]
    return np.ascontiguousarray(xfm.transpose(1, 0, 2).reshape(1024, t).T)


def _alibi_tables():
    slopes = np.array([2.0 ** (-8.0 * (i + 1) / 8) for i in range(8)], dtype=np.float32)
    j = np.arange(128)[:, None]
    i = np.arange(128)[None, :]
    tab = np.zeros((128, 2, 2, 4, 128), np.float32)
    for kh in range(2):
        for kb in range(2):
            kpos = kb * 128 + j
            qpos = 128 + i
            dist = np.abs(qpos - kpos).astype(np.float32)
            kch = kpos // 64
            qch = qpos // 64
            valid = (kch >= qch - 2) & (kch <= qch)
            for g in range(4):
                tab[:, kh, kb, g, :] = np.where(valid, -slopes[kh * 4 + g] * dist, np.float32(NEG))
    return tab


_PROGS = {}


def _prog(depth):
    if depth not in _PROGS:
        _PROGS[depth] = build(depth)
    return _PROGS[depth]


def _prep_inputs(x, mem, ffn1_norm, ffn1_w_gate, ffn1_w_up, ffn1_w_down, mix_norm, w_in, gate_bias,
                 attn_sinks, w_attn_out, conv_w, conv_b, lru_wa, lru_ba, lru_wx, lru_bx, lru_lambda,
                 w_lru_out, mem_norm, w_mem_kv, w_mem_out, w_out, ffn2_norm, ffn2_w_gate, ffn2_w_up,
                 ffn2_w_down, final_norm, depth=None):
    f32 = np.float32
    x = np.asarray(x, f32)
    mem = np.asarray(mem, f32)
    if depth is None:
        depth = ffn1_norm.shape[0]
    ncores = 8
    ca = lambda a: np.ascontiguousarray(np.asarray(a, f32)[:depth])
    tab = _alibi_tables()
    tabF_A = tab[:, :, 0].copy()
    tabF_A[...] = NEG
    tabF_B = np.ascontiguousarray(tab[:, :, 0])
    qperm = np.concatenate([np.concatenate([np.arange(j * 64, j * 64 + 64), np.arange((4 + j) * 64, (4 + j) * 64 + 64)])
                            for j in range(4)])
    win = np.array(np.asarray(w_in, f32)[:depth], copy=True)
    win[:, :, :512] = win[:, :, qperm]
    win = np.ascontiguousarray(win)
    wbd = np.zeros((depth, 128, 16, 128), f32)
    for l in range(depth):
        for gi_, wsrc in enumerate((lru_wa[l], lru_wx[l])):
            for c in range(8):
                wbd[l, 0:64, gi_ * 8 + c, 0:64] = wsrc[2 * c]
                wbd[l, 64:128, gi_ * 8 + c, 64:128] = wsrc[2 * c + 1]
    sinks = np.ascontiguousarray(np.repeat(np.asarray(attn_sinks, f32)[:depth], 128, axis=1)[:, None, :])
    shared = {"w_in": win, "wbd": wbd,
              "f1_wg": ca(ffn1_w_gate), "f1_wu": ca(ffn1_w_up), "f1_wd": ca(ffn1_w_down),
              "f2_wg": ca(ffn2_w_gate), "f2_wu": ca(ffn2_w_up), "f2_wd": ca(ffn2_w_down),
              "biasT": tab, "sinks": sinks, "w_attn": ca(w_attn_out), "w_lru": ca(w_lru_out),
              "w_memkv": ca(w_mem_kv), "w_memo": ca(w_mem_out), "w_out": ca(w_out)}
    in_maps = []
    for c in range(ncores):
        b, half = c // 2, c % 2
        v = np.zeros((128, depth * NV), f32)
        for l in range(depth):
            o = l * NV
            v[:, o + V_G1:o + V_G1 + 8] = _fm(ffn1_norm[l])
            v[:, o + V_GM:o + V_GM + 8] = _fm(mix_norm[l])
            v[:, o + V_G2:o + V_G2 + 8] = _fm(ffn2_norm[l])
            for j in range(4):
                v[:, o + V_CW + j * 8:o + V_CW + j * 8 + 8] = _fm(conv_w[l, j])
            v[:, o + V_CB:o + V_CB + 8] = _fm(conv_b[l])
            v[:, o + V_BA:o + V_BA + 8] = _fm(lru_ba[l])
            v[:, o + V_BX:o + V_BX + 8] = _fm(lru_bx[l])
            v[:, o + V_LAM:o + V_LAM + 8] = _fm(lru_lambda[l])
            v[:, o + V_GB:o + V_GB + 24] = _fm(gate_bias[l])
            v[:, o + V_GMEM:o + V_GMEM + 8] = _fm(mem_norm[l])
            v[:, o + V_GFIN:o + V_GFIN + 8] = _fm(final_norm)
            v[:, o + V_F0] = 1.0 if half == 0 else 0.0
            v[:, o + V_OMF0] = 0.0 if half == 0 else 1.0
            v[:, o + V_MASKA] = NEG if half == 0 else 0.0
        xs = _to_fm(x[b, half * NT:(half + 1) * NT, :])
        if half == 0:
            xh4 = np.zeros((128, 8, 4), f32)
        else:
            xh4 = _to_fm(x[b, NT - 4:NT, :])
        m = dict(shared)
        m.update({"x_in": xs, "xh4_in": xh4, "vecs": v,
                  "memT": _to_fm(mem[b])})
        in_maps.append(m)
    return in_maps, depth


def kernel(x, mem, ffn1_norm, ffn1_w_gate, ffn1_w_up, ffn1_w_down, mix_norm, w_in, gate_bias,
           attn_sinks, w_attn_out, conv_w, conv_b, lru_wa, lru_ba, lru_wx, lru_bx, lru_lambda,
           w_lru_out, mem_norm, w_mem_kv, w_mem_out, w_out, ffn2_norm, ffn2_w_gate, ffn2_w_up,
           ffn2_w_down, final_norm):
    in_maps, depth = _prep_inputs(x, mem, ffn1_norm, ffn1_w_gate, ffn1_w_up, ffn1_w_down, mix_norm, w_in,
                                  gate_bias, attn_sinks, w_attn_out, conv_w, conv_b, lru_wa, lru_ba, lru_wx,
                                  lru_bx, lru_lambda, w_lru_out, mem_norm, w_mem_kv, w_mem_out, w_out,
                                  ffn2_norm, ffn2_w_gate, ffn2_w_up, ffn2_w_down, final_norm)
    ncores = 8
    res = run_bass_kernel_spmd(_prog(depth), in_maps, core_ids=list(range(ncores))).results
    B = np.asarray(x).shape[0]
    out = np.empty((B, 2 * NT, D), np.float32)
    for c in range(ncores):
        out[c // 2, (c % 2) * NT:(c % 2 + 1) * NT, :] = _from_fm(np.asarray(res[c]["y_out"], np.float32))
    return out
```

```python
import numpy as np
import concourse.bass as bass
import concourse.mybir as mybir
from concourse.bass_utils import run_bass_kernel_spmd

F32 = mybir.dt.float32
BF16 = mybir.dt.bfloat16
AF = mybir.ActivationFunctionType
ALU = mybir.AluOpType

D = 1024
NT = 4096
T = 512
DFF = 2816
INW = 6400
O_Q, O_K, O_V, O_XR, O_YR, O_CQ, O_G = 0, 512, 640, 768, 1792, 2816, 3328
NV = 136
V_G1, V_GM, V_G2, V_CW, V_CB, V_BA, V_BX, V_LAM, V_GB, V_GMEM, V_GFIN, V_F0, V_OMF0, V_MASKA = \
    0, 8, 16, 24, 56, 64, 72, 80, 88, 112, 120, 128, 129, 130
EPS = 1e-6
NEG = -30000.0

ENGS = ("pe", "act", "dve", "pool", "sp")


class Buf:
    __slots__ = ("name", "w", "r")

    def __init__(self, name):
        self.name = name
        self.w = None
        self.r = {}


class Sched:
    def __init__(self, nc):
        self.nc = nc
        self.ops = {e: [] for e in ENGS}
        self.cnt = {e: 0 for e in ENGS}
        self.sem = {e: nc.alloc_semaphore("s_" + e) for e in ENGS}
        self.dsem = {}
        self.dcnt = {}
        self.waited = {}
        self.suffix = ""

    def _need(self, eng, dep, waits):
        if dep is None:
            return
        kind, key, val = dep
        if kind == "eng" and key == "pe" and eng == "pe":
            return
        if self.waited.get((eng, kind, key), 0) >= val:
            return
        if val > waits.get((kind, key), 0):
            waits[(kind, key)] = val

    def _collect(self, eng, reads, writes):
        waits = {}
        for b in reads:
            self._need(eng, b.w, waits)
        for b in writes:
            self._need(eng, b.w, waits)
            for d in b.r.values():
                self._need(eng, d, waits)
        for (kind, key), val in waits.items():
            self.waited[(eng, kind, key)] = val
        return waits

    def _mark(self, me, reads, writes):
        for b in reads:
            b.r[(me[0], me[1])] = me
        for b in writes:
            b.w = me
            b.r = {}

    def _emit_waits(self, e, waits):
        for (kind, key), val in waits.items():
            e.wait_ge(self.sem[key] if kind == "eng" else self.dsem[key], val)

    def op(self, eng, fn, reads=(), writes=()):
        waits = self._collect(eng, reads, writes)
        self.cnt[eng] += 1
        self._mark(("eng", eng, self.cnt[eng]), reads, writes)
        sem = self.sem[eng]

        def run(e, fn=fn, waits=waits, sem=sem):
            self._emit_waits(e, waits)
            fn(e).then_inc(sem, 1)
        self.ops[eng].append(run)

    def coll(self, key, fn, reads=(), writes=()):
        key = key + self.suffix
        if key not in self.dsem:
            self.dsem[key] = self.nc.alloc_semaphore("c_" + key)
            self.dcnt[key] = 0
        waits = self._collect("pool", reads, writes)
        self.dcnt[key] += 1
        self._mark(("dma", key, self.dcnt[key]), reads, writes)
        sem = self.dsem[key]

        def run(e, waits=waits, sem=sem, fn=fn):
            self._emit_waits(e, waits)
            fn(e).then_inc(sem, 1)
        self.ops["pool"].append(run)

    def dma(self, queue, key, out, in_, reads=(), writes=(), nosuffix=False):
        if not nosuffix:
            key = key + self.suffix
        if key not in self.dsem:
            self.dsem[key] = self.nc.alloc_semaphore("d_" + key)
            self.dcnt[key] = 0
        waits = self._collect(queue, reads, writes)
        self.dcnt[key] += 16
        self._mark(("dma", key, self.dcnt[key]), reads, writes)
        sem = self.dsem[key]

        def run(e, waits=waits, sem=sem, out=out, in_=in_):
            self._emit_waits(e, waits)
            e.dma_start(out=out, in_=in_).then_inc(sem, 16)
        self.ops[queue].append(run)

    def finish(self, final_bufs):
        waits = self._collect("sp", final_bufs, ())
        self.ops["sp"].append(lambda e, waits=waits: self._emit_waits(e, waits))

    def emit(self):
        nc = self.nc
        with nc.Block() as block:
            @block.tensor
            def _(e):
                for f in self.ops["pe"]:
                    f(e)

            @block.scalar
            def _(e):
                for f in self.ops["act"]:
                    f(e)

            @block.vector
            def _(e):
                for f in self.ops["dve"]:
                    f(e)

            @block.gpsimd
            def _(e):
                for f in self.ops["pool"]:
                    f(e)

            @block.sync
            def _(e):
                for f in self.ops["sp"]:
                    f(e)


class Rot:
    def __init__(self, nc, name, shape, dtype, n):
        self.items = []
        for i in range(n):
            nm = "%s%d" % (name, i)
            self.items.append((nc.alloc_sbuf_tensor(nm, shape, dtype).ap(), Buf(nm)))
        self.i = 0

    def next(self):
        it = self.items[self.i % len(self.items)]
        self.i += 1
        return it


def build(depth=4, ntiles=NT // T):
    nc = bass.Bass("TRN2", target_bir_lowering=False)
    S = Sched(nc)
    L = {"vo": 0, "p23": False}

    def din(name, shape):
        return nc.dram_tensor(name, shape, F32, kind="ExternalInput").ap()

    def dout(name, shape):
        return nc.dram_tensor(name, shape, F32, kind="ExternalOutput").ap()

    x_in = din("x_in", [128, 8, NT])
    xh4_in = din("xh4_in", [128, 8, 4])
    vecs_in = din("vecs", [128, depth * NV])
    w_in_all = din("w_in", [depth, D, INW])
    wbd_all = din("wbd", [depth, 128, 16, 128])
    f1 = [din("f1_wg", [depth, D, DFF]), din("f1_wu", [depth, D, DFF]), din("f1_wd", [depth, DFF, D])]
    f2 = [din("f2_wg", [depth, D, DFF]), din("f2_wu", [depth, D, DFF]), din("f2_wd", [depth, DFF, D])]
    bias_in = din("biasT", [128, 2, 2, 4, 128])
    sinks_all = din("sinks", [depth, 1, 1024])
    memT_in = din("memT", [128, 8, 256])
    w_attn_all = din("w_attn", [depth, 512, D])
    w_lru_all = din("w_lru", [depth, D, D])
    w_memkv_all = din("w_memkv", [depth, D, D])
    w_memo_all = din("w_memo", [depth, 512, D])
    w_out_all = din("w_out", [depth, D, D])
    y_out = dout("y_out", [128, 8, NT])
    xres = nc.dram_tensor("xres", [128, 8, NT], F32).ap()
    Bxres = [Buf("xres%d" % i) for i in range(ntiles)]
    CW1 = 8 * 128 + 8
    cc1_in = [nc.dram_tensor("cc1_in%d" % l, [128, CW1], F32) for l in range(depth)]
    cc1_out = [nc.dram_tensor("cc1_out%d" % l, [256, CW1], F32) for l in range(depth)]
    cc2_in = [nc.dram_tensor("cc2_in%d" % l, [128, 32], F32) for l in range(depth)]
    cc2_out = [nc.dram_tensor("cc2_out%d" % l, [256, 32], F32) for l in range(depth)]
    RG = [[0, 1], [2, 3], [4, 5], [6, 7]]
    rg_d = nc.dram_tensor("rg_d", [128, 2, 8, NT], F32).ap()
    Brg = [[[Buf("rg%d_%d_%d" % (i, c, k)) for k in range(2)] for c in range(8)] for i in range(ntiles)]
    p23 = True

    def sb(name, shape, dt=F32):
        return nc.alloc_sbuf_tensor(name, shape, dt).ap()

    vecs = sb("vecs_s", [128, depth * NV]); Bvecs = Buf("vecs")
    cf = sb("cf", [128, 32]); Bcf = Buf("cf")
    tmpv = sb("tmpv", [128, 4, 8]); Btmpv = Buf("tmpv")
    ones32 = sb("ones32", [128, 128]); Bones32 = Buf("ones32")
    onesb = sb("onesb", [128, 128], BF16); Bonesb = Buf("onesb")
    wbd = sb("wbd_s", [128, 16, 128], BF16); Bwbd = Buf("wbd")
    xt = sb("xt", [128, 8, T]); Bxt = Buf("xt")
    big32 = sb("big32", [128, 8, T]); Bbig = Buf("big32")
    hT = sb("hT", [128, 8, T], BF16); BhT = Buf("hT")
    hist = sb("hist", [128, 8, 4]); Bhist = Buf("hist")
    state = sb("state", [128, 8]); Bstate = Buf("state")
    FH = 11
    actT = sb("actT", [128, FH, T], BF16); Bact = Buf("actT")
    stg = Rot(nc, "stg", [128, 4096], F32, 2)
    wbr = Rot(nc, "wbf", [128, 4096], BF16, 2)
    tA = Rot(nc, "tA", [128, T + 4], F32, 4)
    tB = Rot(nc, "tB", [128, T], F32, 20)
    tC = Rot(nc, "tC", [128, T], BF16, 4)
    psb = [(nc.alloc_psum_tensor("ps%d" % i, [128, 512], F32).ap(), Buf("ps%d" % i)) for i in range(8)]
    psi = [0]

    def psum():
        it = psb[psi[0] % 8]
        psi[0] += 1
        return it

    casti = [0]

    def cast_eng():
        casti[0] += 1
        return "pool" if casti[0] % 2 else "act"

    def copy_op(eng, out, in_):
        if eng == "act":
            return lambda e: e.copy(out=out, in_=in_)
        return lambda e: e.tensor_copy(out=out, in_=in_)

    if p23:
        biasT = sb("biasT_s", [128, 2, 2, 4, 128]); BbiasT = Buf("biasT")
        esink = sb("esink", [64, 512], BF16); Besink = Buf("esink")
        qT = sb("qT", [128, 4, T], BF16); BqT = Buf("qT")
        kT = sb("kT", [128, 128 + T], BF16); BkT = Buf("kT")
        vtm = sb("vtm", [128, 5, 128], BF16); Bvtm = Buf("vtm")
        cqT = sb("cqT", [128, 4, T], BF16); BcqT = Buf("cqT")
        attnT = sb("attnT", [64, 8, T], BF16); BattnT = Buf("attnT")
        lruT = sb("lruT", [128, 8, T], BF16); BlruT = Buf("lruT")
        memoT = sb("memoT", [128, 4, T], BF16); BmemoT = Buf("memoT")
        merged, Bmerged = lruT, BlruT
        kmT = sb("kmT", [128, 4, 256], BF16); BkmT = Buf("kmT")
        vm = sb("vm", [128, 2, 512], BF16); Bvm = Buf("vm")
        pT = Rot(nc, "pT", [128, 2, 512], BF16, 2)

    WSC_COLS = depth * 222000
    wsc = nc.dram_tensor("wsc", [128, WSC_COLS], BF16).ap()
    wcache = {}
    wsc_pos = [0]

    def linear(W, K, c0, ncols, rhs, rhs_bufs, n, consume, kp=128):
        kcs = K // kp
        nch = ncols // 128
        G = max(1, min(nch, 32 // kcs))
        Wv = W.rearrange("(kc p) n -> p kc n", p=kp)
        for g0 in range(0, nch, G):
            gn = min(G, nch - g0)
            wb, wb_b = wbr.next()
            ncol = kcs * gn * 128
            wbv = wb[:kp, :ncol].rearrange("p (k n) -> p k n", k=kcs)
            key = (W.tensor.name, W.offset, K, kp, c0, g0, gn)
            if key in wcache:
                pos, cb = wcache[key]
                S.dma("sp", wb_b.name, wb[:kp, :ncol], wsc[:kp, pos:pos + ncol], reads=[cb], writes=[wb_b])
            else:
                st, st_b = stg.next()
                stv = st[:kp, :ncol].rearrange("p (k n) -> p k n", k=kcs)
                S.dma("sp", st_b.name, stv, Wv[:, :, c0 + g0 * 128:c0 + (g0 + gn) * 128], writes=[st_b])
                ce = cast_eng()
                S.op(ce, copy_op(ce, wb[:kp, :ncol], st[:kp, :ncol]), reads=[st_b], writes=[wb_b])
                pos = wsc_pos[0]
                wsc_pos[0] += ncol
                assert wsc_pos[0] <= WSC_COLS
                cb = Buf("wsc%d" % pos)
                S.dma(ce, "wscw_" + ce, wsc[:kp, pos:pos + ncol], wb[:kp, :ncol], reads=[wb_b], writes=[cb])
                wcache[key] = (pos, cb)
            for j in range(gn):
                ps, ps_b = psum()

                def mm(e, ps=ps, wbv=wbv, j=j):
                    ins = None
                    for kc in range(kcs):
                        ins = e.matmul(ps[:, :n], lhsT=wbv[:, kc, j * 128:(j + 1) * 128], rhs=rhs(kc),
                                       start=(kc == 0), stop=(kc == kcs - 1))
                    return ins
                S.op("pe", mm, reads=[wb_b] + list(rhs_bufs), writes=[ps_b])
                consume(g0 + j, ps, ps_b)

    def rmsnorm(xsrc, Bx, n, gofs, out, Bout):
        vo = L["vo"]
        rstd, Brstd = tB.next()
        for c in range(8):
            S.op("act", lambda e, c=c: e.activation(out=big32[:, c, :n], in_=xsrc[:, c, :n], func=AF.Square),
                 reads=[Bx], writes=[Bbig])
        ps, ps_b = psum()

        def mm(e):
            ins = None
            for c in range(8):
                ins = e.matmul(ps[:, :n], lhsT=ones32, rhs=big32[:, c, :n], start=(c == 0), stop=(c == 7))
            return ins
        S.op("pe", mm, reads=[Bbig, Bones32], writes=[ps_b])
        S.op("act", lambda e: e.activation(out=rstd[:, :n], in_=ps[:, :n], func=AF.Sqrt, bias=EPS, scale=1.0),
             reads=[ps_b], writes=[Brstd])
        S.op("dve", lambda e: e.reciprocal(out=rstd[:, :n], in_=rstd[:, :n]), reads=[Brstd], writes=[Brstd])
        for c in range(8):
            S.op("dve", lambda e, c=c: e.scalar_tensor_tensor(
                out=out[:, c, :n], in0=xsrc[:, c, :n], scalar=vecs[:, vo + gofs + c:vo + gofs + c + 1], in1=rstd[:, :n],
                op0=ALU.mult, op1=ALU.mult), reads=[Bx, Brstd, Bvecs], writes=[Bout])

    def ffn(n, gofs):
        vo = L["vo"]
        rmsnorm(xt, Bxt, n, gofs, hT, BhT)
        for half in range(2):
            j0 = half * FH
            gate_sb = {}

            def cons_gate(j, ps, ps_b):
                t, t_b = tB.next()
                S.op("act", lambda e, t=t, ps=ps: e.activation(out=t[:, :n], in_=ps[:, :n], func=AF.Silu),
                     reads=[ps_b], writes=[t_b])
                gate_sb[j] = (t, t_b)

            def cons_up(j, ps, ps_b):
                t, t_b = gate_sb[j]
                S.op("dve", lambda e, t=t, ps=ps, j=j: e.tensor_tensor(out=actT[:, j, :n], in0=t[:, :n], in1=ps[:, :n],
                                                                       op=ALU.mult),
                     reads=[ps_b, t_b], writes=[Bact])
            for s0 in range(0, FH, 4):
                sn = min(4, FH - s0)
                base = j0 + s0
                linear(L["wg"], D, base * 128, sn * 128, lambda kc: hT[:, kc, :n], [BhT], n,
                       lambda j, ps, ps_b, s0=s0: cons_gate(s0 + j, ps, ps_b))
                linear(L["wu"], D, base * 128, sn * 128, lambda kc: hT[:, kc, :n], [BhT], n,
                       lambda j, ps, ps_b, s0=s0: cons_up(s0 + j, ps, ps_b))

            def cons_down(j, ps, ps_b):
                S.op("dve", lambda e, ps=ps, j=j: e.scalar_tensor_tensor(
                    out=xt[:, j, :n], in0=ps[:, :n], scalar=0.5, in1=xt[:, j, :n], op0=ALU.mult, op1=ALU.add),
                    reads=[ps_b, Bxt], writes=[Bxt])
            linear(L["wd"][j0 * 128:(j0 + FH) * 128, :], FH * 128, 0, D, lambda kc: actT[:, kc, :n], [Bact], n, cons_down)

    NB = 4

    def lru(n, first_tile, ti=0):
        vo = L["vo"]
        want_out = L["p23"]
        hsrc = lambda kc: hT[:, kc, :n]
        t0_ = ti * T
        for cb in range(0, 8, NB):
            cs = list(range(cb, min(8, cb + NB)))
            X = {}
            if want_out:
                for c in cs:
                    xc, xc_b = tB.next()
                    r, r_b = tB.next()
                    T1, T1_b = tB.next()
                    T2, T2_b = tB.next()
                    X[c] = {"xc": (xc, xc_b), "r": (r, r_b), "T1": (T1, T1_b), "T2": (T2, T2_b)}
                    S.dma("pool", "ld_r%d" % c, r[:, :n], rg_d[:, 0, c, t0_:t0_ + n], reads=[Brg[ti][c][0]],
                          writes=[r_b], nosuffix=True)
                    S.dma("pool", "ld_g%d" % c, xc[:, :n], rg_d[:, 1, c, t0_:t0_ + n], reads=[Brg[ti][c][1]],
                          writes=[xc_b], nosuffix=True)
            if not want_out:
                    for c in cs:
                        xrb, xrb_b = tA.next()
                        xc, xc_b = tB.next()
                        X[c] = {"xrb": (xrb, xrb_b), "xc": (xc, xc_b)}
                        S.op("dve", lambda e, xrb=xrb, c=c: e.tensor_copy(out=xrb[:, 0:3], in_=hist[:, c, 0:3]),
                             reads=[Bhist], writes=[xrb_b])

                        def cons_xr(j, ps, ps_b, xrb=xrb, xrb_b=xrb_b):
                            S.op("dve", lambda e: e.tensor_copy(out=xrb[:, 3:3 + n], in_=ps[:, :n]), reads=[ps_b], writes=[xrb_b])
                        linear(L["w_in"], D, O_XR + c * 128, 128, hsrc, [BhT], n, cons_xr)
                    for c in cs:
                        xrb, xrb_b = X[c]["xrb"]
                        xc, xc_b = X[c]["xc"]
                        S.op("dve", lambda e, xc=xc, xrb=xrb, c=c: e.tensor_scalar(
                            out=xc[:, :n], in0=xrb[:, 0:n], scalar1=vecs[:, vo + V_CW + c:vo + V_CW + c + 1],
                            scalar2=vecs[:, vo + V_CB + c:vo + V_CB + c + 1], op0=ALU.mult, op1=ALU.add),
                            reads=[xrb_b, Bvecs], writes=[xc_b])
                        for j in range(1, 4):
                            S.op("dve", lambda e, xc=xc, xrb=xrb, c=c, j=j: e.scalar_tensor_tensor(
                                out=xc[:, :n], in0=xrb[:, j:j + n],
                                scalar=vecs[:, vo + V_CW + j * 8 + c:vo + V_CW + j * 8 + c + 1],
                                in1=xc[:, :n], op0=ALU.mult, op1=ALU.add), reads=[xrb_b, Bvecs, xc_b], writes=[xc_b])
                        S.op("dve", lambda e, xrb=xrb, c=c: e.tensor_copy(out=hist[:, c, 0:3], in_=xrb[:, n:n + 3]),
                             reads=[xrb_b], writes=[Bhist])
                        xcb, xcb_b = tC.next()
                        X[c]["xcb"] = (xcb, xcb_b)
                        S.op("pool", lambda e, xcb=xcb, xc=xc: e.tensor_copy(out=xcb[:, :n], in_=xc[:, :n]), reads=[xc_b],
                             writes=[xcb_b])
                    for c in cs:
                        xcb, xcb_b = X[c]["xcb"]
                        psa, psa_b = psum()
                        S.op("pe", lambda e, psa=psa, xcb=xcb, c=c: e.matmul(psa[:, :n], lhsT=wbd[:, c, :], rhs=xcb[:, :n],
                                                                          start=True, stop=True),
                             reads=[Bwbd, xcb_b], writes=[psa_b])
                        psx, psx_b = psum()
                        S.op("pe", lambda e, psx=psx, xcb=xcb, c=c: e.matmul(psx[:, :n], lhsT=wbd[:, 8 + c, :], rhs=xcb[:, :n],
                                                                          start=True, stop=True),
                             reads=[Bwbd, xcb_b], writes=[psx_b])
                        X[c]["ps"] = (psa, psa_b, psx, psx_b)
                    for c in cs:
                        psa, psa_b, psx, psx_b = X[c]["ps"]
                        r, r_b = tB.next()
                        gi, gi_b = tB.next()
                        X[c]["r"] = (r, r_b)
                        X[c]["gi"] = (gi, gi_b)
                        S.op("act", lambda e, r=r, psa=psa, c=c: e.activation(out=r[:, :n], in_=psa[:, :n], func=AF.Sigmoid,
                                                                           bias=vecs[:, vo + V_BA + c:vo + V_BA + c + 1]),
                             reads=[psa_b, Bvecs], writes=[r_b])
                        S.op("act", lambda e, gi=gi, psx=psx, c=c: e.activation(out=gi[:, :n], in_=psx[:, :n], func=AF.Sigmoid,
                                                                             bias=vecs[:, vo + V_BX + c:vo + V_BX + c + 1]),
                             reads=[psx_b, Bvecs], writes=[gi_b])
                    for c in cs:
                        gi, gi_b = X[c]["gi"]
                        xc, xc_b = X[c]["xc"]
                        S.op("pool", lambda e, gi=gi, xc=xc: e.tensor_tensor(out=gi[:, :n], in0=gi[:, :n], in1=xc[:, :n],
                                                                            op=ALU.mult), reads=[gi_b, xc_b], writes=[gi_b])
            if not want_out:
                for c in cs:
                    r, r_b = X[c]["r"]
                    T1, T1_b = tB.next()
                    T2, T2_b = tB.next()
                    X[c]["T1"] = (T1, T1_b)
                    X[c]["T2"] = (T2, T2_b)
                    S.op("act", lambda e, T1=T1, r=r, c=c: e.activation(out=T1[:, :n], in_=r[:, :n], func=AF.Tanh,
                                                                     scale=cf[:, c:c + 1]), reads=[r_b, Bcf], writes=[T1_b])
                    S.op("act", lambda e, T2=T2, r=r, c=c: e.activation(out=T2[:, :n], in_=r[:, :n], func=AF.Tanh,
                                                                     scale=cf[:, 24 + c:25 + c]), reads=[r_b, Bcf], writes=[T2_b])
                for c in cs:
                    r, r_b = X[c]["r"]
                    S.op("act", lambda e, r=r, c=c: e.activation(out=r[:, :n], in_=r[:, :n], func=AF.Exp,
                                                               scale=cf[:, 8 + c:9 + c]), reads=[r_b, Bcf], writes=[r_b])
                for c in cs:
                    r, r_b = X[c]["r"]
                    xc, xc_b = X[c]["xc"]
                    S.op("pool", lambda e, r=r, xc=xc: e.tensor_tensor(out=xc[:, :n], in0=r[:, :n], in1=r[:, :n],
                                                                      op=ALU.mult), reads=[r_b, xc_b], writes=[xc_b])
                for c in cs:
                    r, r_b = X[c]["r"]
                    T1, T1_b = X[c]["T1"]
                    S.op("dve", lambda e, r=r, T1=T1: e.scalar_tensor_tensor(out=r[:, :n], in0=r[:, :n], scalar=1.0,
                                                                          in1=T1[:, :n], op0=ALU.add, op1=ALU.mult),
                         reads=[r_b, T1_b], writes=[r_b])
                    S.op("pool", lambda e, r=r: e.tensor_scalar(out=r[:, :n], in0=r[:, :n], scalar1=1.0, scalar2=None,
                                                              op0=ALU.add), reads=[r_b], writes=[r_b])
                for c in cs:
                    xc, xc_b = X[c]["xc"]
                    T2, T2_b = X[c]["T2"]
                    S.op("dve", lambda e, xc=xc, T2=T2: e.scalar_tensor_tensor(out=xc[:, :n], in0=xc[:, :n], scalar=1.0,
                                                                            in1=T2[:, :n], op0=ALU.add, op1=ALU.mult),
                         reads=[xc_b, T2_b], writes=[xc_b])
                for c in cs:
                    xc, xc_b = X[c]["xc"]
                    gi, gi_b = X[c]["gi"]
                    S.op("act", lambda e, xc=xc: e.activation(out=xc[:, :n], in_=xc[:, :n], func=AF.Sqrt), reads=[xc_b],
                         writes=[xc_b])
                    if first_tile:
                        S.op("dve", lambda e, xc=xc: e.scalar_tensor_tensor(
                            out=xc[:, 0:1], in0=xc[:, 0:1], scalar=vecs[:, vo + V_OMF0:vo + V_OMF0 + 1],
                            in1=vecs[:, vo + V_F0:vo + V_F0 + 1], op0=ALU.mult, op1=ALU.add), reads=[xc_b, Bvecs],
                            writes=[xc_b])
                    S.op("pool", lambda e, xc=xc, gi=gi: e.tensor_tensor(out=xc[:, :n], in0=xc[:, :n], in1=gi[:, :n],
                                                                        op=ALU.mult), reads=[xc_b, gi_b], writes=[xc_b])
                for c in cs:
                    r, r_b = X[c]["r"]
                    xc, xc_b = X[c]["xc"]
                    S.dma("pool", "sv_r%d" % (c % NB), rg_d[:, 0, c, t0_:t0_ + n], r[:, :n], reads=[r_b],
                          writes=[Brg[ti][c][0]], nosuffix=True)
                    S.dma("pool", "sv_g%d" % (c % NB), rg_d[:, 1, c, t0_:t0_ + n], xc[:, :n], reads=[xc_b],
                          writes=[Brg[ti][c][1]], nosuffix=True)
            for c in cs:
                r, r_b = X[c]["r"]
                xc, xc_b = X[c]["xc"]
                T2, T2_b = X[c]["T2"]
                S.op("dve", lambda e, T2=T2, r=r, xc=xc, c=c: e.tensor_tensor_scan(
                    out=T2[:, :n], data0=r[:, :n], data1=xc[:, :n], initial=state[:, c:c + 1], op0=ALU.mult,
                    op1=ALU.add), reads=[r_b, xc_b, Bstate, T2_b], writes=[T2_b])
                S.op("dve", lambda e, T2=T2, c=c: e.tensor_copy(out=state[:, c:c + 1], in_=T2[:, n - 1:n]),
                     reads=[T2_b], writes=[Bstate])
            if want_out:
                for c in cs:
                    T1, T1_b = X[c]["T1"]

                    def cons_yr(j, ps, ps_b, T1=T1, T1_b=T1_b):
                        S.op("act", lambda e: e.activation(out=T1[:, :n], in_=ps[:, :n], func=AF.Gelu_apprx_tanh),
                             reads=[ps_b, T1_b], writes=[T1_b])
                    linear(L["w_in"], D, O_YR + c * 128, 128, hsrc, [BhT], n, cons_yr)
                for c in cs:
                    T1, T1_b = X[c]["T1"]
                    T2, T2_b = X[c]["T2"]
                    S.op("dve", lambda e, T2=T2, T1=T1, c=c: e.tensor_tensor(out=lruT[:, c, :n], in0=T2[:, :n],
                                                                          in1=T1[:, :n], op=ALU.mult),
                         reads=[T2_b, T1_b], writes=[BlruT])

    S.dma("sp", "vecs", vecs, vecs_in, writes=[Bvecs])
    S.op("dve", lambda e: e.memset(ones32, 1.0 / D), writes=[Bones32])
    S.op("dve", lambda e: e.memset(onesb, 1.0), writes=[Bonesb])
    S.dma("sp", "biasT", biasT, bias_in, writes=[BbiasT])

    def layer_prelude(l):
        vo = L["vo"]
        st, st_b = stg.next()
        stv = st[:, :2048].rearrange("p (k n) -> p k n", k=16)
        S.dma("sp", st_b.name, stv, wbd_all[l], writes=[st_b])
        S.op("dve", lambda e, stv=stv: e.tensor_copy(out=wbd, in_=stv), reads=[st_b], writes=[Bwbd])
        e_ = tmpv[:, 0, :]
        acc = tmpv[:, 1, :]
        S.op("act", lambda e: e.activation(out=e_, in_=vecs[:, vo + V_LAM:vo + V_LAM + 8], func=AF.Exp, scale=-1.0),
             reads=[Bvecs], writes=[Btmpv])
        S.op("dve", lambda e: e.tensor_scalar(out=acc, in0=e_, scalar1=-1.0 / 6.0, scalar2=1.0 / 5.0, op0=ALU.mult,
                                              op1=ALU.add), reads=[Btmpv], writes=[Btmpv])
        for coef in (-1.0 / 4.0, 1.0 / 3.0, -1.0 / 2.0, 1.0):
            S.op("dve", lambda e: e.tensor_tensor(out=acc, in0=acc, in1=e_, op=ALU.mult), reads=[Btmpv], writes=[Btmpv])
            if coef < 0:
                S.op("dve", lambda e, coef=coef: e.tensor_scalar(out=acc, in0=acc, scalar1=-1.0, scalar2=-coef,
                                                                 op0=ALU.mult, op1=ALU.add), reads=[Btmpv], writes=[Btmpv])
            else:
                S.op("dve", lambda e, coef=coef: e.tensor_scalar(out=acc, in0=acc, scalar1=-1.0, scalar2=coef,
                                                                 op0=ALU.mult, op1=ALU.add), reads=[Btmpv], writes=[Btmpv])
        S.op("dve", lambda e: e.tensor_tensor(out=acc, in0=acc, in1=e_, op=ALU.mult), reads=[Btmpv], writes=[Btmpv])
        for k_, mul_ in enumerate((-4.0, -8.0, -16.0, 8.0)):
            S.op("dve", lambda e, k_=k_, mul_=mul_: e.tensor_scalar(out=cf[:, k_ * 8:(k_ + 1) * 8], in0=acc, scalar1=mul_,
                                                                    scalar2=None, op0=ALU.mult), reads=[Btmpv], writes=[Bcf])

    def inproj_xr_hist(n):
        for c in range(8):
            def cons(j, ps, ps_b, c=c):
                S.op("act", lambda e: e.copy(out=hist[:, c, 0:3], in_=ps[:, n - 3:n]), reads=[ps_b], writes=[Bhist])
            linear(L["w_in"], D, O_XR + c * 128, 128, lambda kc: hT[:, kc, :n], [BhT], n, cons)

    def inproj_v(n, blk0):
        st, st_b = stg.next()
        wb_, wb_b = wbr.next()
        stv = st[:, :1024].rearrange("p (k n) -> p k n", k=8)
        wb3 = wb_[:, :1024].rearrange("p (k n) -> p k n", k=8)
        S.dma("sp", st_b.name, stv, L["w_in"].rearrange("(kc p) n -> p kc n", p=128)[:, :, O_V:O_V + 128], writes=[st_b])
        S.op("pool", lambda e, wb_=wb_, st=st: e.tensor_copy(out=wb_[:, :1024], in_=st[:, :1024]), reads=[st_b], writes=[wb_b])
        for i in range(n // 128):
            ps, ps_b = psum()

            def mmv(e, ps=ps, i=i):
                ins = None
                for kc in range(8):
                    ins = e.matmul(ps[:, :128], lhsT=hT[:, kc, i * 128:(i + 1) * 128], rhs=wb3[:, kc, :],
                                   start=(kc == 0), stop=(kc == 7))
                return ins
            S.op("pe", mmv, reads=[BhT, wb_b], writes=[ps_b])
            S.op("act", lambda e, ps=ps, i=i: e.copy(out=vtm[:, blk0 + i, :], in_=ps[:, :128]), reads=[ps_b],
                 writes=[Bvtm])

    def p23_prelude(l):
        sink32, Bsink32 = tB.next()
        for kh_ in range(2):
            S.dma("sp", "sink32", sink32[kh_ * 32:kh_ * 32 + 1, :], sinks_all[l][0:1, kh_ * 512:(kh_ + 1) * 512],
                  writes=[Bsink32])
        for kh_ in range(2):
            S.op("act", lambda e, kh_=kh_, sink32=sink32: e.activation(
                out=esink[kh_ * 32:kh_ * 32 + 1, :], in_=sink32[kh_ * 32:kh_ * 32 + 1, :], func=AF.Exp),
                reads=[Bsink32], writes=[Besink])
        S.dma("sp", "xt", xt[:, :, :256], memT_in, writes=[Bxt])
        rmsnorm(xt, Bxt, 256, V_GMEM, hT, BhT)

        def cons_km(j, ps, ps_b):
            S.op("act", lambda e: e.copy(out=kmT[:, j, :], in_=ps[:, :256]), reads=[ps_b], writes=[BkmT])
        linear(w_memkv_all[l], D, 0, 512, lambda kc: hT[:, kc, :256], [BhT], 256, cons_km)
        st, st_b = stg.next()
        wbv_, wbv_b = wbr.next()
        stv = st.rearrange("p (k n) -> p k n", k=8)
        wv3 = wbv_.rearrange("p (k n) -> p k n", k=8)
        S.dma("sp", st_b.name, stv, w_memkv_all[l].rearrange("(kc p) n -> p kc n", p=128)[:, :, 512:1024], writes=[st_b])
        S.op("pool", lambda e, wbv_=wbv_, st=st: e.tensor_copy(out=wbv_, in_=st), reads=[st_b], writes=[wbv_b])
        for mb in range(2):
            ps, ps_b = psum()

            def mmv(e, ps=ps, mb=mb):
                ins = None
                for kc in range(8):
                    ins = e.matmul(ps, lhsT=hT[:, kc, mb * 128:(mb + 1) * 128], rhs=wv3[:, kc, :], start=(kc == 0),
                                   stop=(kc == 7))
                return ins
            S.op("pe", mmv, reads=[BhT, wbv_b], writes=[ps_b])
            S.op("act", lambda e, ps=ps, mb=mb: e.copy(out=vm[:, mb, :], in_=ps), reads=[ps_b], writes=[Bvm])


    def final_norm_store(t0):
        vo = L["vo"]
        rstd, Brstd = tB.next()
        for c in range(8):
            S.op("act", lambda e, c=c: e.activation(out=big32[:, c, :], in_=xt[:, c, :], func=AF.Square),
                 reads=[Bxt], writes=[Bbig])
        ps, ps_b = psum()

        def mmf(e, ps=ps):
            ins = None
            for c in range(8):
                ins = e.matmul(ps, lhsT=ones32, rhs=big32[:, c, :], start=(c == 0), stop=(c == 7))
            return ins
        S.op("pe", mmf, reads=[Bbig, Bones32], writes=[ps_b])
        S.op("act", lambda e, ps=ps: e.activation(out=rstd, in_=ps, func=AF.Sqrt, bias=EPS, scale=1.0), reads=[ps_b],
             writes=[Brstd])
        S.op("dve", lambda e: e.reciprocal(out=rstd, in_=rstd), reads=[Brstd], writes=[Brstd])
        for c in range(8):
            S.op("dve", lambda e, c=c: e.scalar_tensor_tensor(
                out=big32[:, c, :], in0=xt[:, c, :], scalar=vecs[:, vo + V_GFIN + c:vo + V_GFIN + c + 1], in1=rstd,
                op0=ALU.mult, op1=ALU.mult), reads=[Bxt, Brstd, Bvecs, Bbig], writes=[Bbig])
        S.dma("sp", "y_out", y_out[:, :, t0:t0 + T], big32, reads=[Bbig], writes=[Byo])

    def p23_phase(l, last):
        vo = L["vo"]
        HN = 128
        S.dma("sp", "xt", xt[:, :, :HN].rearrange("p c n -> p (c n)") if False else xt[:, :, :HN],
              cc1_out[l].ap()[0:128, 0:1024].rearrange("p (c n) -> p c n", c=8), reads=[Bcc1o[l]], writes=[Bxt])
        S.op("dve", lambda e: e.tensor_scalar(out=xt[:, :, :HN], in0=xt[:, :, :HN], scalar1=vecs[:, V_OMF0:V_OMF0 + 1],
                                              scalar2=None, op0=ALU.mult), reads=[Bxt, Bvecs], writes=[Bxt])
        S.dma("sp", "state", state, cc1_out[l].ap()[0:128, 1024:1032], reads=[Bcc1o[l]], writes=[Bstate])
        S.op("dve", lambda e: e.tensor_scalar(out=state, in0=state, scalar1=vecs[:, V_OMF0:V_OMF0 + 1],
                                              scalar2=None, op0=ALU.mult), reads=[Bstate, Bvecs], writes=[Bstate])
        rmsnorm(xt, Bxt, HN, V_GM, hT, BhT)

        def cons_k0(j, ps, ps_b):
            S.op("act", lambda e: e.copy(out=kT[:, 0:128], in_=ps[:, :128]), reads=[ps_b], writes=[BkT])
        linear(L["w_in"], D, O_K, 128, lambda kc: hT[:, kc, :HN], [BhT], HN, cons_k0)
        inproj_v(HN, 0)
        inproj_xr_hist(HN)

        for ti in range(ntiles):
            t0 = ti * T
            n = T
            S.dma("sp", "xt", xt, xres[:, :, t0:t0 + T], reads=[Bxres[ti]], writes=[Bxt])
            rmsnorm(xt, Bxt, T, V_GM, hT, BhT)
            rh = lambda kc: hT[:, kc, :n]

            def cons_q(j, ps, ps_b):
                S.op("act", lambda e: e.copy(out=qT[:, j, :], in_=ps), reads=[ps_b], writes=[BqT])
            linear(L["w_in"], D, O_Q, 512, rh, [BhT], n, cons_q)

            def cons_k(j, ps, ps_b):
                S.op("act", lambda e: e.copy(out=kT[:, 128:128 + T], in_=ps), reads=[ps_b], writes=[BkT])
            linear(L["w_in"], D, O_K, 128, rh, [BhT], n, cons_k)
            inproj_v(T, 1)

            def cons_cq(j, ps, ps_b):
                S.op("act", lambda e: e.copy(out=cqT[:, j, :], in_=ps), reads=[ps_b], writes=[BcqT])
            linear(L["w_in"], D, O_CQ, 512, rh, [BhT], n, cons_cq)

            for bi in range(4):
                for kh in range(2):
                    r0 = kh * 64
                    pt, pt_b = pT.next()
                    for kb in range(2):
                        ps, ps_b = psum()
                        kcol = (bi + kb) * 128
                        S.op("pe", lambda e, ps=ps, kcol=kcol, r0=r0, bi=bi: e.matmul(
                            ps, lhsT=kT[r0:r0 + 64, kcol:kcol + 128], rhs=qT[r0:r0 + 64, :, bi * 128:(bi + 1) * 128],
                            start=True, stop=True), reads=[BkT, BqT], writes=[ps_b])
                        sc, sc_b = tB.next()
                        bsrc, bsrc_b = biasT[:, kh, kb], BbiasT
                        S.op("dve", lambda e, sc=sc, ps=ps, bsrc=bsrc: e.scalar_tensor_tensor(
                            out=sc.rearrange("p (g q) -> p g q", g=4), in0=ps.rearrange("p (g q) -> p g q", g=4), scalar=0.125,
                            in1=bsrc, op0=ALU.mult, op1=ALU.add), reads=[ps_b, bsrc_b], writes=[sc_b])
                        if ti == 0 and bi == 0 and kb == 0:
                            S.op("dve", lambda e, sc=sc: e.tensor_scalar(out=sc, in0=sc, scalar1=vecs[:, V_MASKA:V_MASKA + 1],
                                                                        scalar2=None, op0=ALU.add),
                                 reads=[sc_b, Bvecs], writes=[sc_b])
                        S.op("act", lambda e, sc=sc, pt=pt, kb=kb: e.activation(out=pt[:, kb, :], in_=sc, func=AF.Exp),
                             reads=[sc_b], writes=[pt_b])
                    psn, psn_b = psum()

                    def mmn(e, psn=psn, pt=pt, kh=kh, bi=bi):
                        ins = None
                        for kb in range(2):
                            ins = e.matmul(psn[0:64, :], lhsT=vtm[:, bi + kb, kh * 64:(kh + 1) * 64], rhs=pt[:, kb, :],
                                           start=(kb == 0), stop=(kb == 1))
                        return ins
                    S.op("pe", mmn, reads=[Bvtm, pt_b], writes=[psn_b])
                    psd, psd_b = psum()

                    def mmd(e, psd=psd, pt=pt, kh=kh):
                        for kb in range(2):
                            e.matmul(psd[0:64, :], lhsT=onesb[:, 0:64], rhs=pt[:, kb, :], start=(kb == 0), stop=False)
                        return e.matmul(psd[0:64, :], lhsT=onesb[kh * 32:kh * 32 + 1, 0:64], rhs=esink[kh * 32:kh * 32 + 1, :],
                                        start=False, stop=True)
                    S.op("pe", mmd, reads=[Bonesb, pt_b, Besink], writes=[psd_b])
                    rd, rd_b = tB.next()
                    S.op("dve", lambda e, rd=rd, psd=psd: e.reciprocal(out=rd[0:64, :], in_=psd[0:64, :]), reads=[psd_b],
                         writes=[rd_b])
                    S.op("dve", lambda e, rd=rd, psn=psn, kh=kh, bi=bi: e.tensor_tensor(
                        out=attnT[:, kh * 4:(kh + 1) * 4, bi * 128:(bi + 1) * 128],
                        in0=psn[0:64, :].rearrange("p (g q) -> p g q", g=4), in1=rd[0:64, :].rearrange("p (g q) -> p g q", g=4),
                        op=ALU.mult), reads=[psn_b, rd_b], writes=[BattnT])
            S.op("pool", lambda e: e.tensor_copy(out=kT[:, 0:128], in_=kT[:, T:T + 128]), reads=[BkT], writes=[BkT])
            S.op("pool", lambda e: e.tensor_copy(out=vtm[:, 0, :], in_=vtm[:, 4, :]), reads=[Bvtm], writes=[Bvtm])

            for hd in range(4):
                pt, pt_b = pT.next()
                for mb in range(2):
                    ps, ps_b = psum()
                    S.op("pe", lambda e, ps=ps, hd=hd, mb=mb: e.matmul(ps, lhsT=kmT[:, hd, mb * 128:(mb + 1) * 128],
                                                                    rhs=cqT[:, hd, :], start=True, stop=True),
                         reads=[BkmT, BcqT], writes=[ps_b])
                    S.op("act", lambda e, ps=ps, pt=pt, mb=mb: e.activation(out=pt[:, mb, :], in_=ps, func=AF.Exp,
                                                                         scale=128.0 ** -0.5), reads=[ps_b], writes=[pt_b])
                psn, psn_b = psum()

                def mmn2(e, psn=psn, pt=pt, hd=hd):
                    ins = None
                    for mb in range(2):
                        ins = e.matmul(psn, lhsT=vm[:, mb, hd * 128:(hd + 1) * 128], rhs=pt[:, mb, :], start=(mb == 0),
                                       stop=(mb == 1))
                    return ins
                S.op("pe", mmn2, reads=[Bvm, pt_b], writes=[psn_b])
                psd, psd_b = psum()

                def mmd2(e, psd=psd, pt=pt):
                    ins = None
                    for mb in range(2):
                        ins = e.matmul(psd, lhsT=onesb, rhs=pt[:, mb, :], start=(mb == 0), stop=(mb == 1))
                    return ins
                S.op("pe", mmd2, reads=[Bonesb, pt_b], writes=[psd_b])
                rd, rd_b = tB.next()
                S.op("dve", lambda e, rd=rd, psd=psd: e.reciprocal(out=rd, in_=psd), reads=[psd_b], writes=[rd_b])
                S.op("dve", lambda e, rd=rd, psn=psn, hd=hd: e.tensor_tensor(out=memoT[:, hd, :], in0=psn, in1=rd, op=ALU.mult),
                     reads=[psn_b, rd_b], writes=[BmemoT])

            lru(T, ti == 0, ti)

            branches = [
                (w_attn_all[l], 512, 64, lambda kc: attnT[:, kc, :], BattnT),
                (w_lru_all[l], D, 128, lambda kc: lruT[:, kc, :], BlruT),
                (w_memo_all[l], 512, 128, lambda kc: memoT[:, kc, :], BmemoT),
            ]
            for b, (Wb, Kb, kpb, rb, Brb) in enumerate(branches):
                sg_sb = {}
                for g0 in range(0, 8, 4):
                    def cons_g(j, ps, ps_b, b=b, g0=g0):
                        t, t_b = tB.next()
                        jj = g0 + j
                        S.op("act", lambda e, t=t, ps=ps: e.activation(
                            out=t, in_=ps, func=AF.Sigmoid, bias=vecs[:, vo + V_GB + b * 8 + jj:vo + V_GB + b * 8 + jj + 1]),
                            reads=[ps_b, Bvecs], writes=[t_b])
                        sg_sb[jj] = (t, t_b)
                    linear(L["w_in"], D, O_G + b * 1024 + g0 * 128, 512, rh, [BhT], n, cons_g)

                    def cons_b(j, ps, ps_b, b=b, g0=g0):
                        jj = g0 + j
                        t, t_b = sg_sb[jj]
                        if b == 0:
                            S.op("dve", lambda e, t=t, ps=ps: e.tensor_tensor(out=big32[:, jj, :], in0=t, in1=ps, op=ALU.mult),
                                 reads=[t_b, ps_b], writes=[Bbig])
                        else:
                            S.op("dve", lambda e, t=t, ps=ps: e.tensor_tensor(out=t, in0=t, in1=ps, op=ALU.mult),
                                 reads=[t_b, ps_b], writes=[t_b])
                            if b == 1:
                                S.op("pool", lambda e, t=t: e.tensor_tensor(out=big32[:, jj, :], in0=big32[:, jj, :], in1=t,
                                                                            op=ALU.add), reads=[t_b, Bbig], writes=[Bbig])
                            else:
                                S.op("pool", lambda e, t=t: e.tensor_tensor(out=merged[:, jj, :], in0=big32[:, jj, :], in1=t,
                                                                            op=ALU.add), reads=[t_b, Bbig], writes=[Bmerged])
                    linear(Wb, Kb, g0 * 128, 512, rb, [Brb], n, cons_b, kp=kpb)

            def cons_o(j, ps, ps_b):
                S.op("dve", lambda e, ps=ps: e.tensor_tensor(out=xt[:, j, :], in0=xt[:, j, :], in1=ps, op=ALU.add),
                     reads=[ps_b, Bxt], writes=[Bxt])
            linear(w_out_all[l], D, 0, D, lambda kc: merged[:, kc, :], [Bmerged], n, cons_o)

            ffn(T, V_G2)
            if not last:
                S.dma("sp", "xres", xres[:, :, t0:t0 + T], xt, reads=[Bxt], writes=[Bxres[ti]])
            else:
                final_norm_store(t0)

    Bcc1i = [Buf("cc1i%d" % l) for l in range(depth)]
    Bcc1o = [Buf("cc1o%d" % l) for l in range(depth)]
    Bcc2i = [Buf("cc2i%d" % l) for l in range(depth)]
    Bcc2o = [Buf("cc2o%d" % l) for l in range(depth)]
    Byo = Buf("y_out")

    def p1_phase(l):
        S.op("dve", lambda e: e.memset(state, 0.0), writes=[Bstate])
        if l == 0:
            S.dma("sp", "xt", xt[:, :, :4], xh4_in, writes=[Bxt])
        else:
            S.dma("sp", "xt", xt[:, :, :4], cc2_out[l - 1].ap()[0:128, :].rearrange("p (c n) -> p c n", c=8),
                  reads=[Bcc2o[l - 1]], writes=[Bxt])
            S.op("dve", lambda e: e.tensor_scalar(out=xt[:, :, :4], in0=xt[:, :, :4], scalar1=vecs[:, V_OMF0:V_OMF0 + 1],
                                                  scalar2=None, op0=ALU.mult), reads=[Bxt, Bvecs], writes=[Bxt])
        ffn(4, V_G1)
        rmsnorm(xt, Bxt, 4, V_GM, hT, BhT)
        inproj_xr_hist(4)
        for ti in range(ntiles):
            t0 = ti * T
            if l == 0:
                S.dma("sp", "xt", xt, x_in[:, :, t0:t0 + T], writes=[Bxt])
            else:
                S.dma("sp", "xt", xt, xres[:, :, t0:t0 + T], reads=[Bxres[ti]], writes=[Bxt])
            ffn(T, V_G1)
            S.dma("sp", "xres", xres[:, :, t0:t0 + T], xt, reads=[Bxt], writes=[Bxres[ti]])
            rmsnorm(xt, Bxt, T, V_GM, hT, BhT)
            lru(T, ti == 0, ti)

    def exchange1(l):
        S.dma("sp", "cc1i", cc1_in[l].ap()[:, 0:1024].rearrange("p (c n) -> p c n", c=8), xres[:, :, NT - 128:NT],
              reads=[Bxres[ntiles - 1]], writes=[Bcc1i[l]])
        S.dma("sp", "cc1i", cc1_in[l].ap()[:, 1024:1032], state, reads=[Bstate], writes=[Bcc1i[l]])
        S.coll("cc", lambda e: e.collective_compute("AllGather", ALU.bypass, replica_groups=RG,
                                                    ins=[cc1_in[l].ap().opt()], outs=[cc1_out[l].ap().opt()]),
               reads=[Bcc1i[l]], writes=[Bcc1o[l]])

    def exchange2(l):
        S.dma("sp", "cc2i", cc2_in[l].ap().rearrange("p (c n) -> p c n", c=8), xres[:, :, NT - 4:NT],
              reads=[Bxres[ntiles - 1]], writes=[Bcc2i[l]])
        S.coll("cc", lambda e: e.collective_compute("AllGather", ALU.bypass, replica_groups=RG,
                                                    ins=[cc2_in[l].ap().opt()], outs=[cc2_out[l].ap().opt()]),
               reads=[Bcc2i[l]], writes=[Bcc2o[l]])

    for l in range(depth):
        S.suffix = "_L%d" % l
        L["vo"] = l * NV
        L["w_in"] = w_in_all[l]
        layer_prelude(l)
        L["p23"] = False
        L["wg"], L["wu"], L["wd"] = f1[0][l], f1[1][l], f1[2][l]
        p1_phase(l)
        exchange1(l)
        L["p23"] = True
        L["wg"], L["wu"], L["wd"] = f2[0][l], f2[1][l], f2[2][l]
        p23_prelude(l)
        p23_phase(l, l == depth - 1)
        if l < depth - 1:
            exchange2(l)
    S.finish([Byo])
    S.emit()
    return nc


def _fm(v):
    return np.ascontiguousarray(np.asarray(v, np.float32).reshape(-1, 128).T)


def _to_fm(xtok):
    t = xtok.shape[0]
    return np.ascontiguousarray(xtok.T.reshape(8, 128, t).transpose(1, 0, 2))


def _from_fm(xfm):
    t = xfm.shape[2]
    return np.ascontiguousarray(xfm.transpose(1, 0, 2).reshape(1024, t).T)


def _alibi_tables():
    slopes = np.array([2.0 ** (-8.0 * (i + 1) / 8) for i in range(8)], dtype=np.float32)
    j = np.arange(128)[:, None]
    i = np.arange(128)[None, :]
    tab = np.zeros((128, 2, 2, 4, 128), np.float32)
    for kh in range(2):
        for kb in range(2):
            kpos = kb * 128 + j
            qpos = 128 + i
            dist = np.abs(qpos - kpos).astype(np.float32)
            kch = kpos // 64
            qch = qpos // 64
            valid = (kch >= qch - 2) & (kch <= qch)
            for g in range(4):
                tab[:, kh, kb, g, :] = np.where(valid, -slopes[kh * 4 + g] * dist, np.float32(NEG))
    return tab


_PROGS = {}


def _prog(depth):
    if depth not in _PROGS:
        _PROGS[depth] = build(depth)
    return _PROGS[depth]


def _prep_inputs(x, mem, ffn1_norm, ffn1_w_gate, ffn1_w_up, ffn1_w_down, mix_norm, w_in, gate_bias,
                 attn_sinks, w_attn_out, conv_w, conv_b, lru_wa, lru_ba, lru_wx, lru_bx, lru_lambda,
                 w_lru_out, mem_norm, w_mem_kv, w_mem_out, w_out, ffn2_norm, ffn2_w_gate, ffn2_w_up,
                 ffn2_w_down, final_norm, depth=None):
    f32 = np.float32
    x = np.asarray(x, f32)
    mem = np.asarray(mem, f32)
    if depth is None:
        depth = ffn1_norm.shape[0]
    ncores = 8
    ca = lambda a: np.ascontiguousarray(np.asarray(a, f32)[:depth])
    tab = _alibi_tables()
    tabF_A = tab[:, :, 0].copy()
    tabF_A[...] = NEG
    tabF_B = np.ascontiguousarray(tab[:, :, 0])
    qperm = np.concatenate([np.concatenate([np.arange(j * 64, j * 64 + 64), np.arange((4 + j) * 64, (4 + j) * 64 + 64)])
                            for j in range(4)])
    win = np.array(np.asarray(w_in, f32)[:depth], copy=True)
    win[:, :, :512] = win[:, :, qperm]
    win = np.ascontiguousarray(win)
    wbd = np.zeros((depth, 128, 16, 128), f32)
    for l in range(depth):
        for gi_, wsrc in enumerate((lru_wa[l], lru_wx[l])):
            for c in range(8):
                wbd[l, 0:64, gi_ * 8 + c, 0:64] = wsrc[2 * c]
                wbd[l, 64:128, gi_ * 8 + c, 64:128] = wsrc[2 * c + 1]
    sinks = np.ascontiguousarray(np.repeat(np.asarray(attn_sinks, f32)[:depth], 128, axis=1)[:, None, :])
    shared = {"w_in": win, "wbd": wbd,
              "f1_wg": ca(ffn1_w_gate), "f1_wu": ca(ffn1_w_up), "f1_wd": ca(ffn1_w_down),
              "f2_wg": ca(ffn2_w_gate), "f2_wu": ca(ffn2_w_up), "f2_wd": ca(ffn2_w_down),
              "biasT": tab, "sinks": sinks, "w_attn": ca(w_attn_out), "w_lru": ca(w_lru_out),
              "w_memkv": ca(w_mem_kv), "w_memo": ca(w_mem_out), "w_out": ca(w_out)}
    in_maps = []
    for c in range(ncores):
        b, half = c // 2, c % 2
        v = np.zeros((128, depth * NV), f32)
        for l in range(depth):
            o = l * NV
            v[:, o + V_G1:o + V_G1 + 8] = _fm(ffn1_norm[l])
            v[:, o + V_GM:o + V_GM + 8] = _fm(mix_norm[l])
            v[:, o + V_G2:o + V_G2 + 8] = _fm(ffn2_norm[l])
            for j in range(4):
                v[:, o + V_CW + j * 8:o + V_CW + j * 8 + 8] = _fm(conv_w[l, j])
            v[:, o + V_CB:o + V_CB + 8] = _fm(conv_b[l])
            v[:, o + V_BA:o + V_BA + 8] = _fm(lru_ba[l])
            v[:, o + V_BX:o + V_BX + 8] = _fm(lru_bx[l])
            v[:, o + V_LAM:o + V_LAM + 8] = _fm(lru_lambda[l])
            v[:, o + V_GB:o + V_GB + 24] = _fm(gate_bias[l])
            v[:, o + V_GMEM:o + V_GMEM + 8] = _fm(mem_norm[l])
            v[:, o + V_GFIN:o + V_GFIN + 8] = _fm(final_norm)
            v[:, o + V_F0] = 1.0 if half == 0 else 0.0
            v[:, o + V_OMF0] = 0.0 if half == 0 else 1.0
            v[:, o + V_MASKA] = NEG if half == 0 else 0.0
        xs = _to_fm(x[b, half * NT:(half + 1) * NT, :])
        if half == 0:
            xh4 = np.zeros((128, 8, 4), f32)
        else:
            xh4 = _to_fm(x[b, NT - 4:NT, :])
        m = dict(shared)
        m.update({"x_in": xs, "xh4_in": xh4, "vecs": v,
                  "memT": _to_fm(mem[b])})
        in_maps.append(m)
    return in_maps, depth


def kernel(x, mem, ffn1_norm, ffn1_w_gate, ffn1_w_up, ffn1_w_down, mix_norm, w_in, gate_bias,
           attn_sinks, w_attn_out, conv_w, conv_b, lru_wa, lru_ba, lru_wx, lru_bx, lru_lambda,
           w_lru_out, mem_norm, w_mem_kv, w_mem_out, w_out, ffn2_norm, ffn2_w_gate, ffn2_w_up,
           ffn2_w_down, final_norm):
    in_maps, depth = _prep_inputs(x, mem, ffn1_norm, ffn1_w_gate, ffn1_w_up, ffn1_w_down, mix_norm, w_in,
                                  gate_bias, attn_sinks, w_attn_out, conv_w, conv_b, lru_wa, lru_ba, lru_wx,
                                  lru_bx, lru_lambda, w_lru_out, mem_norm, w_mem_kv, w_mem_out, w_out,
                                  ffn2_norm, ffn2_w_gate, ffn2_w_up, ffn2_w_down, final_norm)
    ncores = 8
    res = run_bass_kernel_spmd(_prog(depth), in_maps, core_ids=list(range(ncores))).results
    B = np.asarray(x).shape[0]
    out = np.empty((B, 2 * NT, D), np.float32)
    for c in range(ncores):
        out[c // 2, (c % 2) * NT:(c % 2 + 1) * NT, :] = _from_fm(np.asarray(res[c]["y_out"], np.float32))
    return out
```

```python
import numpy as np
import concourse.bass as bass
import concourse.mybir as mybir
from concourse.bass_utils import run_bass_kernel_spmd

F32 = mybir.dt.float32
BF16 = mybir.dt.bfloat16
AF = mybir.ActivationFunctionType
ALU = mybir.AluOpType

D = 1024
NT = 4096
T = 512
DFF = 2816
INW = 6400
O_Q, O_K, O_V, O_XR, O_YR, O_CQ, O_G = 0, 512, 640, 768, 1792, 2816, 3328
NV = 136
V_G1, V_GM, V_G2, V_CW, V_CB, V_BA, V_BX, V_LAM, V_GB, V_GMEM, V_GFIN, V_F0, V_OMF0, V_MASKA = \
    0, 8, 16, 24, 56, 64, 72, 80, 88, 112, 120, 128, 129, 130
EPS = 1e-6
NEG = -30000.0

ENGS = ("pe", "act", "dve", "pool", "sp")


class Buf:
    __slots__ = ("name", "w", "r")

    def __init__(self, name):
        self.name = name
        self.w = None
        self.r = {}


class Sched:
    def __init__(self, nc):
        self.nc = nc
        self.ops = {e: [] for e in ENGS}
        self.cnt = {e: 0 for e in ENGS}
        self.sem = {e: nc.alloc_semaphore("s_" + e) for e in ENGS}
        self.dsem = {}
        self.dcnt = {}
        self.waited = {}
        self.suffix = ""

    def _need(self, eng, dep, waits):
        if dep is None:
            return
        kind, key, val = dep
        if kind == "eng" and key == "pe" and eng == "pe":
            return
        if self.waited.get((eng, kind, key), 0) >= val:
            return
        if val > waits.get((kind, key), 0):
            waits[(kind, key)] = val

    def _collect(self, eng, reads, writes):
        waits = {}
        for b in reads:
            self._need(eng, b.w, waits)
        for b in writes:
            self._need(eng, b.w, waits)
            for d in b.r.values():
                self._need(eng, d, waits)
        for (kind, key), val in waits.items():
            self.waited[(eng, kind, key)] = val
        return waits

    def _mark(self, me, reads, writes):
        for b in reads:
            b.r[(me[0], me[1])] = me
        for b in writes:
            b.w = me
            b.r = {}

    def _emit_waits(self, e, waits):
        for (kind, key), val in waits.items():
            e.wait_ge(self.sem[key] if kind == "eng" else self.dsem[key], val)

    def op(self, eng, fn, reads=(), writes=()):
        waits = self._collect(eng, reads, writes)
        self.cnt[eng] += 1
        self._mark(("eng", eng, self.cnt[eng]), reads, writes)
        sem = self.sem[eng]

        def run(e, fn=fn, waits=waits, sem=sem):
            self._emit_waits(e, waits)
            fn(e).then_inc(sem, 1)
        self.ops[eng].append(run)

    def coll(self, key, fn, reads=(), writes=()):
        key = key + self.suffix
        if key not in self.dsem:
            self.dsem[key] = self.nc.alloc_semaphore("c_" + key)
            self.dcnt[key] = 0
        waits = self._collect("pool", reads, writes)
        self.dcnt[key] += 1
        self._mark(("dma", key, self.dcnt[key]), reads, writes)
        sem = self.dsem[key]

        def run(e, waits=waits, sem=sem, fn=fn):
            self._emit_waits(e, waits)
            fn(e).then_inc(sem, 1)
        self.ops["pool"].append(run)

    def dma(self, queue, key, out, in_, reads=(), writes=(), nosuffix=False):
        if not nosuffix:
            key = key + self.suffix
        if key not in self.dsem:
            self.dsem[key] = self.nc.alloc_semaphore("d_" + key)
            self.dcnt[key] = 0
        waits = self._collect(queue, reads, writes)
        self.dcnt[key] += 16
        self._mark(("dma", key, self.dcnt[key]), reads, writes)
        sem = self.dsem[key]

        def run(e, waits=waits, sem=sem, out=out, in_=in_):
            self._emit_waits(e, waits)
            e.dma_start(out=out, in_=in_).then_inc(sem, 16)
        self.ops[queue].append(run)

    def finish(self, final_bufs):
        waits = self._collect("sp", final_bufs, ())
        self.ops["sp"].append(lambda e, waits=waits: self._emit_waits(e, waits))

    def emit(self):
        nc = self.nc
        with nc.Block() as block:
            @block.tensor
            def _(e):
                for f in self.ops["pe"]:
                    f(e)

            @block.scalar
            def _(e):
                for f in self.ops["act"]:
                    f(e)

            @block.vector
            def _(e):
                for f in self.ops["dve"]:
                    f(e)

            @block.gpsimd
            def _(e):
                for f in self.ops["pool"]:
                    f(e)

            @block.sync
            def _(e):
                for f in self.ops["sp"]:
                    f(e)


class Rot:
    def __init__(self, nc, name, shape, dtype, n):
        self.items = []
        for i in range(n):
            nm = "%s%d" % (name, i)
            self.items.append((nc.alloc_sbuf_tensor(nm, shape, dtype).ap(), Buf(nm)))
        self.i = 0

    def next(self):
        it = self.items[self.i % len(self.items)]
        self.i += 1
        return it


def build(depth=4, ntiles=NT // T):
    nc = bass.Bass("TRN2", target_bir_lowering=False)
    S = Sched(nc)
    L = {"vo": 0, "p23": False}

    def din(name, shape):
        return nc.dram_tensor(name, shape, F32, kind="ExternalInput").ap()

    def dout(name, shape):
        return nc.dram_tensor(name, shape, F32, kind="ExternalOutput").ap()

    x_in = din("x_in", [128, 8, NT])
    xh4_in = din("xh4_in", [128, 8, 4])
    vecs_in = din("vecs", [128, depth * NV])
    w_in_all = din("w_in", [depth, D, INW])
    wbd_all = din("wbd", [depth, 128, 16, 128])
    f1 = [din("f1_wg", [depth, D, DFF]), din("f1_wu", [depth, D, DFF]), din("f1_wd", [depth, DFF, D])]
    f2 = [din("f2_wg", [depth, D, DFF]), din("f2_wu", [depth, D, DFF]), din("f2_wd", [depth, DFF, D])]
    bias_in = din("biasT", [128, 2, 2, 4, 128])
    sinks_all = din("sinks", [depth, 1, 1024])
    memT_in = din("memT", [128, 8, 256])
    w_attn_all = din("w_attn", [depth, 512, D])
    w_lru_all = din("w_lru", [depth, D, D])
    w_memkv_all = din("w_memkv", [depth, D, D])
    w_memo_all = din("w_memo", [depth, 512, D])
    w_out_all = din("w_out", [depth, D, D])
    y_out = dout("y_out", [128, 8, NT])
    xres = nc.dram_tensor("xres", [128, 8, NT], F32).ap()
    Bxres = [Buf("xres%d" % i) for i in range(ntiles)]
    CW1 = 8 * 128 + 8
    cc1_in = [nc.dram_tensor("cc1_in%d" % l, [128, CW1], F32) for l in range(depth)]
    cc1_out = [nc.dram_tensor("cc1_out%d" % l, [256, CW1], F32) for l in range(depth)]
    cc2_in = [nc.dram_tensor("cc2_in%d" % l, [128, 32], F32) for l in range(depth)]
    cc2_out = [nc.dram_tensor("cc2_out%d" % l, [256, 32], F32) for l in range(depth)]
    RG = [[0, 1], [2, 3], [4, 5], [6, 7]]
    rg_d = nc.dram_tensor("rg_d", [128, 2, 8, NT], F32).ap()
    hsc = nc.dram_tensor("hsc", [128, 8, NT], BF16).ap()
    Bhsc = [Buf("hsc%d" % i) for i in range(ntiles)]
    Brg = [[[Buf("rg%d_%d_%d" % (i, c, k)) for k in range(2)] for c in range(8)] for i in range(ntiles)]
    p23 = True

    def sb(name, shape, dt=F32):
        return nc.alloc_sbuf_tensor(name, shape, dt).ap()

    vecs = sb("vecs_s", [128, depth * NV]); Bvecs = Buf("vecs")
    cf = sb("cf", [128, 32]); Bcf = Buf("cf")
    tmpv = sb("tmpv", [128, 4, 8]); Btmpv = Buf("tmpv")
    ones32 = sb("ones32", [128, 128]); Bones32 = Buf("ones32")
    onesb = sb("onesb", [128, 128], BF16); Bonesb = Buf("onesb")
    wbd = sb("wbd_s", [128, 16, 128], BF16); Bwbd = Buf("wbd")
    xt = sb("xt", [128, 8, T]); Bxt = Buf("xt")
    big32 = sb("big32", [128, 8, T]); Bbig = Buf("big32")
    hT = sb("hT", [128, 8, T], BF16); BhT = Buf("hT")
    hist = sb("hist", [128, 8, 4]); Bhist = Buf("hist")
    state = sb("state", [128, 8]); Bstate = Buf("state")
    FH = 11
    actT = sb("actT", [128, FH, T], BF16); Bact = Buf("actT")
    stg = Rot(nc, "stg", [128, 4096], F32, 2)
    wbr = Rot(nc, "wbf", [128, 4096], BF16, 2)
    tA = Rot(nc, "tA", [128, T + 4], F32, 4)
    tB = Rot(nc, "tB", [128, T], F32, 20)
    tC = Rot(nc, "tC", [128, T], BF16, 4)
    psb = [(nc.alloc_psum_tensor("ps%d" % i, [128, 512], F32).ap(), Buf("ps%d" % i)) for i in range(8)]
    psi = [0]

    def psum():
        it = psb[psi[0] % 8]
        psi[0] += 1
        return it

    casti = [0]

    def cast_eng():
        casti[0] += 1
        return "pool" if casti[0] % 2 else "act"

    def copy_op(eng, out, in_):
        if eng == "act":
            return lambda e: e.copy(out=out, in_=in_)
        return lambda e: e.tensor_copy(out=out, in_=in_)

    if p23:
        biasT = sb("biasT_s", [128, 2, 2, 4, 128]); BbiasT = Buf("biasT")
        esink = sb("esink", [64, 512], BF16); Besink = Buf("esink")
        qT = sb("qT", [128, 4, T], BF16); BqT = Buf("qT")
        kT = sb("kT", [128, 128 + T], BF16); BkT = Buf("kT")
        vtm = sb("vtm", [128, 5, 128], BF16); Bvtm = Buf("vtm")
        cqT = sb("cqT", [128, 4, T], BF16); BcqT = Buf("cqT")
        attnT = sb("attnT", [64, 8, T], BF16); BattnT = Buf("attnT")
        lruT = sb("lruT", [128, 8, T], BF16); BlruT = Buf("lruT")
        memoT = sb("memoT", [128, 4, T], BF16); BmemoT = Buf("memoT")
        merged, Bmerged = lruT, BlruT
        kmT = sb("kmT", [128, 4, 256], BF16); BkmT = Buf("kmT")
        vm = sb("vm", [128, 2, 512], BF16); Bvm = Buf("vm")
        pT = Rot(nc, "pT", [128, 2, 512], BF16, 2)

    WSC_COLS = depth * 222000
    wsc = nc.dram_tensor("wsc", [128, WSC_COLS], BF16).ap()
    wcache = {}
    wsc_pos = [0]

    def linear(W, K, c0, ncols, rhs, rhs_bufs, n, consume, kp=128):
        kcs = K // kp
        nch = ncols // 128
        G = max(1, min(nch, 32 // kcs))
        Wv = W.rearrange("(kc p) n -> p kc n", p=kp)
        for g0 in range(0, nch, G):
            gn = min(G, nch - g0)
            wb, wb_b = wbr.next()
            ncol = kcs * gn * 128
            wbv = wb[:kp, :ncol].rearrange("p (k n) -> p k n", k=kcs)
            key = (W.tensor.name, W.offset, K, kp, c0, g0, gn)
            if key in wcache:
                pos, cb = wcache[key]
                S.dma("sp", wb_b.name, wb[:kp, :ncol], wsc[:kp, pos:pos + ncol], reads=[cb], writes=[wb_b])
            else:
                st, st_b = stg.next()
                stv = st[:kp, :ncol].rearrange("p (k n) -> p k n", k=kcs)
                S.dma("sp", st_b.name, stv, Wv[:, :, c0 + g0 * 128:c0 + (g0 + gn) * 128], writes=[st_b])
                ce = cast_eng()
                S.op(ce, copy_op(ce, wb[:kp, :ncol], st[:kp, :ncol]), reads=[st_b], writes=[wb_b])
                pos = wsc_pos[0]
                wsc_pos[0] += ncol
                assert wsc_pos[0] <= WSC_COLS
                cb = Buf("wsc%d" % pos)
                S.dma(ce, "wscw_" + ce, wsc[:kp, pos:pos + ncol], wb[:kp, :ncol], reads=[wb_b], writes=[cb])
                wcache[key] = (pos, cb)
            for j in range(gn):
                ps, ps_b = psum()

                def mm(e, ps=ps, wbv=wbv, j=j):
                    ins = None
                    for kc in range(kcs):
                        ins = e.matmul(ps[:, :n], lhsT=wbv[:, kc, j * 128:(j + 1) * 128], rhs=rhs(kc),
                                       start=(kc == 0), stop=(kc == kcs - 1))
                    return ins
                S.op("pe", mm, reads=[wb_b] + list(rhs_bufs), writes=[ps_b])
                consume(g0 + j, ps, ps_b)

    def rmsnorm(xsrc, Bx, n, gofs, out, Bout):
        vo = L["vo"]
        rstd, Brstd = tB.next()
        for c in range(8):
            S.op("act", lambda e, c=c: e.activation(out=big32[:, c, :n], in_=xsrc[:, c, :n], func=AF.Square),
                 reads=[Bx], writes=[Bbig])
        ps, ps_b = psum()

        def mm(e):
            ins = None
            for c in range(8):
                ins = e.matmul(ps[:, :n], lhsT=ones32, rhs=big32[:, c, :n], start=(c == 0), stop=(c == 7))
            return ins
        S.op("pe", mm, reads=[Bbig, Bones32], writes=[ps_b])
        S.op("act", lambda e: e.activation(out=rstd[:, :n], in_=ps[:, :n], func=AF.Sqrt, bias=EPS, scale=1.0),
             reads=[ps_b], writes=[Brstd])
        S.op("dve", lambda e: e.reciprocal(out=rstd[:, :n], in_=rstd[:, :n]), reads=[Brstd], writes=[Brstd])
        for c in range(8):
            S.op("dve", lambda e, c=c: e.scalar_tensor_tensor(
                out=out[:, c, :n], in0=xsrc[:, c, :n], scalar=vecs[:, vo + gofs + c:vo + gofs + c + 1], in1=rstd[:, :n],
                op0=ALU.mult, op1=ALU.mult), reads=[Bx, Brstd, Bvecs], writes=[Bout])

    def ffn(n, gofs):
        vo = L["vo"]
        rmsnorm(xt, Bxt, n, gofs, hT, BhT)
        for half in range(2):
            j0 = half * FH
            gate_sb = {}

            def cons_gate(j, ps, ps_b):
                t, t_b = tB.next()
                S.op("act", lambda e, t=t, ps=ps: e.activation(out=t[:, :n], in_=ps[:, :n], func=AF.Silu),
                     reads=[ps_b], writes=[t_b])
                gate_sb[j] = (t, t_b)

            def cons_up(j, ps, ps_b):
                t, t_b = gate_sb[j]
                S.op("dve", lambda e, t=t, ps=ps, j=j: e.tensor_tensor(out=actT[:, j, :n], in0=t[:, :n], in1=ps[:, :n],
                                                                       op=ALU.mult),
                     reads=[ps_b, t_b], writes=[Bact])
            for s0 in range(0, FH, 4):
                sn = min(4, FH - s0)
                base = j0 + s0
                linear(L["wg"], D, base * 128, sn * 128, lambda kc: hT[:, kc, :n], [BhT], n,
                       lambda j, ps, ps_b, s0=s0: cons_gate(s0 + j, ps, ps_b))
                linear(L["wu"], D, base * 128, sn * 128, lambda kc: hT[:, kc, :n], [BhT], n,
                       lambda j, ps, ps_b, s0=s0: cons_up(s0 + j, ps, ps_b))

            def cons_down(j, ps, ps_b):
                S.op("dve", lambda e, ps=ps, j=j: e.scalar_tensor_tensor(
                    out=xt[:, j, :n], in0=ps[:, :n], scalar=0.5, in1=xt[:, j, :n], op0=ALU.mult, op1=ALU.add),
                    reads=[ps_b, Bxt], writes=[Bxt])
            linear(L["wd"][j0 * 128:(j0 + FH) * 128, :], FH * 128, 0, D, lambda kc: actT[:, kc, :n], [Bact], n, cons_down)

    NB = 4

    def lru(n, first_tile, ti=0):
        vo = L["vo"]
        want_out = L["p23"]
        hsrc = lambda kc: hT[:, kc, :n]
        t0_ = ti * T
        for cb in range(0, 8, NB):
            cs = list(range(cb, min(8, cb + NB)))
            X = {}
            if want_out:
                for c in cs:
                    xc, xc_b = tB.next()
                    r, r_b = tB.next()
                    T1, T1_b = tB.next()
                    T2, T2_b = tB.next()
                    X[c] = {"xc": (xc, xc_b), "r": (r, r_b), "T1": (T1, T1_b), "T2": (T2, T2_b)}
                    S.dma("pool", "ld_r%d" % c, r[:, :n], rg_d[:, 0, c, t0_:t0_ + n], reads=[Brg[ti][c][0]],
                          writes=[r_b], nosuffix=True)
                    S.dma("pool", "ld_g%d" % c, xc[:, :n], rg_d[:, 1, c, t0_:t0_ + n], reads=[Brg[ti][c][1]],
                          writes=[xc_b], nosuffix=True)
            if not want_out:
                    for c in cs:
                        xrb, xrb_b = tA.next()
                        xc, xc_b = tB.next()
                        X[c] = {"xrb": (xrb, xrb_b), "xc": (xc, xc_b)}
                        S.op("dve", lambda e, xrb=xrb, c=c: e.tensor_copy(out=xrb[:, 0:3], in_=hist[:, c, 0:3]),
                             reads=[Bhist], writes=[xrb_b])

                        def cons_xr(j, ps, ps_b, xrb=xrb, xrb_b=xrb_b):
                            S.op("dve", lambda e: e.tensor_copy(out=xrb[:, 3:3 + n], in_=ps[:, :n]), reads=[ps_b], writes=[xrb_b])
                        linear(L["w_in"], D, O_XR + c * 128, 128, hsrc, [BhT], n, cons_xr)
                    for c in cs:
                        xrb, xrb_b = X[c]["xrb"]
                        xc, xc_b = X[c]["xc"]
                        S.op("dve", lambda e, xc=xc, xrb=xrb, c=c: e.tensor_scalar(
                            out=xc[:, :n], in0=xrb[:, 0:n], scalar1=vecs[:, vo + V_CW + c:vo + V_CW + c + 1],
                            scalar2=vecs[:, vo + V_CB + c:vo + V_CB + c + 1], op0=ALU.mult, op1=ALU.add),
                            reads=[xrb_b, Bvecs], writes=[xc_b])
                        for j in range(1, 4):
                            S.op("dve", lambda e, xc=xc, xrb=xrb, c=c, j=j: e.scalar_tensor_tensor(
                                out=xc[:, :n], in0=xrb[:, j:j + n],
                                scalar=vecs[:, vo + V_CW + j * 8 + c:vo + V_CW + j * 8 + c + 1],
                                in1=xc[:, :n], op0=ALU.mult, op1=ALU.add), reads=[xrb_b, Bvecs, xc_b], writes=[xc_b])
                        S.op("dve", lambda e, xrb=xrb, c=c: e.tensor_copy(out=hist[:, c, 0:3], in_=xrb[:, n:n + 3]),
                             reads=[xrb_b], writes=[Bhist])
                        xcb, xcb_b = tC.next()
                        X[c]["xcb"] = (xcb, xcb_b)
                        S.op("pool", lambda e, xcb=xcb, xc=xc: e.tensor_copy(out=xcb[:, :n], in_=xc[:, :n]), reads=[xc_b],
                             writes=[xcb_b])
                    for c in cs:
                        xcb, xcb_b = X[c]["xcb"]
                        psa, psa_b = psum()
                        S.op("pe", lambda e, psa=psa, xcb=xcb, c=c: e.matmul(psa[:, :n], lhsT=wbd[:, c, :], rhs=xcb[:, :n],
                                                                          start=True, stop=True),
                             reads=[Bwbd, xcb_b], writes=[psa_b])
                        psx, psx_b = psum()
                        S.op("pe", lambda e, psx=psx, xcb=xcb, c=c: e.matmul(psx[:, :n], lhsT=wbd[:, 8 + c, :], rhs=xcb[:, :n],
                                                                          start=True, stop=True),
                             reads=[Bwbd, xcb_b], writes=[psx_b])
                        X[c]["ps"] = (psa, psa_b, psx, psx_b)
                    for c in cs:
                        psa, psa_b, psx, psx_b = X[c]["ps"]
                        r, r_b = tB.next()
                        gi, gi_b = tB.next()
                        X[c]["r"] = (r, r_b)
                        X[c]["gi"] = (gi, gi_b)
                        S.op("act", lambda e, r=r, psa=psa, c=c: e.activation(out=r[:, :n], in_=psa[:, :n], func=AF.Sigmoid,
                                                                           bias=vecs[:, vo + V_BA + c:vo + V_BA + c + 1]),
                             reads=[psa_b, Bvecs], writes=[r_b])
                        S.op("act", lambda e, gi=gi, psx=psx, c=c: e.activation(out=gi[:, :n], in_=psx[:, :n], func=AF.Sigmoid,
                                                                             bias=vecs[:, vo + V_BX + c:vo + V_BX + c + 1]),
                             reads=[psx_b, Bvecs], writes=[gi_b])
                    for c in cs:
                        gi, gi_b = X[c]["gi"]
                        xc, xc_b = X[c]["xc"]
                        S.op("pool", lambda e, gi=gi, xc=xc: e.tensor_tensor(out=gi[:, :n], in0=gi[:, :n], in1=xc[:, :n],
                                                                            op=ALU.mult), reads=[gi_b, xc_b], writes=[gi_b])
            if not want_out:
                for c in cs:
                    r, r_b = X[c]["r"]
                    T1, T1_b = tB.next()
                    T2, T2_b = tB.next()
                    X[c]["T1"] = (T1, T1_b)
                    X[c]["T2"] = (T2, T2_b)
                    S.op("act", lambda e, T1=T1, r=r, c=c: e.activation(out=T1[:, :n], in_=r[:, :n], func=AF.Tanh,
                                                                     scale=cf[:, c:c + 1]), reads=[r_b, Bcf], writes=[T1_b])
                    S.op("act", lambda e, T2=T2, r=r, c=c: e.activation(out=T2[:, :n], in_=r[:, :n], func=AF.Tanh,
                                                                     scale=cf[:, 24 + c:25 + c]), reads=[r_b, Bcf], writes=[T2_b])
                for c in cs:
                    r, r_b = X[c]["r"]
                    S.op("act", lambda e, r=r, c=c: e.activation(out=r[:, :n], in_=r[:, :n], func=AF.Exp,
                                                               scale=cf[:, 8 + c:9 + c]), reads=[r_b, Bcf], writes=[r_b])
                for c in cs:
                    r, r_b = X[c]["r"]
                    xc, xc_b = X[c]["xc"]
                    S.op("pool", lambda e, r=r, xc=xc: e.tensor_tensor(out=xc[:, :n], in0=r[:, :n], in1=r[:, :n],
                                                                      op=ALU.mult), reads=[r_b, xc_b], writes=[xc_b])
                for c in cs:
                    r, r_b = X[c]["r"]
                    T1, T1_b = X[c]["T1"]
                    S.op("dve", lambda e, r=r, T1=T1: e.scalar_tensor_tensor(out=r[:, :n], in0=r[:, :n], scalar=1.0,
                                                                          in1=T1[:, :n], op0=ALU.add, op1=ALU.mult),
                         reads=[r_b, T1_b], writes=[r_b])
                    S.op("pool", lambda e, r=r: e.tensor_scalar(out=r[:, :n], in0=r[:, :n], scalar1=1.0, scalar2=None,
                                                              op0=ALU.add), reads=[r_b], writes=[r_b])
                for c in cs:
                    xc, xc_b = X[c]["xc"]
                    T2, T2_b = X[c]["T2"]
                    S.op("dve", lambda e, xc=xc, T2=T2: e.scalar_tensor_tensor(out=xc[:, :n], in0=xc[:, :n], scalar=1.0,
                                                                            in1=T2[:, :n], op0=ALU.add, op1=ALU.mult),
                         reads=[xc_b, T2_b], writes=[xc_b])
                for c in cs:
                    xc, xc_b = X[c]["xc"]
                    gi, gi_b = X[c]["gi"]
                    S.op("act", lambda e, xc=xc: e.activation(out=xc[:, :n], in_=xc[:, :n], func=AF.Sqrt), reads=[xc_b],
                         writes=[xc_b])
                    if first_tile:
                        S.op("dve", lambda e, xc=xc: e.scalar_tensor_tensor(
                            out=xc[:, 0:1], in0=xc[:, 0:1], scalar=vecs[:, vo + V_OMF0:vo + V_OMF0 + 1],
                            in1=vecs[:, vo + V_F0:vo + V_F0 + 1], op0=ALU.mult, op1=ALU.add), reads=[xc_b, Bvecs],
                            writes=[xc_b])
                    S.op("pool", lambda e, xc=xc, gi=gi: e.tensor_tensor(out=xc[:, :n], in0=xc[:, :n], in1=gi[:, :n],
                                                                        op=ALU.mult), reads=[xc_b, gi_b], writes=[xc_b])
                for c in cs:
                    r, r_b = X[c]["r"]
                    xc, xc_b = X[c]["xc"]
                    S.dma("pool", "sv_r%d" % (c % NB), rg_d[:, 0, c, t0_:t0_ + n], r[:, :n], reads=[r_b],
                          writes=[Brg[ti][c][0]], nosuffix=True)
                    S.dma("pool", "sv_g%d" % (c % NB), rg_d[:, 1, c, t0_:t0_ + n], xc[:, :n], reads=[xc_b],
                          writes=[Brg[ti][c][1]], nosuffix=True)
            for c in cs:
                r, r_b = X[c]["r"]
                xc, xc_b = X[c]["xc"]
                T2, T2_b = X[c]["T2"]
                S.op("dve", lambda e, T2=T2, r=r, xc=xc, c=c: e.tensor_tensor_scan(
                    out=T2[:, :n], data0=r[:, :n], data1=xc[:, :n], initial=state[:, c:c + 1], op0=ALU.mult,
                    op1=ALU.add), reads=[r_b, xc_b, Bstate, T2_b], writes=[T2_b])
                S.op("dve", lambda e, T2=T2, c=c: e.tensor_copy(out=state[:, c:c + 1], in_=T2[:, n - 1:n]),
                     reads=[T2_b], writes=[Bstate])
            if want_out:
                for c in cs:
                    T1, T1_b = X[c]["T1"]

                    def cons_yr(j, ps, ps_b, T1=T1, T1_b=T1_b):
                        S.op("act", lambda e: e.activation(out=T1[:, :n], in_=ps[:, :n], func=AF.Gelu_apprx_tanh),
                             reads=[ps_b, T1_b], writes=[T1_b])
                    linear(L["w_in"], D, O_YR + c * 128, 128, hsrc, [BhT], n, cons_yr)
                for c in cs:
                    T1, T1_b = X[c]["T1"]
                    T2, T2_b = X[c]["T2"]
                    S.op("dve", lambda e, T2=T2, T1=T1, c=c: e.tensor_tensor(out=lruT[:, c, :n], in0=T2[:, :n],
                                                                          in1=T1[:, :n], op=ALU.mult),
                         reads=[T2_b, T1_b], writes=[BlruT])

    S.dma("sp", "vecs", vecs, vecs_in, writes=[Bvecs])
    S.op("dve", lambda e: e.memset(ones32, 1.0 / D), writes=[Bones32])
    S.op("dve", lambda e: e.memset(onesb, 1.0), writes=[Bonesb])
    S.dma("sp", "biasT", biasT, bias_in, writes=[BbiasT])

    def layer_prelude(l):
        vo = L["vo"]
        st, st_b = stg.next()
        stv = st[:, :2048].rearrange("p (k n) -> p k n", k=16)
        S.dma("sp", st_b.name, stv, wbd_all[l], writes=[st_b])
        S.op("dve", lambda e, stv=stv: e.tensor_copy(out=wbd, in_=stv), reads=[st_b], writes=[Bwbd])
        e_ = tmpv[:, 0, :]
        acc = tmpv[:, 1, :]
        S.op("act", lambda e: e.activation(out=e_, in_=vecs[:, vo + V_LAM:vo + V_LAM + 8], func=AF.Exp, scale=-1.0),
             reads=[Bvecs], writes=[Btmpv])
        S.op("dve", lambda e: e.tensor_scalar(out=acc, in0=e_, scalar1=-1.0 / 6.0, scalar2=1.0 / 5.0, op0=ALU.mult,
                                              op1=ALU.add), reads=[Btmpv], writes=[Btmpv])
        for coef in (-1.0 / 4.0, 1.0 / 3.0, -1.0 / 2.0, 1.0):
            S.op("dve", lambda e: e.tensor_tensor(out=acc, in0=acc, in1=e_, op=ALU.mult), reads=[Btmpv], writes=[Btmpv])
            if coef < 0:
                S.op("dve", lambda e, coef=coef: e.tensor_scalar(out=acc, in0=acc, scalar1=-1.0, scalar2=-coef,
                                                                 op0=ALU.mult, op1=ALU.add), reads=[Btmpv], writes=[Btmpv])
            else:
                S.op("dve", lambda e, coef=coef: e.tensor_scalar(out=acc, in0=acc, scalar1=-1.0, scalar2=coef,
                                                                 op0=ALU.mult, op1=ALU.add), reads=[Btmpv], writes=[Btmpv])
        S.op("dve", lambda e: e.tensor_tensor(out=acc, in0=acc, in1=e_, op=ALU.mult), reads=[Btmpv], writes=[Btmpv])
        for k_, mul_ in enumerate((-4.0, -8.0, -16.0, 8.0)):
            S.op("dve", lambda e, k_=k_, mul_=mul_: e.tensor_scalar(out=cf[:, k_ * 8:(k_ + 1) * 8], in0=acc, scalar1=mul_,
                                                                    scalar2=None, op0=ALU.mult), reads=[Btmpv], writes=[Bcf])

    def inproj_xr_hist(n):
        for c in range(8):
            def cons(j, ps, ps_b, c=c):
                S.op("act", lambda e: e.copy(out=hist[:, c, 0:3], in_=ps[:, n - 3:n]), reads=[ps_b], writes=[Bhist])
            linear(L["w_in"], D, O_XR + c * 128, 128, lambda kc: hT[:, kc, :n], [BhT], n, cons)

    def inproj_v(n, blk0):
        st, st_b = stg.next()
        wb_, wb_b = wbr.next()
        stv = st[:, :1024].rearrange("p (k n) -> p k n", k=8)
        wb3 = wb_[:, :1024].rearrange("p (k n) -> p k n", k=8)
        S.dma("sp", st_b.name, stv, L["w_in"].rearrange("(kc p) n -> p kc n", p=128)[:, :, O_V:O_V + 128], writes=[st_b])
        S.op("pool", lambda e, wb_=wb_, st=st: e.tensor_copy(out=wb_[:, :1024], in_=st[:, :1024]), reads=[st_b], writes=[wb_b])
        for i in range(n // 128):
            ps, ps_b = psum()

            def mmv(e, ps=ps, i=i):
                ins = None
                for kc in range(8):
                    ins = e.matmul(ps[:, :128], lhsT=hT[:, kc, i * 128:(i + 1) * 128], rhs=wb3[:, kc, :],
                                   start=(kc == 0), stop=(kc == 7))
                return ins
            S.op("pe", mmv, reads=[BhT, wb_b], writes=[ps_b])
            S.op("act", lambda e, ps=ps, i=i: e.copy(out=vtm[:, blk0 + i, :], in_=ps[:, :128]), reads=[ps_b],
                 writes=[Bvtm])

    def p23_prelude(l):
        sink32, Bsink32 = tB.next()
        for kh_ in range(2):
            S.dma("sp", "sink32", sink32[kh_ * 32:kh_ * 32 + 1, :], sinks_all[l][0:1, kh_ * 512:(kh_ + 1) * 512],
                  writes=[Bsink32])
        for kh_ in range(2):
            S.op("act", lambda e, kh_=kh_, sink32=sink32: e.activation(
                out=esink[kh_ * 32:kh_ * 32 + 1, :], in_=sink32[kh_ * 32:kh_ * 32 + 1, :], func=AF.Exp),
                reads=[Bsink32], writes=[Besink])
        S.dma("sp", "xt", xt[:, :, :256], memT_in, writes=[Bxt])
        rmsnorm(xt, Bxt, 256, V_GMEM, hT, BhT)

        def cons_km(j, ps, ps_b):
            S.op("act", lambda e: e.copy(out=kmT[:, j, :], in_=ps[:, :256]), reads=[ps_b], writes=[BkmT])
        linear(w_memkv_all[l], D, 0, 512, lambda kc: hT[:, kc, :256], [BhT], 256, cons_km)
        st, st_b = stg.next()
        wbv_, wbv_b = wbr.next()
        stv = st.rearrange("p (k n) -> p k n", k=8)
        wv3 = wbv_.rearrange("p (k n) -> p k n", k=8)
        S.dma("sp", st_b.name, stv, w_memkv_all[l].rearrange("(kc p) n -> p kc n", p=128)[:, :, 512:1024], writes=[st_b])
        S.op("pool", lambda e, wbv_=wbv_, st=st: e.tensor_copy(out=wbv_, in_=st), reads=[st_b], writes=[wbv_b])
        for mb in range(2):
            ps, ps_b = psum()

            def mmv(e, ps=ps, mb=mb):
                ins = None
                for kc in range(8):
                    ins = e.matmul(ps, lhsT=hT[:, kc, mb * 128:(mb + 1) * 128], rhs=wv3[:, kc, :], start=(kc == 0),
                                   stop=(kc == 7))
                return ins
            S.op("pe", mmv, reads=[BhT, wbv_b], writes=[ps_b])
            S.op("act", lambda e, ps=ps, mb=mb: e.copy(out=vm[:, mb, :], in_=ps), reads=[ps_b], writes=[Bvm])


    def final_norm_store(t0):
        vo = L["vo"]
        rstd, Brstd = tB.next()
        for c in range(8):
            S.op("act", lambda e, c=c: e.activation(out=big32[:, c, :], in_=xt[:, c, :], func=AF.Square),
                 reads=[Bxt], writes=[Bbig])
        ps, ps_b = psum()

        def mmf(e, ps=ps):
            ins = None
            for c in range(8):
                ins = e.matmul(ps, lhsT=ones32, rhs=big32[:, c, :], start=(c == 0), stop=(c == 7))
            return ins
        S.op("pe", mmf, reads=[Bbig, Bones32], writes=[ps_b])
        S.op("act", lambda e, ps=ps: e.activation(out=rstd, in_=ps, func=AF.Sqrt, bias=EPS, scale=1.0), reads=[ps_b],
             writes=[Brstd])
        S.op("dve", lambda e: e.reciprocal(out=rstd, in_=rstd), reads=[Brstd], writes=[Brstd])
        for c in range(8):
            S.op("dve", lambda e, c=c: e.scalar_tensor_tensor(
                out=big32[:, c, :], in0=xt[:, c, :], scalar=vecs[:, vo + V_GFIN + c:vo + V_GFIN + c + 1], in1=rstd,
                op0=ALU.mult, op1=ALU.mult), reads=[Bxt, Brstd, Bvecs, Bbig], writes=[Bbig])
        S.dma("sp", "y_out", y_out[:, :, t0:t0 + T], big32, reads=[Bbig], writes=[Byo])

    def p23_phase(l, last):
        vo = L["vo"]
        HN = 128
        S.dma("sp", "xt", xt[:, :, :HN].rearrange("p c n -> p (c n)") if False else xt[:, :, :HN],
              cc1_out[l].ap()[0:128, 0:1024].rearrange("p (c n) -> p c n", c=8), reads=[Bcc1o[l]], writes=[Bxt])
        S.op("dve", lambda e: e.tensor_scalar(out=xt[:, :, :HN], in0=xt[:, :, :HN], scalar1=vecs[:, V_OMF0:V_OMF0 + 1],
                                              scalar2=None, op0=ALU.mult), reads=[Bxt, Bvecs], writes=[Bxt])
        S.dma("sp", "state", state, cc1_out[l].ap()[0:128, 1024:1032], reads=[Bcc1o[l]], writes=[Bstate])
        S.op("dve", lambda e: e.tensor_scalar(out=state, in0=state, scalar1=vecs[:, V_OMF0:V_OMF0 + 1],
                                              scalar2=None, op0=ALU.mult), reads=[Bstate, Bvecs], writes=[Bstate])
        rmsnorm(xt, Bxt, HN, V_GM, hT, BhT)

        def cons_k0(j, ps, ps_b):
            S.op("act", lambda e: e.copy(out=kT[:, 0:128], in_=ps[:, :128]), reads=[ps_b], writes=[BkT])
        linear(L["w_in"], D, O_K, 128, lambda kc: hT[:, kc, :HN], [BhT], HN, cons_k0)
        inproj_v(HN, 0)

        for ti in range(ntiles):
            t0 = ti * T
            n = T
            S.dma("sp", "xt", xt, xres[:, :, t0:t0 + T], reads=[Bxres[ti]], writes=[Bxt])
            S.dma("sp", "hT", hT, hsc[:, :, t0:t0 + T], reads=[Bhsc[ti]], writes=[BhT])
            rh = lambda kc: hT[:, kc, :n]

            def cons_q(j, ps, ps_b):
                S.op("act", lambda e: e.copy(out=qT[:, j, :], in_=ps), reads=[ps_b], writes=[BqT])
            linear(L["w_in"], D, O_Q, 512, rh, [BhT], n, cons_q)

            def cons_k(j, ps, ps_b):
                S.op("act", lambda e: e.copy(out=kT[:, 128:128 + T], in_=ps), reads=[ps_b], writes=[BkT])
            linear(L["w_in"], D, O_K, 128, rh, [BhT], n, cons_k)
            inproj_v(T, 1)

            def cons_cq(j, ps, ps_b):
                S.op("act", lambda e: e.copy(out=cqT[:, j, :], in_=ps), reads=[ps_b], writes=[BcqT])
            linear(L["w_in"], D, O_CQ, 512, rh, [BhT], n, cons_cq)

            for bi in range(4):
                for kh in range(2):
                    r0 = kh * 64
                    pt, pt_b = pT.next()
                    for kb in range(2):
                        ps, ps_b = psum()
                        kcol = (bi + kb) * 128
                        S.op("pe", lambda e, ps=ps, kcol=kcol, r0=r0, bi=bi: e.matmul(
                            ps, lhsT=kT[r0:r0 + 64, kcol:kcol + 128], rhs=qT[r0:r0 + 64, :, bi * 128:(bi + 1) * 128],
                            start=True, stop=True), reads=[BkT, BqT], writes=[ps_b])
                        sc, sc_b = tB.next()
                        bsrc, bsrc_b = biasT[:, kh, kb], BbiasT
                        S.op("dve", lambda e, sc=sc, ps=ps, bsrc=bsrc: e.scalar_tensor_tensor(
                            out=sc.rearrange("p (g q) -> p g q", g=4), in0=ps.rearrange("p (g q) -> p g q", g=4), scalar=0.125,
                            in1=bsrc, op0=ALU.mult, op1=ALU.add), reads=[ps_b, bsrc_b], writes=[sc_b])
                        if ti == 0 and bi == 0 and kb == 0:
                            S.op("dve", lambda e, sc=sc: e.tensor_scalar(out=sc, in0=sc, scalar1=vecs[:, V_MASKA:V_MASKA + 1],
                                                                        scalar2=None, op0=ALU.add),
                                 reads=[sc_b, Bvecs], writes=[sc_b])
                        S.op("act", lambda e, sc=sc, pt=pt, kb=kb: e.activation(out=pt[:, kb, :], in_=sc, func=AF.Exp),
                             reads=[sc_b], writes=[pt_b])
                    psn, psn_b = psum()

                    def mmn(e, psn=psn, pt=pt, kh=kh, bi=bi):
                        ins = None
                        for kb in range(2):
                            ins = e.matmul(psn[0:64, :], lhsT=vtm[:, bi + kb, kh * 64:(kh + 1) * 64], rhs=pt[:, kb, :],
                                           start=(kb == 0), stop=(kb == 1))
                        return ins
                    S.op("pe", mmn, reads=[Bvtm, pt_b], writes=[psn_b])
                    psd, psd_b = psum()

                    def mmd(e, psd=psd, pt=pt, kh=kh):
                        for kb in range(2):
                            e.matmul(psd[0:64, :], lhsT=onesb[:, 0:64], rhs=pt[:, kb, :], start=(kb == 0), stop=False)
                        return e.matmul(psd[0:64, :], lhsT=onesb[kh * 32:kh * 32 + 1, 0:64], rhs=esink[kh * 32:kh * 32 + 1, :],
                                        start=False, stop=True)
                    S.op("pe", mmd, reads=[Bonesb, pt_b, Besink], writes=[psd_b])
                    rd, rd_b = tB.next()
                    S.op("dve", lambda e, rd=rd, psd=psd: e.reciprocal(out=rd[0:64, :], in_=psd[0:64, :]), reads=[psd_b],
                         writes=[rd_b])
                    S.op("dve", lambda e, rd=rd, psn=psn, kh=kh, bi=bi: e.tensor_tensor(
                        out=attnT[:, kh * 4:(kh + 1) * 4, bi * 128:(bi + 1) * 128],
                        in0=psn[0:64, :].rearrange("p (g q) -> p g q", g=4), in1=rd[0:64, :].rearrange("p (g q) -> p g q", g=4),
                        op=ALU.mult), reads=[psn_b, rd_b], writes=[BattnT])
            S.op("pool", lambda e: e.tensor_copy(out=kT[:, 0:128], in_=kT[:, T:T + 128]), reads=[BkT], writes=[BkT])
            S.op("pool", lambda e: e.tensor_copy(out=vtm[:, 0, :], in_=vtm[:, 4, :]), reads=[Bvtm], writes=[Bvtm])

            for hd in range(4):
                pt, pt_b = pT.next()
                for mb in range(2):
                    ps, ps_b = psum()
                    S.op("pe", lambda e, ps=ps, hd=hd, mb=mb: e.matmul(ps, lhsT=kmT[:, hd, mb * 128:(mb + 1) * 128],
                                                                    rhs=cqT[:, hd, :], start=True, stop=True),
                         reads=[BkmT, BcqT], writes=[ps_b])
                    S.op("act", lambda e, ps=ps, pt=pt, mb=mb: e.activation(out=pt[:, mb, :], in_=ps, func=AF.Exp,
                                                                         scale=128.0 ** -0.5), reads=[ps_b], writes=[pt_b])
                psn, psn_b = psum()

                def mmn2(e, psn=psn, pt=pt, hd=hd):
                    ins = None
                    for mb in range(2):
                        ins = e.matmul(psn, lhsT=vm[:, mb, hd * 128:(hd + 1) * 128], rhs=pt[:, mb, :], start=(mb == 0),
                                       stop=(mb == 1))
                    return ins
                S.op("pe", mmn2, reads=[Bvm, pt_b], writes=[psn_b])
                psd, psd_b = psum()

                def mmd2(e, psd=psd, pt=pt):
                    ins = None
                    for mb in range(2):
                        ins = e.matmul(psd, lhsT=onesb, rhs=pt[:, mb, :], start=(mb == 0), stop=(mb == 1))
                    return ins
                S.op("pe", mmd2, reads=[Bonesb, pt_b], writes=[psd_b])
                rd, rd_b = tB.next()
                S.op("dve", lambda e, rd=rd, psd=psd: e.reciprocal(out=rd, in_=psd), reads=[psd_b], writes=[rd_b])
                S.op("dve", lambda e, rd=rd, psn=psn, hd=hd: e.tensor_tensor(out=memoT[:, hd, :], in0=psn, in1=rd, op=ALU.mult),
                     reads=[psn_b, rd_b], writes=[BmemoT])

            lru(T, ti == 0, ti)

            branches = [
                (w_attn_all[l], 512, 64, lambda kc: attnT[:, kc, :], BattnT),
                (w_lru_all[l], D, 128, lambda kc: lruT[:, kc, :], BlruT),
                (w_memo_all[l], 512, 128, lambda kc: memoT[:, kc, :], BmemoT),
            ]
            for b, (Wb, Kb, kpb, rb, Brb) in enumerate(branches):
                sg_sb = {}
                for g0 in range(0, 8, 4):
                    def cons_g(j, ps, ps_b, b=b, g0=g0):
                        t, t_b = tB.next()
                        jj = g0 + j
                        S.op("act", lambda e, t=t, ps=ps: e.activation(
                            out=t, in_=ps, func=AF.Sigmoid, bias=vecs[:, vo + V_GB + b * 8 + jj:vo + V_GB + b * 8 + jj + 1]),
                            reads=[ps_b, Bvecs], writes=[t_b])
                        sg_sb[jj] = (t, t_b)
                    linear(L["w_in"], D, O_G + b * 1024 + g0 * 128, 512, rh, [BhT], n, cons_g)

                    def cons_b(j, ps, ps_b, b=b, g0=g0):
                        jj = g0 + j
                        t, t_b = sg_sb[jj]
                        if b == 0:
                            S.op("dve", lambda e, t=t, ps=ps: e.tensor_tensor(out=big32[:, jj, :], in0=t, in1=ps, op=ALU.mult),
                                 reads=[t_b, ps_b], writes=[Bbig])
                        else:
                            S.op("dve", lambda e, t=t, ps=ps: e.tensor_tensor(out=t, in0=t, in1=ps, op=ALU.mult),
                                 reads=[t_b, ps_b], writes=[t_b])
                            if b == 1:
                                S.op("pool", lambda e, t=t: e.tensor_tensor(out=big32[:, jj, :], in0=big32[:, jj, :], in1=t,
                                                                            op=ALU.add), reads=[t_b, Bbig], writes=[Bbig])
                            else:
                                S.op("pool", lambda e, t=t: e.tensor_tensor(out=merged[:, jj, :], in0=big32[:, jj, :], in1=t,
                                                                            op=ALU.add), reads=[t_b, Bbig], writes=[Bmerged])
                    linear(Wb, Kb, g0 * 128, 512, rb, [Brb], n, cons_b, kp=kpb)

            def cons_o(j, ps, ps_b):
                S.op("dve", lambda e, ps=ps: e.tensor_tensor(out=xt[:, j, :], in0=xt[:, j, :], in1=ps, op=ALU.add),
                     reads=[ps_b, Bxt], writes=[Bxt])
            linear(w_out_all[l], D, 0, D, lambda kc: merged[:, kc, :], [Bmerged], n, cons_o)

            ffn(T, V_G2)
            if not last:
                S.dma("sp", "xres", xres[:, :, t0:t0 + T], xt, reads=[Bxt], writes=[Bxres[ti]])
            else:
                final_norm_store(t0)

    Bcc1i = [Buf("cc1i%d" % l) for l in range(depth)]
    Bcc1o = [Buf("cc1o%d" % l) for l in range(depth)]
    Bcc2i = [Buf("cc2i%d" % l) for l in range(depth)]
    Bcc2o = [Buf("cc2o%d" % l) for l in range(depth)]
    Byo = Buf("y_out")

    def p1_phase(l):
        S.op("dve", lambda e: e.memset(state, 0.0), writes=[Bstate])
        if l == 0:
            S.dma("sp", "xt", xt[:, :, :4], xh4_in, writes=[Bxt])
        else:
            S.dma("sp", "xt", xt[:, :, :4], cc2_out[l - 1].ap()[0:128, :].rearrange("p (c n) -> p c n", c=8),
                  reads=[Bcc2o[l - 1]], writes=[Bxt])
            S.op("dve", lambda e: e.tensor_scalar(out=xt[:, :, :4], in0=xt[:, :, :4], scalar1=vecs[:, V_OMF0:V_OMF0 + 1],
                                                  scalar2=None, op0=ALU.mult), reads=[Bxt, Bvecs], writes=[Bxt])
        ffn(4, V_G1)
        rmsnorm(xt, Bxt, 4, V_GM, hT, BhT)
        inproj_xr_hist(4)
        for ti in range(ntiles):
            t0 = ti * T
            if l == 0:
                S.dma("sp", "xt", xt, x_in[:, :, t0:t0 + T], writes=[Bxt])
            else:
                S.dma("sp", "xt", xt, xres[:, :, t0:t0 + T], reads=[Bxres[ti]], writes=[Bxt])
            ffn(T, V_G1)
            S.dma("sp", "xres", xres[:, :, t0:t0 + T], xt, reads=[Bxt], writes=[Bxres[ti]])
            rmsnorm(xt, Bxt, T, V_GM, hT, BhT)
            S.dma("pool", "hsv", hsc[:, :, t0:t0 + T], hT, reads=[BhT], writes=[Bhsc[ti]])
            lru(T, ti == 0, ti)

    def exchange1(l):
        S.dma("sp", "cc1i", cc1_in[l].ap()[:, 0:1024].rearrange("p (c n) -> p c n", c=8), xres[:, :, NT - 128:NT],
              reads=[Bxres[ntiles - 1]], writes=[Bcc1i[l]])
        S.dma("sp", "cc1i", cc1_in[l].ap()[:, 1024:1032], state, reads=[Bstate], writes=[Bcc1i[l]])
        S.coll("cc", lambda e: e.collective_compute("AllGather", ALU.bypass, replica_groups=RG,
                                                    ins=[cc1_in[l].ap().opt()], outs=[cc1_out[l].ap().opt()]),
               reads=[Bcc1i[l]], writes=[Bcc1o[l]])

    def exchange2(l):
        S.dma("sp", "cc2i", cc2_in[l].ap().rearrange("p (c n) -> p c n", c=8), xres[:, :, NT - 4:NT],
              reads=[Bxres[ntiles - 1]], writes=[Bcc2i[l]])
        S.coll("cc", lambda e: e.collective_compute("AllGather", ALU.bypass, replica_groups=RG,
                                                    ins=[cc2_in[l].ap().opt()], outs=[cc2_out[l].ap().opt()]),
               reads=[Bcc2i[l]], writes=[Bcc2o[l]])

    for l in range(depth):
        S.suffix = "_L%d" % l
        L["vo"] = l * NV
        L["w_in"] = w_in_all[l]
        layer_prelude(l)
        L["p23"] = False
        L["wg"], L["wu"], L["wd"] = f1[0][l], f1[1][l], f1[2][l]
        p1_phase(l)
        exchange1(l)
        L["p23"] = True
        L["wg"], L["wu"], L["wd"] = f2[0][l], f2[1][l], f2[2][l]
        p23_prelude(l)
        p23_phase(l, l == depth - 1)
        if l < depth - 1:
            exchange2(l)
    S.finish([Byo])
    S.emit()
    return nc


def _fm(v):
    return np.ascontiguousarray(np.asarray(v, np.float32).reshape(-1, 128).T)


def _to_fm(xtok):
    t = xtok.shape[0]
    return np.ascontiguousarray(xtok.T.reshape(8, 128, t).transpose(1, 0, 2))


def _from_fm(xfm):
    t = xfm.shape[2]
    return np.ascontiguousarray(xfm.transpose(1, 0, 2).reshape(1024, t).T)


def _alibi_tables():
    slopes = np.array([2.0 ** (-8.0 * (i + 1) / 8) for i in range(8)], dtype=np.float32)
    j = np.arange(128)[:, None]
    i = np.arange(128)[None, :]
    tab = np.zeros((128, 2, 2, 4, 128), np.float32)
    for kh in range(2):
        for kb in range(2):
            kpos = kb * 128 + j
            qpos = 128 + i
            dist = np.abs(qpos - kpos).astype(np.float32)
            kch = kpos // 64
            qch = qpos // 64
            valid = (kch >= qch - 2) & (kch <= qch)
            for g in range(4):
                tab[:, kh, kb, g, :] = np.where(valid, -slopes[kh * 4 + g] * dist, np.float32(NEG))
    return tab


_PROGS = {}


def _prog(depth):
    if depth not in _PROGS:
        _PROGS[depth] = build(depth)
    return _PROGS[depth]


def _prep_inputs(x, mem, ffn1_norm, ffn1_w_gate, ffn1_w_up, ffn1_w_down, mix_norm, w_in, gate_bias,
                 attn_sinks, w_attn_out, conv_w, conv_b, lru_wa, lru_ba, lru_wx, lru_bx, lru_lambda,
                 w_lru_out, mem_norm, w_mem_kv, w_mem_out, w_out, ffn2_norm, ffn2_w_gate, ffn2_w_up,
                 ffn2_w_down, final_norm, depth=None):
    f32 = np.float32
    x = np.asarray(x, f32)
    mem = np.asarray(mem, f32)
    if depth is None:
        depth = ffn1_norm.shape[0]
    ncores = 8
    ca = lambda a: np.ascontiguousarray(np.asarray(a, f32)[:depth])
    tab = _alibi_tables()
    tabF_A = tab[:, :, 0].copy()
    tabF_A[...] = NEG
    tabF_B = np.ascontiguousarray(tab[:, :, 0])
    qperm = np.concatenate([np.concatenate([np.arange(j * 64, j * 64 + 64), np.arange((4 + j) * 64, (4 + j) * 64 + 64)])
                            for j in range(4)])
    win = np.array(np.asarray(w_in, f32)[:depth], copy=True)
    win[:, :, :512] = win[:, :, qperm]
    win = np.ascontiguousarray(win)
    wbd = np.zeros((depth, 128, 16, 128), f32)
    for l in range(depth):
        for gi_, wsrc in enumerate((lru_wa[l], lru_wx[l])):
            for c in range(8):
                wbd[l, 0:64, gi_ * 8 + c, 0:64] = wsrc[2 * c]
                wbd[l, 64:128, gi_ * 8 + c, 64:128] = wsrc[2 * c + 1]
    sinks = np.ascontiguousarray(np.repeat(np.asarray(attn_sinks, f32)[:depth], 128, axis=1)[:, None, :])
    shared = {"w_in": win, "wbd": wbd,
              "f1_wg": ca(ffn1_w_gate), "f1_wu": ca(ffn1_w_up), "f1_wd": ca(ffn1_w_down),
              "f2_wg": ca(ffn2_w_gate), "f2_wu": ca(ffn2_w_up), "f2_wd": ca(ffn2_w_down),
              "biasT": tab, "sinks": sinks, "w_attn": ca(w_attn_out), "w_lru": ca(w_lru_out),
              "w_memkv": ca(w_mem_kv), "w_memo": ca(w_mem_out), "w_out": ca(w_out)}
    in_maps = []
    for c in range(ncores):
        b, half = c // 2, c % 2
        v = np.zeros((128, depth * NV), f32)
        for l in range(depth):
            o = l * NV
            v[:, o + V_G1:o + V_G1 + 8] = _fm(ffn1_norm[l])
            v[:, o + V_GM:o + V_GM + 8] = _fm(mix_norm[l])
            v[:, o + V_G2:o + V_G2 + 8] = _fm(ffn2_norm[l])
            for j in range(4):
                v[:, o + V_CW + j * 8:o + V_CW + j * 8 + 8] = _fm(conv_w[l, j])
            v[:, o + V_CB:o + V_CB + 8] = _fm(conv_b[l])
            v[:, o + V_BA:o + V_BA + 8] = _fm(lru_ba[l])
            v[:, o + V_BX:o + V_BX + 8] = _fm(lru_bx[l])
            v[:, o + V_LAM:o + V_LAM + 8] = _fm(lru_lambda[l])
            v[:, o + V_GB:o + V_GB + 24] = _fm(gate_bias[l])
            v[:, o + V_GMEM:o + V_GMEM + 8] = _fm(mem_norm[l])
            v[:, o + V_GFIN:o + V_GFIN + 8] = _fm(final_norm)
            v[:, o + V_F0] = 1.0 if half == 0 else 0.0
            v[:, o + V_OMF0] = 0.0 if half == 0 else 1.0
            v[:, o + V_MASKA] = NEG if half == 0 else 0.0
        xs = _to_fm(x[b, half * NT:(half + 1) * NT, :])
        if half == 0:
            xh4 = np.zeros((128, 8, 4), f32)
        else:
            xh4 = _to_fm(x[b, NT - 4:NT, :])
        m = dict(shared)
        m.update({"x_in": xs, "xh4_in": xh4, "vecs": v,
                  "memT": _to_fm(mem[b])})
        in_maps.append(m)
    return in_maps, depth


def kernel(x, mem, ffn1_norm, ffn1_w_gate, ffn1_w_up, ffn1_w_down, mix_norm, w_in, gate_bias,
           attn_sinks, w_attn_out, conv_w, conv_b, lru_wa, lru_ba, lru_wx, lru_bx, lru_lambda,
           w_lru_out, mem_norm, w_mem_kv, w_mem_out, w_out, ffn2_norm, ffn2_w_gate, ffn2_w_up,
           ffn2_w_down, final_norm):
    in_maps, depth = _prep_inputs(x, mem, ffn1_norm, ffn1_w_gate, ffn1_w_up, ffn1_w_down, mix_norm, w_in,
                                  gate_bias, attn_sinks, w_attn_out, conv_w, conv_b, lru_wa, lru_ba, lru_wx,
                                  lru_bx, lru_lambda, w_lru_out, mem_norm, w_mem_kv, w_mem_out, w_out,
                                  ffn2_norm, ffn2_w_gate, ffn2_w_up, ffn2_w_down, final_norm)
    ncores = 8
    res = run_bass_kernel_spmd(_prog(depth), in_maps, core_ids=list(range(ncores))).results
    B = np.asarray(x).shape[0]
    out = np.empty((B, 2 * NT, D), np.float32)
    for c in range(ncores):
        out[c // 2, (c % 2) * NT:(c % 2 + 1) * NT, :] = _from_fm(np.asarray(res[c]["y_out"], np.float32))
    return out
```

```python
import numpy as np
import concourse.bass as bass
import concourse.mybir as mybir
from concourse.bass_utils import run_bass_kernel_spmd

F32 = mybir.dt.float32
BF16 = mybir.dt.bfloat16
AF = mybir.ActivationFunctionType
ALU = mybir.AluOpType

D = 1024
NT = 4096
T = 512
DFF = 2816
INW = 6400
O_Q, O_K, O_V, O_XR, O_YR, O_CQ, O_G = 0, 512, 640, 768, 1792, 2816, 3328
NV = 136
V_G1, V_GM, V_G2, V_CW, V_CB, V_BA, V_BX, V_LAM, V_GB, V_GMEM, V_GFIN, V_F0, V_OMF0, V_MASKA = \
    0, 8, 16, 24, 56, 64, 72, 80, 88, 112, 120, 128, 129, 130
EPS = 1e-6
NEG = -30000.0

ENGS = ("pe", "act", "dve", "pool", "sp")


class Buf:
    __slots__ = ("name", "w", "r")

    def __init__(self, name):
        self.name = name
        self.w = None
        self.r = {}


class Sched:
    def __init__(self, nc):
        self.nc = nc
        self.ops = {e: [] for e in ENGS}
        self.cnt = {e: 0 for e in ENGS}
        self.sem = {e: nc.alloc_semaphore("s_" + e) for e in ENGS}
        self.dsem = {}
        self.dcnt = {}
        self.waited = {}
        self.suffix = ""

    def _need(self, eng, dep, waits):
        if dep is None:
            return
        kind, key, val = dep
        if kind == "eng" and key == "pe" and eng == "pe":
            return
        if self.waited.get((eng, kind, key), 0) >= val:
            return
        if val > waits.get((kind, key), 0):
            waits[(kind, key)] = val

    def _collect(self, eng, reads, writes):
        waits = {}
        for b in reads:
            self._need(eng, b.w, waits)
        for b in writes:
            self._need(eng, b.w, waits)
            for d in b.r.values():
                self._need(eng, d, waits)
        for (kind, key), val in waits.items():
            self.waited[(eng, kind, key)] = val
        return waits

    def _mark(self, me, reads, writes):
        for b in reads:
            b.r[(me[0], me[1])] = me
        for b in writes:
            b.w = me
            b.r = {}

    def _emit_waits(self, e, waits):
        for (kind, key), val in waits.items():
            e.wait_ge(self.sem[key] if kind == "eng" else self.dsem[key], val)

    def op(self, eng, fn, reads=(), writes=()):
        waits = self._collect(eng, reads, writes)
        self.cnt[eng] += 1
        self._mark(("eng", eng, self.cnt[eng]), reads, writes)
        sem = self.sem[eng]

        def run(e, fn=fn, waits=waits, sem=sem):
            self._emit_waits(e, waits)
            fn(e).then_inc(sem, 1)
        self.ops[eng].append(run)

    def coll(self, key, fn, reads=(), writes=()):
        key = key + self.suffix
        if key not in self.dsem:
            self.dsem[key] = self.nc.alloc_semaphore("c_" + key)
            self.dcnt[key] = 0
        waits = self._collect("pool", reads, writes)
        self.dcnt[key] += 1
        self._mark(("dma", key, self.dcnt[key]), reads, writes)
        sem = self.dsem[key]

        def run(e, waits=waits, sem=sem, fn=fn):
            self._emit_waits(e, waits)
            fn(e).then_inc(sem, 1)
        self.ops["pool"].append(run)

    def dma(self, queue, key, out, in_, reads=(), writes=(), nosuffix=False):
        if not nosuffix:
            key = key + self.suffix
        if key not in self.dsem:
            self.dsem[key] = self.nc.alloc_semaphore("d_" + key)
            self.dcnt[key] = 0
        waits = self._collect(queue, reads, writes)
        self.dcnt[key] += 16
        self._mark(("dma", key, self.dcnt[key]), reads, writes)
        sem = self.dsem[key]

        def run(e, waits=waits, sem=sem, out=out, in_=in_):
            self._emit_waits(e, waits)
            e.dma_start(out=out, in_=in_).then_inc(sem, 16)
        self.ops[queue].append(run)

    def finish(self, final_bufs):
        waits = self._collect("sp", final_bufs, ())
        self.ops["sp"].append(lambda e, waits=waits: self._emit_waits(e, waits))

    def emit(self):
        nc = self.nc
        with nc.Block() as block:
            @block.tensor
            def _(e):
                for f in self.ops["pe"]:
                    f(e)

            @block.scalar
            def _(e):
                for f in self.ops["act"]:
                    f(e)

            @block.vector
            def _(e):
                for f in self.ops["dve"]:
                    f(e)

            @block.gpsimd
            def _(e):
                for f in self.ops["pool"]:
                    f(e)

            @block.sync
            def _(e):
                for f in self.ops["sp"]:
                    f(e)


class Rot:
    def __init__(self, nc, name, shape, dtype, n):
        self.items = []
        for i in range(n):
            nm = "%s%d" % (name, i)
            self.items.append((nc.alloc_sbuf_tensor(nm, shape, dtype).ap(), Buf(nm)))
        self.i = 0

    def next(self):
        it = self.items[self.i % len(self.items)]
        self.i += 1
        return it


def build(depth=4, ntiles=NT // T):
    nc = bass.Bass("TRN2", target_bir_lowering=False)
    S = Sched(nc)
    L = {"vo": 0, "p23": False}

    def din(name, shape):
        return nc.dram_tensor(name, shape, F32, kind="ExternalInput").ap()

    def dout(name, shape):
        return nc.dram_tensor(name, shape, F32, kind="ExternalOutput").ap()

    x_in = din("x_in", [128, 8, NT])
    xh4_in = din("xh4_in", [128, 8, 4])
    vecs_in = din("vecs", [128, depth * NV])
    w_in_all = din("w_in", [depth, D, INW])
    wbd_all = din("wbd", [depth, 128, 16, 128])
    f1 = [din("f1_wg", [depth, D, DFF]), din("f1_wu", [depth, D, DFF]), din("f1_wd", [depth, DFF, D])]
    f2 = [din("f2_wg", [depth, D, DFF]), din("f2_wu", [depth, D, DFF]), din("f2_wd", [depth, DFF, D])]
    bias_in = din("biasT", [128, 2, 2, 4, 128])
    sinks_all = din("sinks", [depth, 1, 1024])
    memT_in = din("memT", [128, 8, 256])
    w_attn_all = din("w_attn", [depth, 512, D])
    w_lru_all = din("w_lru", [depth, D, D])
    w_memkv_all = din("w_memkv", [depth, D, D])
    w_memo_all = din("w_memo", [depth, 512, D])
    w_out_all = din("w_out", [depth, D, D])
    y_out = dout("y_out", [128, 8, NT])
    xres = nc.dram_tensor("xres", [128, 8, NT], F32).ap()
    Bxres = [Buf("xres%d" % i) for i in range(ntiles)]
    CW1 = 8 * 128 + 8
    cc1_in = [nc.dram_tensor("cc1_in%d" % l, [128, CW1], F32) for l in range(depth)]
    cc1_out = [nc.dram_tensor("cc1_out%d" % l, [256, CW1], F32) for l in range(depth)]
    cc2_in = [nc.dram_tensor("cc2_in%d" % l, [128, 32], F32) for l in range(depth)]
    cc2_out = [nc.dram_tensor("cc2_out%d" % l, [256, 32], F32) for l in range(depth)]
    RG = [[0, 1], [2, 3], [4, 5], [6, 7]]
    rg_d = nc.dram_tensor("rg_d", [128, 2, 8, NT], F32).ap()
    hsc = nc.dram_tensor("hsc", [128, 8, NT], BF16).ap()
    Bhsc = [Buf("hsc%d" % i) for i in range(ntiles)]
    h1sc = nc.dram_tensor("h1sc", [128, 8, NT], BF16).ap()
    Bh1sc = [Buf("h1sc%d" % i) for i in range(ntiles)]
    Brg = [[[Buf("rg%d_%d_%d" % (i, c, k)) for k in range(2)] for c in range(8)] for i in range(ntiles)]
    p23 = True

    def sb(name, shape, dt=F32):
        return nc.alloc_sbuf_tensor(name, shape, dt).ap()

    vecs = sb("vecs_s", [128, depth * NV]); Bvecs = Buf("vecs")
    cf = sb("cf", [128, 32]); Bcf = Buf("cf")
    tmpv = sb("tmpv", [128, 4, 8]); Btmpv = Buf("tmpv")
    ones32 = sb("ones32", [128, 128]); Bones32 = Buf("ones32")
    onesb = sb("onesb", [128, 128], BF16); Bonesb = Buf("onesb")
    wbd = sb("wbd_s", [128, 16, 128], BF16); Bwbd = Buf("wbd")
    xt = sb("xt", [128, 8, T]); Bxt = Buf("xt")
    big32 = sb("big32", [128, 8, T]); Bbig = Buf("big32")
    hT = sb("hT", [128, 8, T], BF16); BhT = Buf("hT")
    hist = sb("hist", [128, 8, 4]); Bhist = Buf("hist")
    state = sb("state", [128, 8]); Bstate = Buf("state")
    FH = 11
    actT = sb("actT", [128, FH, T], BF16); Bact = Buf("actT")
    stg = Rot(nc, "stg", [128, 4096], F32, 2)
    wbr = Rot(nc, "wbf", [128, 4096], BF16, 2)
    tA = Rot(nc, "tA", [128, T + 4], F32, 4)
    tB = Rot(nc, "tB", [128, T], F32, 20)
    tC = Rot(nc, "tC", [128, T], BF16, 4)
    psb = [(nc.alloc_psum_tensor("ps%d" % i, [128, 512], F32).ap(), Buf("ps%d" % i)) for i in range(8)]
    psi = [0]

    def psum():
        it = psb[psi[0] % 8]
        psi[0] += 1
        return it

    casti = [0]

    def cast_eng():
        casti[0] += 1
        return "pool" if casti[0] % 2 else "act"

    def copy_op(eng, out, in_):
        if eng == "act":
            return lambda e: e.copy(out=out, in_=in_)
        return lambda e: e.tensor_copy(out=out, in_=in_)

    if p23:
        biasT = sb("biasT_s", [128, 2, 2, 4, 128]); BbiasT = Buf("biasT")
        esink = sb("esink", [64, 512], BF16); Besink = Buf("esink")
        qT = sb("qT", [128, 4, T], BF16); BqT = Buf("qT")
        kT = sb("kT", [128, 128 + T], BF16); BkT = Buf("kT")
        vtm = sb("vtm", [128, 5, 128], BF16); Bvtm = Buf("vtm")
        cqT = sb("cqT", [128, 4, T], BF16); BcqT = Buf("cqT")
        attnT = sb("attnT", [64, 8, T], BF16); BattnT = Buf("attnT")
        lruT = sb("lruT", [128, 8, T], BF16); BlruT = Buf("lruT")
        memoT = sb("memoT", [128, 4, T], BF16); BmemoT = Buf("memoT")
        merged, Bmerged = lruT, BlruT
        kmT = sb("kmT", [128, 4, 256], BF16); BkmT = Buf("kmT")
        vm = sb("vm", [128, 2, 512], BF16); Bvm = Buf("vm")
        pT = Rot(nc, "pT", [128, 2, 512], BF16, 2)

    WSC_COLS = depth * 222000
    wsc = nc.dram_tensor("wsc", [128, WSC_COLS], BF16).ap()
    wcache = {}
    wsc_pos = [0]

    def linear(W, K, c0, ncols, rhs, rhs_bufs, n, consume, kp=128):
        kcs = K // kp
        nch = ncols // 128
        G = max(1, min(nch, 32 // kcs))
        Wv = W.rearrange("(kc p) n -> p kc n", p=kp)
        for g0 in range(0, nch, G):
            gn = min(G, nch - g0)
            wb, wb_b = wbr.next()
            ncol = kcs * gn * 128
            wbv = wb[:kp, :ncol].rearrange("p (k n) -> p k n", k=kcs)
            key = (W.tensor.name, W.offset, K, kp, c0, g0, gn)
            if key in wcache:
                pos, cb = wcache[key]
                S.dma("sp", wb_b.name, wb[:kp, :ncol], wsc[:kp, pos:pos + ncol], reads=[cb], writes=[wb_b])
            else:
                st, st_b = stg.next()
                stv = st[:kp, :ncol].rearrange("p (k n) -> p k n", k=kcs)
                S.dma("sp", st_b.name, stv, Wv[:, :, c0 + g0 * 128:c0 + (g0 + gn) * 128], writes=[st_b])
                ce = cast_eng()
                S.op(ce, copy_op(ce, wb[:kp, :ncol], st[:kp, :ncol]), reads=[st_b], writes=[wb_b])
                pos = wsc_pos[0]
                wsc_pos[0] += ncol
                assert wsc_pos[0] <= WSC_COLS
                cb = Buf("wsc%d" % pos)
                S.dma(ce, "wscw_" + ce, wsc[:kp, pos:pos + ncol], wb[:kp, :ncol], reads=[wb_b], writes=[cb])
                wcache[key] = (pos, cb)
            for j in range(gn):
                ps, ps_b = psum()

                def mm(e, ps=ps, wbv=wbv, j=j):
                    ins = None
                    for kc in range(kcs):
                        ins = e.matmul(ps[:, :n], lhsT=wbv[:, kc, j * 128:(j + 1) * 128], rhs=rhs(kc),
                                       start=(kc == 0), stop=(kc == kcs - 1))
                    return ins
                S.op("pe", mm, reads=[wb_b] + list(rhs_bufs), writes=[ps_b])
                consume(g0 + j, ps, ps_b)

    def rmsnorm(xsrc, Bx, n, gofs, out, Bout):
        vo = L["vo"]
        rstd, Brstd = tB.next()
        for c in range(8):
            S.op("act", lambda e, c=c: e.activation(out=big32[:, c, :n], in_=xsrc[:, c, :n], func=AF.Square),
                 reads=[Bx], writes=[Bbig])
        ps, ps_b = psum()

        def mm(e):
            ins = None
            for c in range(8):
                ins = e.matmul(ps[:, :n], lhsT=ones32, rhs=big32[:, c, :n], start=(c == 0), stop=(c == 7))
            return ins
        S.op("pe", mm, reads=[Bbig, Bones32], writes=[ps_b])
        S.op("act", lambda e: e.activation(out=rstd[:, :n], in_=ps[:, :n], func=AF.Sqrt, bias=EPS, scale=1.0),
             reads=[ps_b], writes=[Brstd])
        S.op("dve", lambda e: e.reciprocal(out=rstd[:, :n], in_=rstd[:, :n]), reads=[Brstd], writes=[Brstd])
        for c in range(8):
            S.op("dve", lambda e, c=c: e.scalar_tensor_tensor(
                out=out[:, c, :n], in0=xsrc[:, c, :n], scalar=vecs[:, vo + gofs + c:vo + gofs + c + 1], in1=rstd[:, :n],
                op0=ALU.mult, op1=ALU.mult), reads=[Bx, Brstd, Bvecs], writes=[Bout])

    def ffn(n, gofs, hload=None):
        vo = L["vo"]
        if hload is None:
            rmsnorm(xt, Bxt, n, gofs, hT, BhT)
        else:
            S.dma("sp", "hT", hT, hload[0], reads=[hload[1]], writes=[BhT])
        for half in range(2):
            j0 = half * FH
            gate_sb = {}

            def cons_gate(j, ps, ps_b):
                t, t_b = tB.next()
                S.op("act", lambda e, t=t, ps=ps: e.activation(out=t[:, :n], in_=ps[:, :n], func=AF.Silu),
                     reads=[ps_b], writes=[t_b])
                gate_sb[j] = (t, t_b)

            def cons_up(j, ps, ps_b):
                t, t_b = gate_sb[j]
                S.op("dve", lambda e, t=t, ps=ps, j=j: e.tensor_tensor(out=actT[:, j, :n], in0=t[:, :n], in1=ps[:, :n],
                                                                       op=ALU.mult),
                     reads=[ps_b, t_b], writes=[Bact])
            for s0 in range(0, FH, 4):
                sn = min(4, FH - s0)
                base = j0 + s0
                linear(L["wg"], D, base * 128, sn * 128, lambda kc: hT[:, kc, :n], [BhT], n,
                       lambda j, ps, ps_b, s0=s0: cons_gate(s0 + j, ps, ps_b))
                linear(L["wu"], D, base * 128, sn * 128, lambda kc: hT[:, kc, :n], [BhT], n,
                       lambda j, ps, ps_b, s0=s0: cons_up(s0 + j, ps, ps_b))

            def cons_down(j, ps, ps_b):
                S.op("dve", lambda e, ps=ps, j=j: e.scalar_tensor_tensor(
                    out=xt[:, j, :n], in0=ps[:, :n], scalar=0.5, in1=xt[:, j, :n], op0=ALU.mult, op1=ALU.add),
                    reads=[ps_b, Bxt], writes=[Bxt])
            linear(L["wd"][j0 * 128:(j0 + FH) * 128, :], FH * 128, 0, D, lambda kc: actT[:, kc, :n], [Bact], n, cons_down)

    NB = 4

    def lru(n, first_tile, ti=0):
        vo = L["vo"]
        want_out = L["p23"]
        hsrc = lambda kc: hT[:, kc, :n]
        t0_ = ti * T
        for cb in range(0, 8, NB):
            cs = list(range(cb, min(8, cb + NB)))
            X = {}
            if want_out:
                for c in cs:
                    xc, xc_b = tB.next()
                    r, r_b = tB.next()
                    T1, T1_b = tB.next()
                    T2, T2_b = tB.next()
                    X[c] = {"xc": (xc, xc_b), "r": (r, r_b), "T1": (T1, T1_b), "T2": (T2, T2_b)}
                    S.dma("pool", "ld_r%d" % c, r[:, :n], rg_d[:, 0, c, t0_:t0_ + n], reads=[Brg[ti][c][0]],
                          writes=[r_b], nosuffix=True)
                    S.dma("pool", "ld_g%d" % c, xc[:, :n], rg_d[:, 1, c, t0_:t0_ + n], reads=[Brg[ti][c][1]],
                          writes=[xc_b], nosuffix=True)
            if not want_out:
                    for c in cs:
                        xrb, xrb_b = tA.next()
                        xc, xc_b = tB.next()
                        X[c] = {"xrb": (xrb, xrb_b), "xc": (xc, xc_b)}
                        S.op("dve", lambda e, xrb=xrb, c=c: e.tensor_copy(out=xrb[:, 0:3], in_=hist[:, c, 0:3]),
                             reads=[Bhist], writes=[xrb_b])

                        def cons_xr(j, ps, ps_b, xrb=xrb, xrb_b=xrb_b):
                            S.op("dve", lambda e: e.tensor_copy(out=xrb[:, 3:3 + n], in_=ps[:, :n]), reads=[ps_b], writes=[xrb_b])
                        linear(L["w_in"], D, O_XR + c * 128, 128, hsrc, [BhT], n, cons_xr)
                    for c in cs:
                        xrb, xrb_b = X[c]["xrb"]
                        xc, xc_b = X[c]["xc"]
                        S.op("dve", lambda e, xc=xc, xrb=xrb, c=c: e.tensor_scalar(
                            out=xc[:, :n], in0=xrb[:, 0:n], scalar1=vecs[:, vo + V_CW + c:vo + V_CW + c + 1],
                            scalar2=vecs[:, vo + V_CB + c:vo + V_CB + c + 1], op0=ALU.mult, op1=ALU.add),
                            reads=[xrb_b, Bvecs], writes=[xc_b])
                        for j in range(1, 4):
                            S.op("dve", lambda e, xc=xc, xrb=xrb, c=c, j=j: e.scalar_tensor_tensor(
                                out=xc[:, :n], in0=xrb[:, j:j + n],
                                scalar=vecs[:, vo + V_CW + j * 8 + c:vo + V_CW + j * 8 + c + 1],
                                in1=xc[:, :n], op0=ALU.mult, op1=ALU.add), reads=[xrb_b, Bvecs, xc_b], writes=[xc_b])
                        S.op("dve", lambda e, xrb=xrb, c=c: e.tensor_copy(out=hist[:, c, 0:3], in_=xrb[:, n:n + 3]),
                             reads=[xrb_b], writes=[Bhist])
                        xcb, xcb_b = tC.next()
                        X[c]["xcb"] = (xcb, xcb_b)
                        S.op("pool", lambda e, xcb=xcb, xc=xc: e.tensor_copy(out=xcb[:, :n], in_=xc[:, :n]), reads=[xc_b],
                             writes=[xcb_b])
                    for c in cs:
                        xcb, xcb_b = X[c]["xcb"]
                        psa, psa_b = psum()
                        S.op("pe", lambda e, psa=psa, xcb=xcb, c=c: e.matmul(psa[:, :n], lhsT=wbd[:, c, :], rhs=xcb[:, :n],
                                                                          start=True, stop=True),
                             reads=[Bwbd, xcb_b], writes=[psa_b])
                        psx, psx_b = psum()
                        S.op("pe", lambda e, psx=psx, xcb=xcb, c=c: e.matmul(psx[:, :n], lhsT=wbd[:, 8 + c, :], rhs=xcb[:, :n],
                                                                          start=True, stop=True),
                             reads=[Bwbd, xcb_b], writes=[psx_b])
                        X[c]["ps"] = (psa, psa_b, psx, psx_b)
                    for c in cs:
                        psa, psa_b, psx, psx_b = X[c]["ps"]
                        r, r_b = tB.next()
                        gi, gi_b = tB.next()
                        X[c]["r"] = (r, r_b)
                        X[c]["gi"] = (gi, gi_b)
                        S.op("act", lambda e, r=r, psa=psa, c=c: e.activation(out=r[:, :n], in_=psa[:, :n], func=AF.Sigmoid,
                                                                           bias=vecs[:, vo + V_BA + c:vo + V_BA + c + 1]),
                             reads=[psa_b, Bvecs], writes=[r_b])
                        S.op("act", lambda e, gi=gi, psx=psx, c=c: e.activation(out=gi[:, :n], in_=psx[:, :n], func=AF.Sigmoid,
                                                                             bias=vecs[:, vo + V_BX + c:vo + V_BX + c + 1]),
                             reads=[psx_b, Bvecs], writes=[gi_b])
                    for c in cs:
                        gi, gi_b = X[c]["gi"]
                        xc, xc_b = X[c]["xc"]
                        S.op("pool", lambda e, gi=gi, xc=xc: e.tensor_tensor(out=gi[:, :n], in0=gi[:, :n], in1=xc[:, :n],
                                                                            op=ALU.mult), reads=[gi_b, xc_b], writes=[gi_b])
            if not want_out:
                for c in cs:
                    r, r_b = X[c]["r"]
                    T1, T1_b = tB.next()
                    T2, T2_b = tB.next()
                    X[c]["T1"] = (T1, T1_b)
                    X[c]["T2"] = (T2, T2_b)
                    S.op("act", lambda e, T1=T1, r=r, c=c: e.activation(out=T1[:, :n], in_=r[:, :n], func=AF.Tanh,
                                                                     scale=cf[:, c:c + 1]), reads=[r_b, Bcf], writes=[T1_b])
                    S.op("act", lambda e, T2=T2, r=r, c=c: e.activation(out=T2[:, :n], in_=r[:, :n], func=AF.Tanh,
                                                                     scale=cf[:, 24 + c:25 + c]), reads=[r_b, Bcf], writes=[T2_b])
                for c in cs:
                    r, r_b = X[c]["r"]
                    S.op("act", lambda e, r=r, c=c: e.activation(out=r[:, :n], in_=r[:, :n], func=AF.Exp,
                                                               scale=cf[:, 8 + c:9 + c]), reads=[r_b, Bcf], writes=[r_b])
                for c in cs:
                    r, r_b = X[c]["r"]
                    xc, xc_b = X[c]["xc"]
                    S.op("pool", lambda e, r=r, xc=xc: e.tensor_tensor(out=xc[:, :n], in0=r[:, :n], in1=r[:, :n],
                                                                      op=ALU.mult), reads=[r_b, xc_b], writes=[xc_b])
                for c in cs:
                    r, r_b = X[c]["r"]
                    T1, T1_b = X[c]["T1"]
                    S.op("dve", lambda e, r=r, T1=T1: e.scalar_tensor_tensor(out=r[:, :n], in0=r[:, :n], scalar=1.0,
                                                                          in1=T1[:, :n], op0=ALU.add, op1=ALU.mult),
                         reads=[r_b, T1_b], writes=[r_b])
                    S.op("pool", lambda e, r=r: e.tensor_scalar(out=r[:, :n], in0=r[:, :n], scalar1=1.0, scalar2=None,
                                                              op0=ALU.add), reads=[r_b], writes=[r_b])
                for c in cs:
                    xc, xc_b = X[c]["xc"]
                    T2, T2_b = X[c]["T2"]
                    S.op("dve", lambda e, xc=xc, T2=T2: e.scalar_tensor_tensor(out=xc[:, :n], in0=xc[:, :n], scalar=1.0,
                                                                            in1=T2[:, :n], op0=ALU.add, op1=ALU.mult),
                         reads=[xc_b, T2_b], writes=[xc_b])
                for c in cs:
                    xc, xc_b = X[c]["xc"]
                    gi, gi_b = X[c]["gi"]
                    S.op("act", lambda e, xc=xc: e.activation(out=xc[:, :n], in_=xc[:, :n], func=AF.Sqrt), reads=[xc_b],
                         writes=[xc_b])
                    if first_tile:
                        S.op("dve", lambda e, xc=xc: e.scalar_tensor_tensor(
                            out=xc[:, 0:1], in0=xc[:, 0:1], scalar=vecs[:, vo + V_OMF0:vo + V_OMF0 + 1],
                            in1=vecs[:, vo + V_F0:vo + V_F0 + 1], op0=ALU.mult, op1=ALU.add), reads=[xc_b, Bvecs],
                            writes=[xc_b])
                    S.op("pool", lambda e, xc=xc, gi=gi: e.tensor_tensor(out=xc[:, :n], in0=xc[:, :n], in1=gi[:, :n],
                                                                        op=ALU.mult), reads=[xc_b, gi_b], writes=[xc_b])
                for c in cs:
                    r, r_b = X[c]["r"]
                    xc, xc_b = X[c]["xc"]
                    S.dma("pool", "sv_r%d" % (c % NB), rg_d[:, 0, c, t0_:t0_ + n], r[:, :n], reads=[r_b],
                          writes=[Brg[ti][c][0]], nosuffix=True)
                    S.dma("pool", "sv_g%d" % (c % NB), rg_d[:, 1, c, t0_:t0_ + n], xc[:, :n], reads=[xc_b],
                          writes=[Brg[ti][c][1]], nosuffix=True)
            for c in cs:
                r, r_b = X[c]["r"]
                xc, xc_b = X[c]["xc"]
                T2, T2_b = X[c]["T2"]
                S.op("dve", lambda e, T2=T2, r=r, xc=xc, c=c: e.tensor_tensor_scan(
                    out=T2[:, :n], data0=r[:, :n], data1=xc[:, :n], initial=state[:, c:c + 1], op0=ALU.mult,
                    op1=ALU.add), reads=[r_b, xc_b, Bstate, T2_b], writes=[T2_b])
                S.op("dve", lambda e, T2=T2, c=c: e.tensor_copy(out=state[:, c:c + 1], in_=T2[:, n - 1:n]),
                     reads=[T2_b], writes=[Bstate])
            if want_out:
                for c in cs:
                    T1, T1_b = X[c]["T1"]

                    def cons_yr(j, ps, ps_b, T1=T1, T1_b=T1_b):
                        S.op("act", lambda e: e.activation(out=T1[:, :n], in_=ps[:, :n], func=AF.Gelu_apprx_tanh),
                             reads=[ps_b, T1_b], writes=[T1_b])
                    linear(L["w_in"], D, O_YR + c * 128, 128, hsrc, [BhT], n, cons_yr)
                for c in cs:
                    T1, T1_b = X[c]["T1"]
                    T2, T2_b = X[c]["T2"]
                    S.op("dve", lambda e, T2=T2, T1=T1, c=c: e.tensor_tensor(out=lruT[:, c, :n], in0=T2[:, :n],
                                                                          in1=T1[:, :n], op=ALU.mult),
                         reads=[T2_b, T1_b], writes=[BlruT])

    S.dma("sp", "vecs", vecs, vecs_in, writes=[Bvecs])
    S.op("dve", lambda e: e.memset(ones32, 1.0 / D), writes=[Bones32])
    S.op("dve", lambda e: e.memset(onesb, 1.0), writes=[Bonesb])
    S.dma("sp", "biasT", biasT, bias_in, writes=[BbiasT])

    def layer_prelude(l):
        vo = L["vo"]
        st, st_b = stg.next()
        stv = st[:, :2048].rearrange("p (k n) -> p k n", k=16)
        S.dma("sp", st_b.name, stv, wbd_all[l], writes=[st_b])
        S.op("dve", lambda e, stv=stv: e.tensor_copy(out=wbd, in_=stv), reads=[st_b], writes=[Bwbd])
        e_ = tmpv[:, 0, :]
        acc = tmpv[:, 1, :]
        S.op("act", lambda e: e.activation(out=e_, in_=vecs[:, vo + V_LAM:vo + V_LAM + 8], func=AF.Exp, scale=-1.0),
             reads=[Bvecs], writes=[Btmpv])
        S.op("dve", lambda e: e.tensor_scalar(out=acc, in0=e_, scalar1=-1.0 / 6.0, scalar2=1.0 / 5.0, op0=ALU.mult,
                                              op1=ALU.add), reads=[Btmpv], writes=[Btmpv])
        for coef in (-1.0 / 4.0, 1.0 / 3.0, -1.0 / 2.0, 1.0):
            S.op("dve", lambda e: e.tensor_tensor(out=acc, in0=acc, in1=e_, op=ALU.mult), reads=[Btmpv], writes=[Btmpv])
            if coef < 0:
                S.op("dve", lambda e, coef=coef: e.tensor_scalar(out=acc, in0=acc, scalar1=-1.0, scalar2=-coef,
                                                                 op0=ALU.mult, op1=ALU.add), reads=[Btmpv], writes=[Btmpv])
            else:
                S.op("dve", lambda e, coef=coef: e.tensor_scalar(out=acc, in0=acc, scalar1=-1.0, scalar2=coef,
                                                                 op0=ALU.mult, op1=ALU.add), reads=[Btmpv], writes=[Btmpv])
        S.op("dve", lambda e: e.tensor_tensor(out=acc, in0=acc, in1=e_, op=ALU.mult), reads=[Btmpv], writes=[Btmpv])
        for k_, mul_ in enumerate((-4.0, -8.0, -16.0, 8.0)):
            S.op("dve", lambda e, k_=k_, mul_=mul_: e.tensor_scalar(out=cf[:, k_ * 8:(k_ + 1) * 8], in0=acc, scalar1=mul_,
                                                                    scalar2=None, op0=ALU.mult), reads=[Btmpv], writes=[Bcf])

    def inproj_xr_hist(n):
        for c in range(8):
            def cons(j, ps, ps_b, c=c):
                S.op("act", lambda e: e.copy(out=hist[:, c, 0:3], in_=ps[:, n - 3:n]), reads=[ps_b], writes=[Bhist])
            linear(L["w_in"], D, O_XR + c * 128, 128, lambda kc: hT[:, kc, :n], [BhT], n, cons)

    def inproj_v(n, blk0):
        st, st_b = stg.next()
        wb_, wb_b = wbr.next()
        stv = st[:, :1024].rearrange("p (k n) -> p k n", k=8)
        wb3 = wb_[:, :1024].rearrange("p (k n) -> p k n", k=8)
        S.dma("sp", st_b.name, stv, L["w_in"].rearrange("(kc p) n -> p kc n", p=128)[:, :, O_V:O_V + 128], writes=[st_b])
        S.op("pool", lambda e, wb_=wb_, st=st: e.tensor_copy(out=wb_[:, :1024], in_=st[:, :1024]), reads=[st_b], writes=[wb_b])
        for i in range(n // 128):
            ps, ps_b = psum()

            def mmv(e, ps=ps, i=i):
                ins = None
                for kc in range(8):
                    ins = e.matmul(ps[:, :128], lhsT=hT[:, kc, i * 128:(i + 1) * 128], rhs=wb3[:, kc, :],
                                   start=(kc == 0), stop=(kc == 7))
                return ins
            S.op("pe", mmv, reads=[BhT, wb_b], writes=[ps_b])
            S.op("act", lambda e, ps=ps, i=i: e.copy(out=vtm[:, blk0 + i, :], in_=ps[:, :128]), reads=[ps_b],
                 writes=[Bvtm])

    def p23_prelude(l):
        sink32, Bsink32 = tB.next()
        for kh_ in range(2):
            S.dma("sp", "sink32", sink32[kh_ * 32:kh_ * 32 + 1, :], sinks_all[l][0:1, kh_ * 512:(kh_ + 1) * 512],
                  writes=[Bsink32])
        for kh_ in range(2):
            S.op("act", lambda e, kh_=kh_, sink32=sink32: e.activation(
                out=esink[kh_ * 32:kh_ * 32 + 1, :], in_=sink32[kh_ * 32:kh_ * 32 + 1, :], func=AF.Exp),
                reads=[Bsink32], writes=[Besink])
        S.dma("sp", "xt", xt[:, :, :256], memT_in, writes=[Bxt])
        rmsnorm(xt, Bxt, 256, V_GMEM, hT, BhT)

        def cons_km(j, ps, ps_b):
            S.op("act", lambda e: e.copy(out=kmT[:, j, :], in_=ps[:, :256]), reads=[ps_b], writes=[BkmT])
        linear(w_memkv_all[l], D, 0, 512, lambda kc: hT[:, kc, :256], [BhT], 256, cons_km)
        st, st_b = stg.next()
        wbv_, wbv_b = wbr.next()
        stv = st.rearrange("p (k n) -> p k n", k=8)
        wv3 = wbv_.rearrange("p (k n) -> p k n", k=8)
        S.dma("sp", st_b.name, stv, w_memkv_all[l].rearrange("(kc p) n -> p kc n", p=128)[:, :, 512:1024], writes=[st_b])
        S.op("pool", lambda e, wbv_=wbv_, st=st: e.tensor_copy(out=wbv_, in_=st), reads=[st_b], writes=[wbv_b])
        for mb in range(2):
            ps, ps_b = psum()

            def mmv(e, ps=ps, mb=mb):
                ins = None
                for kc in range(8):
                    ins = e.matmul(ps, lhsT=hT[:, kc, mb * 128:(mb + 1) * 128], rhs=wv3[:, kc, :], start=(kc == 0),
                                   stop=(kc == 7))
                return ins
            S.op("pe", mmv, reads=[BhT, wbv_b], writes=[ps_b])
            S.op("act", lambda e, ps=ps, mb=mb: e.copy(out=vm[:, mb, :], in_=ps), reads=[ps_b], writes=[Bvm])


    def final_norm_store(t0):
        vo = L["vo"]
        rstd, Brstd = tB.next()
        for c in range(8):
            S.op("act", lambda e, c=c: e.activation(out=big32[:, c, :], in_=xt[:, c, :], func=AF.Square),
                 reads=[Bxt], writes=[Bbig])
        ps, ps_b = psum()

        def mmf(e, ps=ps):
            ins = None
            for c in range(8):
                ins = e.matmul(ps, lhsT=ones32, rhs=big32[:, c, :], start=(c == 0), stop=(c == 7))
            return ins
        S.op("pe", mmf, reads=[Bbig, Bones32], writes=[ps_b])
        S.op("act", lambda e, ps=ps: e.activation(out=rstd, in_=ps, func=AF.Sqrt, bias=EPS, scale=1.0), reads=[ps_b],
             writes=[Brstd])
        S.op("dve", lambda e: e.reciprocal(out=rstd, in_=rstd), reads=[Brstd], writes=[Brstd])
        for c in range(8):
            S.op("dve", lambda e, c=c: e.scalar_tensor_tensor(
                out=big32[:, c, :], in0=xt[:, c, :], scalar=vecs[:, vo + V_GFIN + c:vo + V_GFIN + c + 1], in1=rstd,
                op0=ALU.mult, op1=ALU.mult), reads=[Bxt, Brstd, Bvecs, Bbig], writes=[Bbig])
        S.dma("sp", "y_out", y_out[:, :, t0:t0 + T], big32, reads=[Bbig], writes=[Byo])

    def p23_phase(l, last):
        vo = L["vo"]
        HN = 128
        S.dma("sp", "xt", xt[:, :, :HN].rearrange("p c n -> p (c n)") if False else xt[:, :, :HN],
              cc1_out[l].ap()[0:128, 0:1024].rearrange("p (c n) -> p c n", c=8), reads=[Bcc1o[l]], writes=[Bxt])
        S.op("dve", lambda e: e.tensor_scalar(out=xt[:, :, :HN], in0=xt[:, :, :HN], scalar1=vecs[:, V_OMF0:V_OMF0 + 1],
                                              scalar2=None, op0=ALU.mult), reads=[Bxt, Bvecs], writes=[Bxt])
        S.dma("sp", "state", state, cc1_out[l].ap()[0:128, 1024:1032], reads=[Bcc1o[l]], writes=[Bstate])
        S.op("dve", lambda e: e.tensor_scalar(out=state, in0=state, scalar1=vecs[:, V_OMF0:V_OMF0 + 1],
                                              scalar2=None, op0=ALU.mult), reads=[Bstate, Bvecs], writes=[Bstate])
        rmsnorm(xt, Bxt, HN, V_GM, hT, BhT)

        def cons_k0(j, ps, ps_b):
            S.op("act", lambda e: e.copy(out=kT[:, 0:128], in_=ps[:, :128]), reads=[ps_b], writes=[BkT])
        linear(L["w_in"], D, O_K, 128, lambda kc: hT[:, kc, :HN], [BhT], HN, cons_k0)
        inproj_v(HN, 0)

        for ti in range(ntiles):
            t0 = ti * T
            n = T
            S.dma("sp", "hT", hT, hsc[:, :, t0:t0 + T], reads=[Bhsc[ti]], writes=[BhT])
            rh = lambda kc: hT[:, kc, :n]

            def cons_q(j, ps, ps_b):
                S.op("act", lambda e: e.copy(out=qT[:, j, :], in_=ps), reads=[ps_b], writes=[BqT])
            linear(L["w_in"], D, O_Q, 512, rh, [BhT], n, cons_q)

            def cons_k(j, ps, ps_b):
                S.op("act", lambda e: e.copy(out=kT[:, 128:128 + T], in_=ps), reads=[ps_b], writes=[BkT])
            linear(L["w_in"], D, O_K, 128, rh, [BhT], n, cons_k)
            inproj_v(T, 1)

            def cons_cq(j, ps, ps_b):
                S.op("act", lambda e: e.copy(out=cqT[:, j, :], in_=ps), reads=[ps_b], writes=[BcqT])
            linear(L["w_in"], D, O_CQ, 512, rh, [BhT], n, cons_cq)

            for bi in range(4):
                for kh in range(2):
                    r0 = kh * 64
                    pt, pt_b = pT.next()
                    for kb in range(2):
                        ps, ps_b = psum()
                        kcol = (bi + kb) * 128
                        S.op("pe", lambda e, ps=ps, kcol=kcol, r0=r0, bi=bi: e.matmul(
                            ps, lhsT=kT[r0:r0 + 64, kcol:kcol + 128], rhs=qT[r0:r0 + 64, :, bi * 128:(bi + 1) * 128],
                            start=True, stop=True), reads=[BkT, BqT], writes=[ps_b])
                        sc, sc_b = tB.next()
                        bsrc, bsrc_b = biasT[:, kh, kb], BbiasT
                        S.op("dve", lambda e, sc=sc, ps=ps, bsrc=bsrc: e.scalar_tensor_tensor(
                            out=sc.rearrange("p (g q) -> p g q", g=4), in0=ps.rearrange("p (g q) -> p g q", g=4), scalar=0.125,
                            in1=bsrc, op0=ALU.mult, op1=ALU.add), reads=[ps_b, bsrc_b], writes=[sc_b])
                        if ti == 0 and bi == 0 and kb == 0:
                            S.op("dve", lambda e, sc=sc: e.tensor_scalar(out=sc, in0=sc, scalar1=vecs[:, V_MASKA:V_MASKA + 1],
                                                                        scalar2=None, op0=ALU.add),
                                 reads=[sc_b, Bvecs], writes=[sc_b])
                        S.op("act", lambda e, sc=sc, pt=pt, kb=kb: e.activation(out=pt[:, kb, :], in_=sc, func=AF.Exp),
                             reads=[sc_b], writes=[pt_b])
                    psn, psn_b = psum()

                    def mmn(e, psn=psn, pt=pt, kh=kh, bi=bi):
                        ins = None
                        for kb in range(2):
                            ins = e.matmul(psn[0:64, :], lhsT=vtm[:, bi + kb, kh * 64:(kh + 1) * 64], rhs=pt[:, kb, :],
                                           start=(kb == 0), stop=(kb == 1))
                        return ins
                    S.op("pe", mmn, reads=[Bvtm, pt_b], writes=[psn_b])
                    psd, psd_b = psum()

                    def mmd(e, psd=psd, pt=pt, kh=kh):
                        for kb in range(2):
                            e.matmul(psd[0:64, :], lhsT=onesb[:, 0:64], rhs=pt[:, kb, :], start=(kb == 0), stop=False)
                        return e.matmul(psd[0:64, :], lhsT=onesb[kh * 32:kh * 32 + 1, 0:64], rhs=esink[kh * 32:kh * 32 + 1, :],
                                        start=False, stop=True)
                    S.op("pe", mmd, reads=[Bonesb, pt_b, Besink], writes=[psd_b])
                    rd, rd_b = tB.next()
                    S.op("dve", lambda e, rd=rd, psd=psd: e.reciprocal(out=rd[0:64, :], in_=psd[0:64, :]), reads=[psd_b],
                         writes=[rd_b])
                    S.op("dve", lambda e, rd=rd, psn=psn, kh=kh, bi=bi: e.tensor_tensor(
                        out=attnT[:, kh * 4:(kh + 1) * 4, bi * 128:(bi + 1) * 128],
                        in0=psn[0:64, :].rearrange("p (g q) -> p g q", g=4), in1=rd[0:64, :].rearrange("p (g q) -> p g q", g=4),
                        op=ALU.mult), reads=[psn_b, rd_b], writes=[BattnT])
            S.op("pool", lambda e: e.tensor_copy(out=kT[:, 0:128], in_=kT[:, T:T + 128]), reads=[BkT], writes=[BkT])
            S.op("pool", lambda e: e.tensor_copy(out=vtm[:, 0, :], in_=vtm[:, 4, :]), reads=[Bvtm], writes=[Bvtm])

            for hd in range(4):
                pt, pt_b = pT.next()
                for mb in range(2):
                    ps, ps_b = psum()
                    S.op("pe", lambda e, ps=ps, hd=hd, mb=mb: e.matmul(ps, lhsT=kmT[:, hd, mb * 128:(mb + 1) * 128],
                                                                    rhs=cqT[:, hd, :], start=True, stop=True),
                         reads=[BkmT, BcqT], writes=[ps_b])
                    S.op("act", lambda e, ps=ps, pt=pt, mb=mb: e.activation(out=pt[:, mb, :], in_=ps, func=AF.Exp,
                                                                         scale=128.0 ** -0.5), reads=[ps_b], writes=[pt_b])
                psn, psn_b = psum()

                def mmn2(e, psn=psn, pt=pt, hd=hd):
                    ins = None
                    for mb in range(2):
                        ins = e.matmul(psn, lhsT=vm[:, mb, hd * 128:(hd + 1) * 128], rhs=pt[:, mb, :], start=(mb == 0),
                                       stop=(mb == 1))
                    return ins
                S.op("pe", mmn2, reads=[Bvm, pt_b], writes=[psn_b])
                psd, psd_b = psum()

                def mmd2(e, psd=psd, pt=pt):
                    ins = None
                    for mb in range(2):
                        ins = e.matmul(psd, lhsT=onesb, rhs=pt[:, mb, :], start=(mb == 0), stop=(mb == 1))
                    return ins
                S.op("pe", mmd2, reads=[Bonesb, pt_b], writes=[psd_b])
                rd, rd_b = tB.next()
                S.op("dve", lambda e, rd=rd, psd=psd: e.reciprocal(out=rd, in_=psd), reads=[psd_b], writes=[rd_b])
                S.op("dve", lambda e, rd=rd, psn=psn, hd=hd: e.tensor_tensor(out=memoT[:, hd, :], in0=psn, in1=rd, op=ALU.mult),
                     reads=[psn_b, rd_b], writes=[BmemoT])

            S.dma("sp", "xt", xt, xres[:, :, t0:t0 + T], reads=[Bxres[ti]], writes=[Bxt])
            lru(T, ti == 0, ti)

            branches = [
                (w_attn_all[l], 512, 64, lambda kc: attnT[:, kc, :], BattnT),
                (w_lru_all[l], D, 128, lambda kc: lruT[:, kc, :], BlruT),
                (w_memo_all[l], 512, 128, lambda kc: memoT[:, kc, :], BmemoT),
            ]
            for b, (Wb, Kb, kpb, rb, Brb) in enumerate(branches):
                sg_sb = {}
                for g0 in range(0, 8, 4):
                    def cons_g(j, ps, ps_b, b=b, g0=g0):
                        t, t_b = tB.next()
                        jj = g0 + j
                        S.op("act", lambda e, t=t, ps=ps: e.activation(
                            out=t, in_=ps, func=AF.Sigmoid, bias=vecs[:, vo + V_GB + b * 8 + jj:vo + V_GB + b * 8 + jj + 1]),
                            reads=[ps_b, Bvecs], writes=[t_b])
                        sg_sb[jj] = (t, t_b)
                    linear(L["w_in"], D, O_G + b * 1024 + g0 * 128, 512, rh, [BhT], n, cons_g)

                    def cons_b(j, ps, ps_b, b=b, g0=g0):
                        jj = g0 + j
                        t, t_b = sg_sb[jj]
                        if b == 0:
                            S.op("dve", lambda e, t=t, ps=ps: e.tensor_tensor(out=big32[:, jj, :], in0=t, in1=ps, op=ALU.mult),
                                 reads=[t_b, ps_b], writes=[Bbig])
                        else:
                            S.op("dve", lambda e, t=t, ps=ps: e.tensor_tensor(out=t, in0=t, in1=ps, op=ALU.mult),
                                 reads=[t_b, ps_b], writes=[t_b])
                            if b == 1:
                                S.op("pool", lambda e, t=t: e.tensor_tensor(out=big32[:, jj, :], in0=big32[:, jj, :], in1=t,
                                                                            op=ALU.add), reads=[t_b, Bbig], writes=[Bbig])
                            else:
                                S.op("pool", lambda e, t=t: e.tensor_tensor(out=merged[:, jj, :], in0=big32[:, jj, :], in1=t,
                                                                            op=ALU.add), reads=[t_b, Bbig], writes=[Bmerged])
                    linear(Wb, Kb, g0 * 128, 512, rb, [Brb], n, cons_b, kp=kpb)

            def cons_o(j, ps, ps_b):
                S.op("dve", lambda e, ps=ps: e.tensor_tensor(out=xt[:, j, :], in0=xt[:, j, :], in1=ps, op=ALU.add),
                     reads=[ps_b, Bxt], writes=[Bxt])
            linear(w_out_all[l], D, 0, D, lambda kc: merged[:, kc, :], [Bmerged], n, cons_o)

            ffn(T, V_G2)
            if not last:
                S.dma("sp", "xres", xres[:, :, t0:t0 + T], xt, reads=[Bxt], writes=[Bxres[ti]])
                rmsnorm(xt, Bxt, T, NV + V_G1, actT, Bact)
                S.dma("pool", "h1sv", h1sc[:, :, t0:t0 + T], actT[:, 0:8, :], reads=[Bact], writes=[Bh1sc[ti]])
            else:
                final_norm_store(t0)

    Bcc1i = [Buf("cc1i%d" % l) for l in range(depth)]
    Bcc1o = [Buf("cc1o%d" % l) for l in range(depth)]
    Bcc2i = [Buf("cc2i%d" % l) for l in range(depth)]
    Bcc2o = [Buf("cc2o%d" % l) for l in range(depth)]
    Byo = Buf("y_out")

    def p1_phase(l):
        S.op("dve", lambda e: e.memset(state, 0.0), writes=[Bstate])
        if l == 0:
            S.dma("sp", "xt", xt[:, :, :4], xh4_in, writes=[Bxt])
        else:
            S.dma("sp", "xt", xt[:, :, :4], cc2_out[l - 1].ap()[0:128, :].rearrange("p (c n) -> p c n", c=8),
                  reads=[Bcc2o[l - 1]], writes=[Bxt])
            S.op("dve", lambda e: e.tensor_scalar(out=xt[:, :, :4], in0=xt[:, :, :4], scalar1=vecs[:, V_OMF0:V_OMF0 + 1],
                                                  scalar2=None, op0=ALU.mult), reads=[Bxt, Bvecs], writes=[Bxt])
        ffn(4, V_G1)
        rmsnorm(xt, Bxt, 4, V_GM, hT, BhT)
        inproj_xr_hist(4)
        for ti in range(ntiles):
            t0 = ti * T
            if l == 0:
                S.dma("sp", "xt", xt, x_in[:, :, t0:t0 + T], writes=[Bxt])
            else:
                S.dma("sp", "xt", xt, xres[:, :, t0:t0 + T], reads=[Bxres[ti]], writes=[Bxt])
            ffn(T, V_G1, None if l == 0 else (h1sc[:, :, t0:t0 + T], Bh1sc[ti]))
            S.dma("sp", "xres", xres[:, :, t0:t0 + T], xt, reads=[Bxt], writes=[Bxres[ti]])
            rmsnorm(xt, Bxt, T, V_GM, hT, BhT)
            S.dma("pool", "hsv", hsc[:, :, t0:t0 + T], hT, reads=[BhT], writes=[Bhsc[ti]])
            lru(T, ti == 0, ti)

    def exchange1(l):
        S.dma("sp", "cc1i", cc1_in[l].ap()[:, 0:1024].rearrange("p (c n) -> p c n", c=8), xres[:, :, NT - 128:NT],
              reads=[Bxres[ntiles - 1]], writes=[Bcc1i[l]])
        S.dma("sp", "cc1i", cc1_in[l].ap()[:, 1024:1032], state, reads=[Bstate], writes=[Bcc1i[l]])
        S.coll("cc", lambda e: e.collective_compute("AllGather", ALU.bypass, replica_groups=RG,
                                                    ins=[cc1_in[l].ap().opt()], outs=[cc1_out[l].ap().opt()]),
               reads=[Bcc1i[l]], writes=[Bcc1o[l]])

    def exchange2(l):
        S.dma("sp", "cc2i", cc2_in[l].ap().rearrange("p (c n) -> p c n", c=8), xres[:, :, NT - 4:NT],
              reads=[Bxres[ntiles - 1]], writes=[Bcc2i[l]])
        S.coll("cc", lambda e: e.collective_compute("AllGather", ALU.bypass, replica_groups=RG,
                                                    ins=[cc2_in[l].ap().opt()], outs=[cc2_out[l].ap().opt()]),
               reads=[Bcc2i[l]], writes=[Bcc2o[l]])

    for l in range(depth):
        S.suffix = "_L%d" % l
        L["vo"] = l * NV
        L["w_in"] = w_in_all[l]
        layer_prelude(l)
        L["p23"] = False
        L["wg"], L["wu"], L["wd"] = f1[0][l], f1[1][l], f1[2][l]
        p1_phase(l)
        exchange1(l)
        L["p23"] = True
        L["wg"], L["wu"], L["wd"] = f2[0][l], f2[1][l], f2[2][l]
        p23_prelude(l)
        p23_phase(l, l == depth - 1)
        if l < depth - 1:
            exchange2(l)
    S.finish([Byo])
    S.emit()
    return nc


def _fm(v):
    return np.ascontiguousarray(np.asarray(v, np.float32).reshape(-1, 128).T)


def _to_fm(xtok):
    t = xtok.shape[0]
    return np.ascontiguousarray(xtok.T.reshape(8, 128, t).transpose(1, 0, 2))


def _from_fm(xfm):
    t = xfm.shape[2]
    return np.ascontiguousarray(xfm.transpose(1, 0, 2).reshape(1024, t).T)


def _alibi_tables():
    slopes = np.array([2.0 ** (-8.0 * (i + 1) / 8) for i in range(8)], dtype=np.float32)
    j = np.arange(128)[:, None]
    i = np.arange(128)[None, :]
    tab = np.zeros((128, 2, 2, 4, 128), np.float32)
    for kh in range(2):
        for kb in range(2):
            kpos = kb * 128 + j
            qpos = 128 + i
            dist = np.abs(qpos - kpos).astype(np.float32)
            kch = kpos // 64
            qch = qpos // 64
            valid = (kch >= qch - 2) & (kch <= qch)
            for g in range(4):
                tab[:, kh, kb, g, :] = np.where(valid, -slopes[kh * 4 + g] * dist, np.float32(NEG))
    return tab


_PROGS = {}


def _prog(depth):
    if depth not in _PROGS:
        _PROGS[depth] = build(depth)
    return _PROGS[depth]


def _prep_inputs(x, mem, ffn1_norm, ffn1_w_gate, ffn1_w_up, ffn1_w_down, mix_norm, w_in, gate_bias,
                 attn_sinks, w_attn_out, conv_w, conv_b, lru_wa, lru_ba, lru_wx, lru_bx, lru_lambda,
                 w_lru_out, mem_norm, w_mem_kv, w_mem_out, w_out, ffn2_norm, ffn2_w_gate, ffn2_w_up,
                 ffn2_w_down, final_norm, depth=None):
    f32 = np.float32
    x = np.asarray(x, f32)
    mem = np.asarray(mem, f32)
    if depth is None:
        depth = ffn1_norm.shape[0]
    ncores = 8
    ca = lambda a: np.ascontiguousarray(np.asarray(a, f32)[:depth])
    tab = _alibi_tables()
    tabF_A = tab[:, :, 0].copy()
    tabF_A[...] = NEG
    tabF_B = np.ascontiguousarray(tab[:, :, 0])
    qperm = np.concatenate([np.concatenate([np.arange(j * 64, j * 64 + 64), np.arange((4 + j) * 64, (4 + j) * 64 + 64)])
                            for j in range(4)])
    win = np.array(np.asarray(w_in, f32)[:depth], copy=True)
    win[:, :, :512] = win[:, :, qperm]
    win = np.ascontiguousarray(win)
    wbd = np.zeros((depth, 128, 16, 128), f32)
    for l in range(depth):
        for gi_, wsrc in enumerate((lru_wa[l], lru_wx[l])):
            for c in range(8):
                wbd[l, 0:64, gi_ * 8 + c, 0:64] = wsrc[2 * c]
                wbd[l, 64:128, gi_ * 8 + c, 64:128] = wsrc[2 * c + 1]
    sinks = np.ascontiguousarray(np.repeat(np.asarray(attn_sinks, f32)[:depth], 128, axis=1)[:, None, :])
    shared = {"w_in": win, "wbd": wbd,
              "f1_wg": ca(ffn1_w_gate), "f1_wu": ca(ffn1_w_up), "f1_wd": ca(ffn1_w_down),
              "f2_wg": ca(ffn2_w_gate), "f2_wu": ca(ffn2_w_up), "f2_wd": ca(ffn2_w_down),
              "biasT": tab, "sinks": sinks, "w_attn": ca(w_attn_out), "w_lru": ca(w_lru_out),
              "w_memkv": ca(w_mem_kv), "w_memo": ca(w_mem_out), "w_out": ca(w_out)}
    in_maps = []
    for c in range(ncores):
        b, half = c // 2, c % 2
        v = np.zeros((128, depth * NV), f32)
        for l in range(depth):
            o = l * NV
            v[:, o + V_G1:o + V_G1 + 8] = _fm(ffn1_norm[l])
            v[:, o + V_GM:o + V_GM + 8] = _fm(mix_norm[l])
            v[:, o + V_G2:o + V_G2 + 8] = _fm(ffn2_norm[l])
            for j in range(4):
                v[:, o + V_CW + j * 8:o + V_CW + j * 8 + 8] = _fm(conv_w[l, j])
            v[:, o + V_CB:o + V_CB + 8] = _fm(conv_b[l])
            v[:, o + V_BA:o + V_BA + 8] = _fm(lru_ba[l])
            v[:, o + V_BX:o + V_BX + 8] = _fm(lru_bx[l])
            v[:, o + V_LAM:o + V_LAM + 8] = _fm(lru_lambda[l])
            v[:, o + V_GB:o + V_GB + 24] = _fm(gate_bias[l])
            v[:, o + V_GMEM:o + V_GMEM + 8] = _fm(mem_norm[l])
            v[:, o + V_GFIN:o + V_GFIN + 8] = _fm(final_norm)
            v[:, o + V_F0] = 1.0 if half == 0 else 0.0
            v[:, o + V_OMF0] = 0.0 if half == 0 else 1.0
            v[:, o + V_MASKA] = NEG if half == 0 else 0.0
        xs = _to_fm(x[b, half * NT:(half + 1) * NT, :])
        if half == 0:
            xh4 = np.zeros((128, 8, 4), f32)
        else:
            xh4 = _to_fm(x[b, NT - 4:NT, :])
        m = dict(shared)
        m.update({"x_in": xs, "xh4_in": xh4, "vecs": v,
                  "memT": _to_fm(mem[b])})
        in_maps.append(m)
    return in_maps, depth


def kernel(x, mem, ffn1_norm, ffn1_w_gate, ffn1_w_up, ffn1_w_down, mix_norm, w_in, gate_bias,
           attn_sinks, w_attn_out, conv_w, conv_b, lru_wa, lru_ba, lru_wx, lru_bx, lru_lambda,
           w_lru_out, mem_norm, w_mem_kv, w_mem_out, w_out, ffn2_norm, ffn2_w_gate, ffn2_w_up,
           ffn2_w_down, final_norm):
    in_maps, depth = _prep_inputs(x, mem, ffn1_norm, ffn1_w_gate, ffn1_w_up, ffn1_w_down, mix_norm, w_in,
                                  gate_bias, attn_sinks, w_attn_out, conv_w, conv_b, lru_wa, lru_ba, lru_wx,
                                  lru_bx, lru_lambda, w_lru_out, mem_norm, w_mem_kv, w_mem_out, w_out,
                                  ffn2_norm, ffn2_w_gate, ffn2_w_up, ffn2_w_down, final_norm)
    ncores = 8
    res = run_bass_kernel_spmd(_prog(depth), in_maps, core_ids=list(range(ncores))).results
    B = np.asarray(x).shape[0]
    out = np.empty((B, 2 * NT, D), np.float32)
    for c in range(ncores):
        out[c // 2, (c % 2) * NT:(c % 2 + 1) * NT, :] = _from_fm(np.asarray(res[c]["y_out"], np.float32))
    return out
```

```python
import numpy as np
import concourse.bass as bass
import concourse.mybir as mybir
from concourse.bass_utils import run_bass_kernel_spmd

F32 = mybir.dt.float32
BF16 = mybir.dt.bfloat16
AF = mybir.ActivationFunctionType
ALU = mybir.AluOpType

D = 1024
NT = 4096
T = 512
DFF = 2816
INW = 6400
O_Q, O_K, O_V, O_XR, O_YR, O_CQ, O_G = 0, 512, 640, 768, 1792, 2816, 3328
NV = 136
V_G1, V_GM, V_G2, V_CW, V_CB, V_BA, V_BX, V_LAM, V_GB, V_GMEM, V_GFIN, V_F0, V_OMF0, V_MASKA = \
    0, 8, 16, 24, 56, 64, 72, 80, 88, 112, 120, 128, 129, 130
EPS = 1e-6
NEG = -30000.0

ENGS = ("pe", "act", "dve", "pool", "sp")


class Buf:
    __slots__ = ("name", "w", "r")

    def __init__(self, name):
        self.name = name
        self.w = None
        self.r = {}


class Sched:
    def __init__(self, nc):
        self.nc = nc
        self.ops = {e: [] for e in ENGS}
        self.cnt = {e: 0 for e in ENGS}
        self.sem = {e: nc.alloc_semaphore("s_" + e) for e in ENGS}
        self.dsem = {}
        self.dcnt = {}
        self.waited = {}
        self.suffix = ""

    def _need(self, eng, dep, waits):
        if dep is None:
            return
        kind, key, val = dep
        if kind == "eng" and key == "pe" and eng == "pe":
            return
        if self.waited.get((eng, kind, key), 0) >= val:
            return
        if val > waits.get((kind, key), 0):
            waits[(kind, key)] = val

    def _collect(self, eng, reads, writes):
        waits = {}
        for b in reads:
            self._need(eng, b.w, waits)
        for b in writes:
            self._need(eng, b.w, waits)
            for d in b.r.values():
                self._need(eng, d, waits)
        for (kind, key), val in waits.items():
            self.waited[(eng, kind, key)] = val
        return waits

    def _mark(self, me, reads, writes):
        for b in reads:
            b.r[(me[0], me[1])] = me
        for b in writes:
            b.w = me
            b.r = {}

    def _emit_waits(self, e, waits):
        for (kind, key), val in waits.items():
            e.wait_ge(self.sem[key] if kind == "eng" else self.dsem[key], val)

    def op(self, eng, fn, reads=(), writes=()):
        waits = self._collect(eng, reads, writes)
        self.cnt[eng] += 1
        self._mark(("eng", eng, self.cnt[eng]), reads, writes)
        sem = self.sem[eng]

        def run(e, fn=fn, waits=waits, sem=sem):
            self._emit_waits(e, waits)
            fn(e).then_inc(sem, 1)
        self.ops[eng].append(run)

    def coll(self, key, fn, reads=(), writes=()):
        key = key + self.suffix
        if key not in self.dsem:
            self.dsem[key] = self.nc.alloc_semaphore("c_" + key)
            self.dcnt[key] = 0
        waits = self._collect("pool", reads, writes)
        self.dcnt[key] += 1
        self._mark(("dma", key, self.dcnt[key]), reads, writes)
        sem = self.dsem[key]

        def run(e, waits=waits, sem=sem, fn=fn):
            self._emit_waits(e, waits)
            fn(e).then_inc(sem, 1)
        self.ops["pool"].append(run)

    def dma(self, queue, key, out, in_, reads=(), writes=(), nosuffix=False):
        if not nosuffix:
            key = key + self.suffix
        if key not in self.dsem:
            self.dsem[key] = self.nc.alloc_semaphore("d_" + key)
            self.dcnt[key] = 0
        waits = self._collect(queue, reads, writes)
        self.dcnt[key] += 16
        self._mark(("dma", key, self.dcnt[key]), reads, writes)
        sem = self.dsem[key]

        def run(e, waits=waits, sem=sem, out=out, in_=in_):
            self._emit_waits(e, waits)
            e.dma_start(out=out, in_=in_).then_inc(sem, 16)
        self.ops[queue].append(run)

    def finish(self, final_bufs):
        waits = self._collect("sp", final_bufs, ())
        self.ops["sp"].append(lambda e, waits=waits: self._emit_waits(e, waits))

    def emit(self):
        nc = self.nc
        with nc.Block() as block:
            @block.tensor
            def _(e):
                for f in self.ops["pe"]:
                    f(e)

            @block.scalar
            def _(e):
                for f in self.ops["act"]:
                    f(e)

            @block.vector
            def _(e):
                for f in self.ops["dve"]:
                    f(e)

            @block.gpsimd
            def _(e):
                for f in self.ops["pool"]:
                    f(e)

            @block.sync
            def _(e):
                for f in self.ops["sp"]:
                    f(e)


class Rot:
    def __init__(self, nc, name, shape, dtype, n):
        self.items = []
        for i in range(n):
            nm = "%s%d" % (name, i)
            self.items.append((nc.alloc_sbuf_tensor(nm, shape, dtype).ap(), Buf(nm)))
        self.i = 0

    def next(self):
        it = self.items[self.i % len(self.items)]
        self.i += 1
        return it


def build(depth=4, ntiles=NT // T):
    nc = bass.Bass("TRN2", target_bir_lowering=False)
    S = Sched(nc)
    L = {"vo": 0, "p23": False}

    def din(name, shape):
        return nc.dram_tensor(name, shape, F32, kind="ExternalInput").ap()

    def dout(name, shape):
        return nc.dram_tensor(name, shape, F32, kind="ExternalOutput").ap()

    x_in = din("x_in", [128, 8, NT])
    xh4_in = din("xh4_in", [128, 8, 4])
    vecs_in = din("vecs", [128, depth * NV])
    w_in_all = din("w_in", [depth, D, INW])
    wbd_all = din("wbd", [depth, 128, 16, 128])
    f1 = [din("f1_wg", [depth, D, DFF]), din("f1_wu", [depth, D, DFF]), din("f1_wd", [depth, DFF, D])]
    f2 = [din("f2_wg", [depth, D, DFF]), din("f2_wu", [depth, D, DFF]), din("f2_wd", [depth, DFF, D])]
    bias_in = din("biasT", [128, 2, 2, 4, 128])
    sinks_all = din("sinks", [depth, 1, 1024])
    memT_in = din("memT", [128, 8, 256])
    w_attn_all = din("w_attn", [depth, 512, D])
    w_lru_all = din("w_lru", [depth, D, D])
    w_memkv_all = din("w_memkv", [depth, D, D])
    w_memo_all = din("w_memo", [depth, 512, D])
    w_out_all = din("w_out", [depth, D, D])
    y_out = dout("y_out", [128, 8, NT])
    xres = nc.dram_tensor("xres", [128, 8, NT], F32).ap()
    Bxres = [Buf("xres%d" % i) for i in range(ntiles)]
    CW1 = 8 * 128 + 8
    cc1_in = [nc.dram_tensor("cc1_in%d" % l, [128, CW1], F32) for l in range(depth)]
    cc1_out = [nc.dram_tensor("cc1_out%d" % l, [256, CW1], F32) for l in range(depth)]
    cc2_in = [nc.dram_tensor("cc2_in%d" % l, [128, 32], F32) for l in range(depth)]
    cc2_out = [nc.dram_tensor("cc2_out%d" % l, [256, 32], F32) for l in range(depth)]
    RG = [[0, 1], [2, 3], [4, 5], [6, 7]]
    rg_d = nc.dram_tensor("rg_d", [128, 2, 8, NT], F32).ap()
    hsc = nc.dram_tensor("hsc", [128, 8, NT], BF16).ap()
    Bhsc = [Buf("hsc%d" % i) for i in range(ntiles)]
    h1sc = nc.dram_tensor("h1sc", [128, 8, NT], BF16).ap()
    Bh1sc = [Buf("h1sc%d" % i) for i in range(ntiles)]
    Brg = [[[Buf("rg%d_%d_%d" % (i, c, k)) for k in range(2)] for c in range(8)] for i in range(ntiles)]
    p23 = True

    def sb(name, shape, dt=F32):
        return nc.alloc_sbuf_tensor(name, shape, dt).ap()

    vecs = sb("vecs_s", [128, depth * NV]); Bvecs = Buf("vecs")
    cf = sb("cf", [128, 32]); Bcf = Buf("cf")
    tmpv = sb("tmpv", [128, 4, 8]); Btmpv = Buf("tmpv")
    ones32 = sb("ones32", [128, 128]); Bones32 = Buf("ones32")
    onesb = sb("onesb", [128, 128], BF16); Bonesb = Buf("onesb")
    wbd = sb("wbd_s", [128, 16, 128], BF16); Bwbd = Buf("wbd")
    xt = sb("xt", [128, 8, T]); Bxt = Buf("xt")
    big32 = sb("big32", [128, 8, T]); Bbig = Buf("big32")
    hT = sb("hT", [128, 8, T], BF16); BhT = Buf("hT")
    hist = sb("hist", [128, 8, 4]); Bhist = Buf("hist")
    state = sb("state", [128, 8]); Bstate = Buf("state")
    FH = 11
    actT = sb("actT", [128, FH, T], BF16); Bact = Buf("actT")
    stg = Rot(nc, "stg", [128, 4096], F32, 2)
    wbr = Rot(nc, "wbf", [128, 4096], BF16, 2)
    tA = Rot(nc, "tA", [128, T + 4], F32, 4)
    tB = Rot(nc, "tB", [128, T], F32, 20)
    tC = Rot(nc, "tC", [128, T], BF16, 4)
    psb = [(nc.alloc_psum_tensor("ps%d" % i, [128, 512], F32).ap(), Buf("ps%d" % i)) for i in range(8)]
    psi = [0]

    def psum():
        it = psb[psi[0] % 8]
        psi[0] += 1
        return it

    casti = [0]

    def cast_eng():
        casti[0] += 1
        return "pool" if casti[0] % 2 else "act"

    def copy_op(eng, out, in_):
        if eng == "act":
            return lambda e: e.copy(out=out, in_=in_)
        return lambda e: e.tensor_copy(out=out, in_=in_)

    if p23:
        biasT = sb("biasT_s", [128, 2, 2, 4, 128]); BbiasT = Buf("biasT")
        esink = sb("esink", [64, 512], BF16); Besink = Buf("esink")
        qT = sb("qT", [128, 4, T], BF16); BqT = Buf("qT")
        kT = sb("kT", [128, 128 + T], BF16); BkT = Buf("kT")
        vtm = sb("vtm", [128, 5, 128], BF16); Bvtm = Buf("vtm")
        cqT = sb("cqT", [128, 4, T], BF16); BcqT = Buf("cqT")
        attnT = sb("attnT", [64, 8, T], BF16); BattnT = Buf("attnT")
        lruT = sb("lruT", [128, 8, T], BF16); BlruT = Buf("lruT")
        memoT = sb("memoT", [128, 4, T], BF16); BmemoT = Buf("memoT")
        merged, Bmerged = lruT, BlruT
        kmT = sb("kmT", [128, 4, 256], BF16); BkmT = Buf("kmT")
        vm = sb("vm", [128, 2, 512], BF16); Bvm = Buf("vm")
        pT = Rot(nc, "pT", [128, 2, 512], BF16, 2)

    WSC_COLS = depth * 222000
    wsc = nc.dram_tensor("wsc", [128, WSC_COLS], BF16).ap()
    wcache = {}
    wsc_pos = [0]

    def linear(W, K, c0, ncols, rhs, rhs_bufs, n, consume, kp=128):
        kcs = K // kp
        nch = ncols // 128
        G = max(1, min(nch, 32 // kcs))
        Wv = W.rearrange("(kc p) n -> p kc n", p=kp)
        for g0 in range(0, nch, G):
            gn = min(G, nch - g0)
            wb, wb_b = wbr.next()
            ncol = kcs * gn * 128
            wbv = wb[:kp, :ncol].rearrange("p (k n) -> p k n", k=kcs)
            key = (W.tensor.name, W.offset, K, kp, c0, g0, gn)
            if key in wcache:
                pos, cb = wcache[key]
                S.dma("sp", wb_b.name, wb[:kp, :ncol], wsc[:kp, pos:pos + ncol], reads=[cb], writes=[wb_b])
            else:
                st, st_b = stg.next()
                stv = st[:kp, :ncol].rearrange("p (k n) -> p k n", k=kcs)
                S.dma("sp", st_b.name, stv, Wv[:, :, c0 + g0 * 128:c0 + (g0 + gn) * 128], writes=[st_b])
                ce = cast_eng()
                S.op(ce, copy_op(ce, wb[:kp, :ncol], st[:kp, :ncol]), reads=[st_b], writes=[wb_b])
                pos = wsc_pos[0]
                wsc_pos[0] += ncol
                assert wsc_pos[0] <= WSC_COLS
                cb = Buf("wsc%d" % pos)
                S.dma(ce, "wscw_" + ce, wsc[:kp, pos:pos + ncol], wb[:kp, :ncol], reads=[wb_b], writes=[cb])
                wcache[key] = (pos, cb)
            for j in range(gn):
                ps, ps_b = psum()

                def mm(e, ps=ps, wbv=wbv, j=j):
                    ins = None
                    for kc in range(kcs):
                        ins = e.matmul(ps[:, :n], lhsT=wbv[:, kc, j * 128:(j + 1) * 128], rhs=rhs(kc),
                                       start=(kc == 0), stop=(kc == kcs - 1))
                    return ins
                S.op("pe", mm, reads=[wb_b] + list(rhs_bufs), writes=[ps_b])
                consume(g0 + j, ps, ps_b)

    def rmsnorm(xsrc, Bx, n, gofs, out, Bout):
        vo = L["vo"]
        rstd, Brstd = tB.next()
        for c in range(8):
            S.op("act", lambda e, c=c: e.activation(out=big32[:, c, :n], in_=xsrc[:, c, :n], func=AF.Square),
                 reads=[Bx], writes=[Bbig])
        ps, ps_b = psum()

        def mm(e):
            ins = None
            for c in range(8):
                ins = e.matmul(ps[:, :n], lhsT=ones32, rhs=big32[:, c, :n], start=(c == 0), stop=(c == 7))
            return ins
        S.op("pe", mm, reads=[Bbig, Bones32], writes=[ps_b])
        S.op("act", lambda e: e.activation(out=rstd[:, :n], in_=ps[:, :n], func=AF.Sqrt, bias=EPS, scale=1.0),
             reads=[ps_b], writes=[Brstd])
        S.op("dve", lambda e: e.reciprocal(out=rstd[:, :n], in_=rstd[:, :n]), reads=[Brstd], writes=[Brstd])
        for c in range(8):
            S.op("dve", lambda e, c=c: e.scalar_tensor_tensor(
                out=out[:, c, :n], in0=xsrc[:, c, :n], scalar=vecs[:, vo + gofs + c:vo + gofs + c + 1], in1=rstd[:, :n],
                op0=ALU.mult, op1=ALU.mult), reads=[Bx, Brstd, Bvecs], writes=[Bout])

    def ffn(n, gofs, hload=None):
        vo = L["vo"]
        if hload is None:
            rmsnorm(xt, Bxt, n, gofs, hT, BhT)
        elif hload != "done":
            S.dma("sp", "hT", hT, hload[0], reads=[hload[1]], writes=[BhT])
        for half in range(2):
            j0 = half * FH
            gate_sb = {}

            def cons_gate(j, ps, ps_b):
                t, t_b = tB.next()
                S.op("act", lambda e, t=t, ps=ps: e.activation(out=t[:, :n], in_=ps[:, :n], func=AF.Silu),
                     reads=[ps_b], writes=[t_b])
                gate_sb[j] = (t, t_b)

            def cons_up(j, ps, ps_b):
                t, t_b = gate_sb[j]
                S.op("dve", lambda e, t=t, ps=ps, j=j: e.tensor_tensor(out=actT[:, j, :n], in0=t[:, :n], in1=ps[:, :n],
                                                                       op=ALU.mult),
                     reads=[ps_b, t_b], writes=[Bact])
            for s0 in range(0, FH, 4):
                sn = min(4, FH - s0)
                base = j0 + s0
                linear(L["wg"], D, base * 128, sn * 128, lambda kc: hT[:, kc, :n], [BhT], n,
                       lambda j, ps, ps_b, s0=s0: cons_gate(s0 + j, ps, ps_b))
                linear(L["wu"], D, base * 128, sn * 128, lambda kc: hT[:, kc, :n], [BhT], n,
                       lambda j, ps, ps_b, s0=s0: cons_up(s0 + j, ps, ps_b))

            def cons_down(j, ps, ps_b):
                S.op("dve", lambda e, ps=ps, j=j: e.scalar_tensor_tensor(
                    out=xt[:, j, :n], in0=ps[:, :n], scalar=0.5, in1=xt[:, j, :n], op0=ALU.mult, op1=ALU.add),
                    reads=[ps_b, Bxt], writes=[Bxt])
            linear(L["wd"][j0 * 128:(j0 + FH) * 128, :], FH * 128, 0, D, lambda kc: actT[:, kc, :n], [Bact], n, cons_down)

    NB = 4

    def lru(n, first_tile, ti=0):
        vo = L["vo"]
        want_out = L["p23"]
        hsrc = lambda kc: hT[:, kc, :n]
        t0_ = ti * T
        for cb in range(0, 8, NB):
            cs = list(range(cb, min(8, cb + NB)))
            X = {}
            if want_out:
                for c in cs:
                    xc, xc_b = tB.next()
                    r, r_b = tB.next()
                    T1, T1_b = tB.next()
                    T2, T2_b = tB.next()
                    X[c] = {"xc": (xc, xc_b), "r": (r, r_b), "T1": (T1, T1_b), "T2": (T2, T2_b)}
                    S.dma("pool", "ld_r%d" % c, r[:, :n], rg_d[:, 0, c, t0_:t0_ + n], reads=[Brg[ti][c][0]],
                          writes=[r_b], nosuffix=True)
                    S.dma("pool", "ld_g%d" % c, xc[:, :n], rg_d[:, 1, c, t0_:t0_ + n], reads=[Brg[ti][c][1]],
                          writes=[xc_b], nosuffix=True)
            if not want_out:
                    for c in cs:
                        xrb, xrb_b = tA.next()
                        xc, xc_b = tB.next()
                        X[c] = {"xrb": (xrb, xrb_b), "xc": (xc, xc_b)}
                        S.op("dve", lambda e, xrb=xrb, c=c: e.tensor_copy(out=xrb[:, 0:3], in_=hist[:, c, 0:3]),
                             reads=[Bhist], writes=[xrb_b])

                        def cons_xr(j, ps, ps_b, xrb=xrb, xrb_b=xrb_b):
                            S.op("dve", lambda e: e.tensor_copy(out=xrb[:, 3:3 + n], in_=ps[:, :n]), reads=[ps_b], writes=[xrb_b])
                        linear(L["w_in"], D, O_XR + c * 128, 128, hsrc, [BhT], n, cons_xr)
                    for c in cs:
                        xrb, xrb_b = X[c]["xrb"]
                        xc, xc_b = X[c]["xc"]
                        S.op("dve", lambda e, xc=xc, xrb=xrb, c=c: e.tensor_scalar(
                            out=xc[:, :n], in0=xrb[:, 0:n], scalar1=vecs[:, vo + V_CW + c:vo + V_CW + c + 1],
                            scalar2=vecs[:, vo + V_CB + c:vo + V_CB + c + 1], op0=ALU.mult, op1=ALU.add),
                            reads=[xrb_b, Bvecs], writes=[xc_b])
                        for j in range(1, 4):
                            S.op("dve", lambda e, xc=xc, xrb=xrb, c=c, j=j: e.scalar_tensor_tensor(
                                out=xc[:, :n], in0=xrb[:, j:j + n],
                                scalar=vecs[:, vo + V_CW + j * 8 + c:vo + V_CW + j * 8 + c + 1],
                                in1=xc[:, :n], op0=ALU.mult, op1=ALU.add), reads=[xrb_b, Bvecs, xc_b], writes=[xc_b])
                        S.op("dve", lambda e, xrb=xrb, c=c: e.tensor_copy(out=hist[:, c, 0:3], in_=xrb[:, n:n + 3]),
                             reads=[xrb_b], writes=[Bhist])
                        xcb, xcb_b = tC.next()
                        X[c]["xcb"] = (xcb, xcb_b)
                        S.op("pool", lambda e, xcb=xcb, xc=xc: e.tensor_copy(out=xcb[:, :n], in_=xc[:, :n]), reads=[xc_b],
                             writes=[xcb_b])
                    for c in cs:
                        xcb, xcb_b = X[c]["xcb"]
                        psa, psa_b = psum()
                        S.op("pe", lambda e, psa=psa, xcb=xcb, c=c: e.matmul(psa[:, :n], lhsT=wbd[:, c, :], rhs=xcb[:, :n],
                                                                          start=True, stop=True),
                             reads=[Bwbd, xcb_b], writes=[psa_b])
                        psx, psx_b = psum()
                        S.op("pe", lambda e, psx=psx, xcb=xcb, c=c: e.matmul(psx[:, :n], lhsT=wbd[:, 8 + c, :], rhs=xcb[:, :n],
                                                                          start=True, stop=True),
                             reads=[Bwbd, xcb_b], writes=[psx_b])
                        X[c]["ps"] = (psa, psa_b, psx, psx_b)
                    for c in cs:
                        psa, psa_b, psx, psx_b = X[c]["ps"]
                        r, r_b = tB.next()
                        gi, gi_b = tB.next()
                        X[c]["r"] = (r, r_b)
                        X[c]["gi"] = (gi, gi_b)
                        S.op("act", lambda e, r=r, psa=psa, c=c: e.activation(out=r[:, :n], in_=psa[:, :n], func=AF.Sigmoid,
                                                                           bias=vecs[:, vo + V_BA + c:vo + V_BA + c + 1]),
                             reads=[psa_b, Bvecs], writes=[r_b])
                        S.op("act", lambda e, gi=gi, psx=psx, c=c: e.activation(out=gi[:, :n], in_=psx[:, :n], func=AF.Sigmoid,
                                                                             bias=vecs[:, vo + V_BX + c:vo + V_BX + c + 1]),
                             reads=[psx_b, Bvecs], writes=[gi_b])
                    for c in cs:
                        gi, gi_b = X[c]["gi"]
                        xc, xc_b = X[c]["xc"]
                        S.op("pool", lambda e, gi=gi, xc=xc: e.tensor_tensor(out=gi[:, :n], in0=gi[:, :n], in1=xc[:, :n],
                                                                            op=ALU.mult), reads=[gi_b, xc_b], writes=[gi_b])
            if not want_out:
                for c in cs:
                    r, r_b = X[c]["r"]
                    T1, T1_b = tB.next()
                    T2, T2_b = tB.next()
                    X[c]["T1"] = (T1, T1_b)
                    X[c]["T2"] = (T2, T2_b)
                    S.op("act", lambda e, T1=T1, r=r, c=c: e.activation(out=T1[:, :n], in_=r[:, :n], func=AF.Tanh,
                                                                     scale=cf[:, c:c + 1]), reads=[r_b, Bcf], writes=[T1_b])
                    S.op("act", lambda e, T2=T2, r=r, c=c: e.activation(out=T2[:, :n], in_=r[:, :n], func=AF.Tanh,
                                                                     scale=cf[:, 24 + c:25 + c]), reads=[r_b, Bcf], writes=[T2_b])
                for c in cs:
                    r, r_b = X[c]["r"]
                    S.op("act", lambda e, r=r, c=c: e.activation(out=r[:, :n], in_=r[:, :n], func=AF.Exp,
                                                               scale=cf[:, 8 + c:9 + c]), reads=[r_b, Bcf], writes=[r_b])
                for c in cs:
                    r, r_b = X[c]["r"]
                    xc, xc_b = X[c]["xc"]
                    S.op("pool", lambda e, r=r, xc=xc: e.tensor_tensor(out=xc[:, :n], in0=r[:, :n], in1=r[:, :n],
                                                                      op=ALU.mult), reads=[r_b, xc_b], writes=[xc_b])
                for c in cs:
                    r, r_b = X[c]["r"]
                    T1, T1_b = X[c]["T1"]
                    S.op("dve", lambda e, r=r, T1=T1: e.scalar_tensor_tensor(out=r[:, :n], in0=r[:, :n], scalar=1.0,
                                                                          in1=T1[:, :n], op0=ALU.add, op1=ALU.mult),
                         reads=[r_b, T1_b], writes=[r_b])
                    S.op("pool", lambda e, r=r: e.tensor_scalar(out=r[:, :n], in0=r[:, :n], scalar1=1.0, scalar2=None,
                                                              op0=ALU.add), reads=[r_b], writes=[r_b])
                for c in cs:
                    xc, xc_b = X[c]["xc"]
                    T2, T2_b = X[c]["T2"]
                    S.op("dve", lambda e, xc=xc, T2=T2: e.scalar_tensor_tensor(out=xc[:, :n], in0=xc[:, :n], scalar=1.0,
                                                                            in1=T2[:, :n], op0=ALU.add, op1=ALU.mult),
                         reads=[xc_b, T2_b], writes=[xc_b])
                for c in cs:
                    xc, xc_b = X[c]["xc"]
                    gi, gi_b = X[c]["gi"]
                    S.op("act", lambda e, xc=xc: e.activation(out=xc[:, :n], in_=xc[:, :n], func=AF.Sqrt), reads=[xc_b],
                         writes=[xc_b])
                    if first_tile:
                        S.op("dve", lambda e, xc=xc: e.scalar_tensor_tensor(
                            out=xc[:, 0:1], in0=xc[:, 0:1], scalar=vecs[:, vo + V_OMF0:vo + V_OMF0 + 1],
                            in1=vecs[:, vo + V_F0:vo + V_F0 + 1], op0=ALU.mult, op1=ALU.add), reads=[xc_b, Bvecs],
                            writes=[xc_b])
                    S.op("pool", lambda e, xc=xc, gi=gi: e.tensor_tensor(out=xc[:, :n], in0=xc[:, :n], in1=gi[:, :n],
                                                                        op=ALU.mult), reads=[xc_b, gi_b], writes=[xc_b])
                for c in cs:
                    r, r_b = X[c]["r"]
                    xc, xc_b = X[c]["xc"]
                    S.dma("pool", "sv_r%d" % (c % NB), rg_d[:, 0, c, t0_:t0_ + n], r[:, :n], reads=[r_b],
                          writes=[Brg[ti][c][0]], nosuffix=True)
                    S.dma("pool", "sv_g%d" % (c % NB), rg_d[:, 1, c, t0_:t0_ + n], xc[:, :n], reads=[xc_b],
                          writes=[Brg[ti][c][1]], nosuffix=True)
            for c in cs:
                r, r_b = X[c]["r"]
                xc, xc_b = X[c]["xc"]
                T2, T2_b = X[c]["T2"]
                S.op("dve", lambda e, T2=T2, r=r, xc=xc, c=c: e.tensor_tensor_scan(
                    out=T2[:, :n], data0=r[:, :n], data1=xc[:, :n], initial=state[:, c:c + 1], op0=ALU.mult,
                    op1=ALU.add), reads=[r_b, xc_b, Bstate, T2_b], writes=[T2_b])
                S.op("dve", lambda e, T2=T2, c=c: e.tensor_copy(out=state[:, c:c + 1], in_=T2[:, n - 1:n]),
                     reads=[T2_b], writes=[Bstate])
            if want_out:
                for c in cs:
                    T1, T1_b = X[c]["T1"]

                    def cons_yr(j, ps, ps_b, T1=T1, T1_b=T1_b):
                        S.op("act", lambda e: e.activation(out=T1[:, :n], in_=ps[:, :n], func=AF.Gelu_apprx_tanh),
                             reads=[ps_b, T1_b], writes=[T1_b])
                    linear(L["w_in"], D, O_YR + c * 128, 128, hsrc, [BhT], n, cons_yr)
                for c in cs:
                    T1, T1_b = X[c]["T1"]
                    T2, T2_b = X[c]["T2"]
                    S.op("dve", lambda e, T2=T2, T1=T1, c=c: e.tensor_tensor(out=lruT[:, c, :n], in0=T2[:, :n],
                                                                          in1=T1[:, :n], op=ALU.mult),
                         reads=[T2_b, T1_b], writes=[BlruT])

    S.dma("sp", "vecs", vecs, vecs_in, writes=[Bvecs])
    S.op("dve", lambda e: e.memset(ones32, 1.0 / D), writes=[Bones32])
    S.op("dve", lambda e: e.memset(onesb, 1.0), writes=[Bonesb])
    S.dma("sp", "biasT", biasT, bias_in, writes=[BbiasT])

    def layer_prelude(l):
        vo = L["vo"]
        st, st_b = stg.next()
        stv = st[:, :2048].rearrange("p (k n) -> p k n", k=16)
        S.dma("sp", st_b.name, stv, wbd_all[l], writes=[st_b])
        S.op("dve", lambda e, stv=stv: e.tensor_copy(out=wbd, in_=stv), reads=[st_b], writes=[Bwbd])
        e_ = tmpv[:, 0, :]
        acc = tmpv[:, 1, :]
        S.op("act", lambda e: e.activation(out=e_, in_=vecs[:, vo + V_LAM:vo + V_LAM + 8], func=AF.Exp, scale=-1.0),
             reads=[Bvecs], writes=[Btmpv])
        S.op("dve", lambda e: e.tensor_scalar(out=acc, in0=e_, scalar1=-1.0 / 6.0, scalar2=1.0 / 5.0, op0=ALU.mult,
                                              op1=ALU.add), reads=[Btmpv], writes=[Btmpv])
        for coef in (-1.0 / 4.0, 1.0 / 3.0, -1.0 / 2.0, 1.0):
            S.op("dve", lambda e: e.tensor_tensor(out=acc, in0=acc, in1=e_, op=ALU.mult), reads=[Btmpv], writes=[Btmpv])
            if coef < 0:
                S.op("dve", lambda e, coef=coef: e.tensor_scalar(out=acc, in0=acc, scalar1=-1.0, scalar2=-coef,
                                                                 op0=ALU.mult, op1=ALU.add), reads=[Btmpv], writes=[Btmpv])
            else:
                S.op("dve", lambda e, coef=coef: e.tensor_scalar(out=acc, in0=acc, scalar1=-1.0, scalar2=coef,
                                                                 op0=ALU.mult, op1=ALU.add), reads=[Btmpv], writes=[Btmpv])
        S.op("dve", lambda e: e.tensor_tensor(out=acc, in0=acc, in1=e_, op=ALU.mult), reads=[Btmpv], writes=[Btmpv])
        for k_, mul_ in enumerate((-4.0, -8.0, -16.0, 8.0)):
            S.op("dve", lambda e, k_=k_, mul_=mul_: e.tensor_scalar(out=cf[:, k_ * 8:(k_ + 1) * 8], in0=acc, scalar1=mul_,
                                                                    scalar2=None, op0=ALU.mult), reads=[Btmpv], writes=[Bcf])

    def inproj_xr_hist(n):
        for c in range(8):
            def cons(j, ps, ps_b, c=c):
                S.op("act", lambda e: e.copy(out=hist[:, c, 0:3], in_=ps[:, n - 3:n]), reads=[ps_b], writes=[Bhist])
            linear(L["w_in"], D, O_XR + c * 128, 128, lambda kc: hT[:, kc, :n], [BhT], n, cons)

    def inproj_v(n, blk0):
        st, st_b = stg.next()
        wb_, wb_b = wbr.next()
        stv = st[:, :1024].rearrange("p (k n) -> p k n", k=8)
        wb3 = wb_[:, :1024].rearrange("p (k n) -> p k n", k=8)
        S.dma("sp", st_b.name, stv, L["w_in"].rearrange("(kc p) n -> p kc n", p=128)[:, :, O_V:O_V + 128], writes=[st_b])
        S.op("pool", lambda e, wb_=wb_, st=st: e.tensor_copy(out=wb_[:, :1024], in_=st[:, :1024]), reads=[st_b], writes=[wb_b])
        for i in range(n // 128):
            ps, ps_b = psum()

            def mmv(e, ps=ps, i=i):
                ins = None
                for kc in range(8):
                    ins = e.matmul(ps[:, :128], lhsT=hT[:, kc, i * 128:(i + 1) * 128], rhs=wb3[:, kc, :],
                                   start=(kc == 0), stop=(kc == 7))
                return ins
            S.op("pe", mmv, reads=[BhT, wb_b], writes=[ps_b])
            S.op("act", lambda e, ps=ps, i=i: e.copy(out=vtm[:, blk0 + i, :], in_=ps[:, :128]), reads=[ps_b],
                 writes=[Bvtm])

    def p23_prelude(l):
        sink32, Bsink32 = tB.next()
        for kh_ in range(2):
            S.dma("sp", "sink32", sink32[kh_ * 32:kh_ * 32 + 1, :], sinks_all[l][0:1, kh_ * 512:(kh_ + 1) * 512],
                  writes=[Bsink32])
        for kh_ in range(2):
            S.op("act", lambda e, kh_=kh_, sink32=sink32: e.activation(
                out=esink[kh_ * 32:kh_ * 32 + 1, :], in_=sink32[kh_ * 32:kh_ * 32 + 1, :], func=AF.Exp),
                reads=[Bsink32], writes=[Besink])
        S.dma("sp", "xt", xt[:, :, :256], memT_in, writes=[Bxt])
        rmsnorm(xt, Bxt, 256, V_GMEM, hT, BhT)

        def cons_km(j, ps, ps_b):
            S.op("act", lambda e: e.copy(out=kmT[:, j, :], in_=ps[:, :256]), reads=[ps_b], writes=[BkmT])
        linear(w_memkv_all[l], D, 0, 512, lambda kc: hT[:, kc, :256], [BhT], 256, cons_km)
        st, st_b = stg.next()
        wbv_, wbv_b = wbr.next()
        stv = st.rearrange("p (k n) -> p k n", k=8)
        wv3 = wbv_.rearrange("p (k n) -> p k n", k=8)
        S.dma("sp", st_b.name, stv, w_memkv_all[l].rearrange("(kc p) n -> p kc n", p=128)[:, :, 512:1024], writes=[st_b])
        S.op("pool", lambda e, wbv_=wbv_, st=st: e.tensor_copy(out=wbv_, in_=st), reads=[st_b], writes=[wbv_b])
        for mb in range(2):
            ps, ps_b = psum()

            def mmv(e, ps=ps, mb=mb):
                ins = None
                for kc in range(8):
                    ins = e.matmul(ps, lhsT=hT[:, kc, mb * 128:(mb + 1) * 128], rhs=wv3[:, kc, :], start=(kc == 0),
                                   stop=(kc == 7))
                return ins
            S.op("pe", mmv, reads=[BhT, wbv_b], writes=[ps_b])
            S.op("act", lambda e, ps=ps, mb=mb: e.copy(out=vm[:, mb, :], in_=ps), reads=[ps_b], writes=[Bvm])


    def final_norm_store(t0):
        vo = L["vo"]
        rstd, Brstd = tB.next()
        for c in range(8):
            S.op("act", lambda e, c=c: e.activation(out=big32[:, c, :], in_=xt[:, c, :], func=AF.Square),
                 reads=[Bxt], writes=[Bbig])
        ps, ps_b = psum()

        def mmf(e, ps=ps):
            ins = None
            for c in range(8):
                ins = e.matmul(ps, lhsT=ones32, rhs=big32[:, c, :], start=(c == 0), stop=(c == 7))
            return ins
        S.op("pe", mmf, reads=[Bbig, Bones32], writes=[ps_b])
        S.op("act", lambda e, ps=ps: e.activation(out=rstd, in_=ps, func=AF.Sqrt, bias=EPS, scale=1.0), reads=[ps_b],
             writes=[Brstd])
        S.op("dve", lambda e: e.reciprocal(out=rstd, in_=rstd), reads=[Brstd], writes=[Brstd])
        for c in range(8):
            S.op("dve", lambda e, c=c: e.scalar_tensor_tensor(
                out=big32[:, c, :], in0=xt[:, c, :], scalar=vecs[:, vo + V_GFIN + c:vo + V_GFIN + c + 1], in1=rstd,
                op0=ALU.mult, op1=ALU.mult), reads=[Bxt, Brstd, Bvecs, Bbig], writes=[Bbig])
        S.dma("sp", "y_out", y_out[:, :, t0:t0 + T], big32, reads=[Bbig], writes=[Byo])

    def p23_phase(l, last):
        vo = L["vo"]
        HN = 128
        S.dma("sp", "xt", xt[:, :, :HN].rearrange("p c n -> p (c n)") if False else xt[:, :, :HN],
              cc1_out[l].ap()[0:128, 0:1024].rearrange("p (c n) -> p c n", c=8), reads=[Bcc1o[l]], writes=[Bxt])
        S.op("dve", lambda e: e.tensor_scalar(out=xt[:, :, :HN], in0=xt[:, :, :HN], scalar1=vecs[:, V_OMF0:V_OMF0 + 1],
                                              scalar2=None, op0=ALU.mult), reads=[Bxt, Bvecs], writes=[Bxt])
        S.dma("sp", "state", state, cc1_out[l].ap()[0:128, 1024:1032], reads=[Bcc1o[l]], writes=[Bstate])
        S.op("dve", lambda e: e.tensor_scalar(out=state, in0=state, scalar1=vecs[:, V_OMF0:V_OMF0 + 1],
                                              scalar2=None, op0=ALU.mult), reads=[Bstate, Bvecs], writes=[Bstate])
        rmsnorm(xt, Bxt, HN, V_GM, hT, BhT)

        def cons_k0(j, ps, ps_b):
            S.op("act", lambda e: e.copy(out=kT[:, 0:128], in_=ps[:, :128]), reads=[ps_b], writes=[BkT])
        linear(L["w_in"], D, O_K, 128, lambda kc: hT[:, kc, :HN], [BhT], HN, cons_k0)
        inproj_v(HN, 0)

        for ti in range(ntiles):
            t0 = ti * T
            n = T
            S.dma("sp", "hT", hT, hsc[:, :, t0:t0 + T], reads=[Bhsc[ti]], writes=[BhT])
            rh = lambda kc: hT[:, kc, :n]

            def cons_q(j, ps, ps_b):
                S.op("act", lambda e: e.copy(out=qT[:, j, :], in_=ps), reads=[ps_b], writes=[BqT])
            linear(L["w_in"], D, O_Q, 512, rh, [BhT], n, cons_q)

            def cons_k(j, ps, ps_b):
                S.op("act", lambda e: e.copy(out=kT[:, 128:128 + T], in_=ps), reads=[ps_b], writes=[BkT])
            linear(L["w_in"], D, O_K, 128, rh, [BhT], n, cons_k)
            inproj_v(T, 1)

            def cons_cq(j, ps, ps_b):
                S.op("act", lambda e: e.copy(out=cqT[:, j, :], in_=ps), reads=[ps_b], writes=[BcqT])
            linear(L["w_in"], D, O_CQ, 512, rh, [BhT], n, cons_cq)

            for bi in range(4):
                for kh in range(2):
                    r0 = kh * 64
                    pt, pt_b = pT.next()
                    for kb in range(2):
                        ps, ps_b = psum()
                        kcol = (bi + kb) * 128
                        S.op("pe", lambda e, ps=ps, kcol=kcol, r0=r0, bi=bi: e.matmul(
                            ps, lhsT=kT[r0:r0 + 64, kcol:kcol + 128], rhs=qT[r0:r0 + 64, :, bi * 128:(bi + 1) * 128],
                            start=True, stop=True), reads=[BkT, BqT], writes=[ps_b])
                        sc, sc_b = tB.next()
                        bsrc, bsrc_b = biasT[:, kh, kb], BbiasT
                        S.op("dve", lambda e, sc=sc, ps=ps, bsrc=bsrc: e.scalar_tensor_tensor(
                            out=sc.rearrange("p (g q) -> p g q", g=4), in0=ps.rearrange("p (g q) -> p g q", g=4), scalar=0.125,
                            in1=bsrc, op0=ALU.mult, op1=ALU.add), reads=[ps_b, bsrc_b], writes=[sc_b])
                        if ti == 0 and bi == 0 and kb == 0:
                            S.op("dve", lambda e, sc=sc: e.tensor_scalar(out=sc, in0=sc, scalar1=vecs[:, V_MASKA:V_MASKA + 1],
                                                                        scalar2=None, op0=ALU.add),
                                 reads=[sc_b, Bvecs], writes=[sc_b])
                        S.op("act", lambda e, sc=sc, pt=pt, kb=kb: e.activation(out=pt[:, kb, :], in_=sc, func=AF.Exp),
                             reads=[sc_b], writes=[pt_b])
                    psn, psn_b = psum()

                    def mmn(e, psn=psn, pt=pt, kh=kh, bi=bi):
                        ins = None
                        for kb in range(2):
                            ins = e.matmul(psn[0:64, :], lhsT=vtm[:, bi + kb, kh * 64:(kh + 1) * 64], rhs=pt[:, kb, :],
                                           start=(kb == 0), stop=(kb == 1))
                        return ins
                    S.op("pe", mmn, reads=[Bvtm, pt_b], writes=[psn_b])
                    psd, psd_b = psum()

                    def mmd(e, psd=psd, pt=pt, kh=kh):
                        for kb in range(2):
                            e.matmul(psd[0:64, :], lhsT=onesb[:, 0:64], rhs=pt[:, kb, :], start=(kb == 0), stop=False)
                        return e.matmul(psd[0:64, :], lhsT=onesb[kh * 32:kh * 32 + 1, 0:64], rhs=esink[kh * 32:kh * 32 + 1, :],
                                        start=False, stop=True)
                    S.op("pe", mmd, reads=[Bonesb, pt_b, Besink], writes=[psd_b])
                    rd, rd_b = tB.next()
                    S.op("dve", lambda e, rd=rd, psd=psd: e.reciprocal(out=rd[0:64, :], in_=psd[0:64, :]), reads=[psd_b],
                         writes=[rd_b])
                    S.op("dve", lambda e, rd=rd, psn=psn, kh=kh, bi=bi: e.tensor_tensor(
                        out=attnT[:, kh * 4:(kh + 1) * 4, bi * 128:(bi + 1) * 128],
                        in0=psn[0:64, :].rearrange("p (g q) -> p g q", g=4), in1=rd[0:64, :].rearrange("p (g q) -> p g q", g=4),
                        op=ALU.mult), reads=[psn_b, rd_b], writes=[BattnT])
            S.op("pool", lambda e: e.tensor_copy(out=kT[:, 0:128], in_=kT[:, T:T + 128]), reads=[BkT], writes=[BkT])
            S.op("pool", lambda e: e.tensor_copy(out=vtm[:, 0, :], in_=vtm[:, 4, :]), reads=[Bvtm], writes=[Bvtm])

            for hd in range(4):
                pt, pt_b = pT.next()
                for mb in range(2):
                    ps, ps_b = psum()
                    S.op("pe", lambda e, ps=ps, hd=hd, mb=mb: e.matmul(ps, lhsT=kmT[:, hd, mb * 128:(mb + 1) * 128],
                                                                    rhs=cqT[:, hd, :], start=True, stop=True),
                         reads=[BkmT, BcqT], writes=[ps_b])
                    S.op("act", lambda e, ps=ps, pt=pt, mb=mb: e.activation(out=pt[:, mb, :], in_=ps, func=AF.Exp,
                                                                         scale=128.0 ** -0.5), reads=[ps_b], writes=[pt_b])
                psn, psn_b = psum()

                def mmn2(e, psn=psn, pt=pt, hd=hd):
                    ins = None
                    for mb in range(2):
                        ins = e.matmul(psn, lhsT=vm[:, mb, hd * 128:(hd + 1) * 128], rhs=pt[:, mb, :], start=(mb == 0),
                                       stop=(mb == 1))
                    return ins
                S.op("pe", mmn2, reads=[Bvm, pt_b], writes=[psn_b])
                psd, psd_b = psum()

                def mmd2(e, psd=psd, pt=pt):
                    ins = None
                    for mb in range(2):
                        ins = e.matmul(psd, lhsT=onesb, rhs=pt[:, mb, :], start=(mb == 0), stop=(mb == 1))
                    return ins
                S.op("pe", mmd2, reads=[Bonesb, pt_b], writes=[psd_b])
                rd, rd_b = tB.next()
                S.op("dve", lambda e, rd=rd, psd=psd: e.reciprocal(out=rd, in_=psd), reads=[psd_b], writes=[rd_b])
                S.op("dve", lambda e, rd=rd, psn=psn, hd=hd: e.tensor_tensor(out=memoT[:, hd, :], in0=psn, in1=rd, op=ALU.mult),
                     reads=[psn_b, rd_b], writes=[BmemoT])

            S.dma("sp", "xt", xt, xres[:, :, t0:t0 + T], reads=[Bxres[ti]], writes=[Bxt])
            lru(T, ti == 0, ti)

            branches = [
                (w_attn_all[l], 512, 64, lambda kc: attnT[:, kc, :], BattnT),
                (w_lru_all[l], D, 128, lambda kc: lruT[:, kc, :], BlruT),
                (w_memo_all[l], 512, 128, lambda kc: memoT[:, kc, :], BmemoT),
            ]
            for b, (Wb, Kb, kpb, rb, Brb) in enumerate(branches):
                sg_sb = {}
                for g0 in range(0, 8, 4):
                    def cons_g(j, ps, ps_b, b=b, g0=g0):
                        t, t_b = tB.next()
                        jj = g0 + j
                        S.op("act", lambda e, t=t, ps=ps: e.activation(
                            out=t, in_=ps, func=AF.Sigmoid, bias=vecs[:, vo + V_GB + b * 8 + jj:vo + V_GB + b * 8 + jj + 1]),
                            reads=[ps_b, Bvecs], writes=[t_b])
                        sg_sb[jj] = (t, t_b)
                    linear(L["w_in"], D, O_G + b * 1024 + g0 * 128, 512, rh, [BhT], n, cons_g)

                    def cons_b(j, ps, ps_b, b=b, g0=g0):
                        jj = g0 + j
                        t, t_b = sg_sb[jj]
                        if b == 0:
                            S.op("dve", lambda e, t=t, ps=ps: e.tensor_tensor(out=big32[:, jj, :], in0=t, in1=ps, op=ALU.mult),
                                 reads=[t_b, ps_b], writes=[Bbig])
                        else:
                            S.op("dve", lambda e, t=t, ps=ps: e.tensor_tensor(out=t, in0=t, in1=ps, op=ALU.mult),
                                 reads=[t_b, ps_b], writes=[t_b])
                            if b == 1:
                                S.op("pool", lambda e, t=t: e.tensor_tensor(out=big32[:, jj, :], in0=big32[:, jj, :], in1=t,
                                                                            op=ALU.add), reads=[t_b, Bbig], writes=[Bbig])
                            else:
                                S.op("pool", lambda e, t=t: e.tensor_tensor(out=merged[:, jj, :], in0=big32[:, jj, :], in1=t,
                                                                            op=ALU.add), reads=[t_b, Bbig], writes=[Bmerged])
                    linear(Wb, Kb, g0 * 128, 512, rb, [Brb], n, cons_b, kp=kpb)

            def cons_o(j, ps, ps_b):
                S.op("dve", lambda e, ps=ps: e.tensor_tensor(out=xt[:, j, :], in0=xt[:, j, :], in1=ps, op=ALU.add),
                     reads=[ps_b, Bxt], writes=[Bxt])
            linear(w_out_all[l], D, 0, D, lambda kc: merged[:, kc, :], [Bmerged], n, cons_o)

            ffn(T, V_G2)
            if not last:
                S.dma("sp", "xres", xres[:, :, t0:t0 + T], xt, reads=[Bxt], writes=[Bxres[ti]])
                rmsnorm(xt, Bxt, T, NV + V_G1, actT, Bact)
                S.dma("pool", "h1sv", h1sc[:, :, t0:t0 + T], actT[:, 0:8, :], reads=[Bact], writes=[Bh1sc[ti]])
            else:
                final_norm_store(t0)

    Bcc1i = [Buf("cc1i%d" % l) for l in range(depth)]
    Bcc1o = [Buf("cc1o%d" % l) for l in range(depth)]
    Bcc2i = [Buf("cc2i%d" % l) for l in range(depth)]
    Bcc2o = [Buf("cc2o%d" % l) for l in range(depth)]
    Byo = Buf("y_out")

    def p1_phase(l):
        S.op("dve", lambda e: e.memset(state, 0.0), writes=[Bstate])
        if l == 0:
            S.dma("sp", "xt", xt[:, :, :4], xh4_in, writes=[Bxt])
        else:
            S.dma("sp", "xt", xt[:, :, :4], cc2_out[l - 1].ap()[0:128, :].rearrange("p (c n) -> p c n", c=8),
                  reads=[Bcc2o[l - 1]], writes=[Bxt])
            S.op("dve", lambda e: e.tensor_scalar(out=xt[:, :, :4], in0=xt[:, :, :4], scalar1=vecs[:, V_OMF0:V_OMF0 + 1],
                                                  scalar2=None, op0=ALU.mult), reads=[Bxt, Bvecs], writes=[Bxt])
        ffn(4, V_G1)
        rmsnorm(xt, Bxt, 4, V_GM, hT, BhT)
        inproj_xr_hist(4)
        for ti in range(ntiles):
            t0 = ti * T
            if l == 0:
                S.dma("sp", "xt", xt, x_in[:, :, t0:t0 + T], writes=[Bxt])
            else:
                S.dma("sp", "hT", hT, h1sc[:, :, t0:t0 + T], reads=[Bh1sc[ti]], writes=[BhT])
                S.dma("sp", "xt", xt, xres[:, :, t0:t0 + T], reads=[Bxres[ti]], writes=[Bxt])
            ffn(T, V_G1, None if l == 0 else "done")
            S.dma("sp", "xres", xres[:, :, t0:t0 + T], xt, reads=[Bxt], writes=[Bxres[ti]])
            rmsnorm(xt, Bxt, T, V_GM, hT, BhT)
            S.dma("pool", "hsv", hsc[:, :, t0:t0 + T], hT, reads=[BhT], writes=[Bhsc[ti]])
            lru(T, ti == 0, ti)

    def exchange1(l):
        S.dma("sp", "cc1i", cc1_in[l].ap()[:, 0:1024].rearrange("p (c n) -> p c n", c=8), xres[:, :, NT - 128:NT],
              reads=[Bxres[ntiles - 1]], writes=[Bcc1i[l]])
        S.dma("sp", "cc1i", cc1_in[l].ap()[:, 1024:1032], state, reads=[Bstate], writes=[Bcc1i[l]])
        S.coll("cc", lambda e: e.collective_compute("AllGather", ALU.bypass, replica_groups=RG,
                                                    ins=[cc1_in[l].ap().opt()], outs=[cc1_out[l].ap().opt()]),
               reads=[Bcc1i[l]], writes=[Bcc1o[l]])

    def exchange2(l):
        S.dma("sp", "cc2i", cc2_in[l].ap().rearrange("p (c n) -> p c n", c=8), xres[:, :, NT - 4:NT],
              reads=[Bxres[ntiles - 1]], writes=[Bcc2i[l]])
        S.coll("cc", lambda e: e.collective_compute("AllGather", ALU.bypass, replica_groups=RG,
                                                    ins=[cc2_in[l].ap().opt()], outs=[cc2_out[l].ap().opt()]),
               reads=[Bcc2i[l]], writes=[Bcc2o[l]])

    for l in range(depth):
        S.suffix = "_L%d" % l
        L["vo"] = l * NV
        L["w_in"] = w_in_all[l]
        layer_prelude(l)
        L["p23"] = False
        L["wg"], L["wu"], L["wd"] = f1[0][l], f1[1][l], f1[2][l]
        p1_phase(l)
        exchange1(l)
        L["p23"] = True
        L["wg"], L["wu"], L["wd"] = f2[0][l], f2[1][l], f2[2][l]
        p23_prelude(l)
        p23_phase(l, l == depth - 1)
        if l < depth - 1:
            exchange2(l)
    S.finish([Byo])
    S.emit()
    return nc


def _fm(v):
    return np.ascontiguousarray(np.asarray(v, np.float32).reshape(-1, 128).T)


def _to_fm(xtok):
    t = xtok.shape[0]
    return np.ascontiguousarray(xtok.T.reshape(8, 128, t).transpose(1, 0, 2))


def _from_fm(xfm):
    t = xfm.shape[2]
    return np.ascontiguousarray(xfm.transpose(1, 0, 2).reshape(1024, t).T)


def _alibi_tables():
    slopes = np.array([2.0 ** (-8.0 * (i + 1) / 8) for i in range(8)], dtype=np.float32)
    j = np.arange(128)[:, None]
    i = np.arange(128)[None, :]
    tab = np.zeros((128, 2, 2, 4, 128), np.float32)
    for kh in range(2):
        for kb in range(2):
            kpos = kb * 128 + j
            qpos = 128 + i
            dist = np.abs(qpos - kpos).astype(np.float32)
            kch = kpos // 64
            qch = qpos // 64
            valid = (kch >= qch - 2) & (kch <= qch)
            for g in range(4):
                tab[:, kh, kb, g, :] = np.where(valid, -slopes[kh * 4 + g] * dist, np.float32(NEG))
    return tab


_PROGS = {}


def _prog(depth):
    if depth not in _PROGS:
        _PROGS[depth] = build(depth)
    return _PROGS[depth]


def _prep_inputs(x, mem, ffn1_norm, ffn1_w_gate, ffn1_w_up, ffn1_w_down, mix_norm, w_in, gate_bias,
                 attn_sinks, w_attn_out, conv_w, conv_b, lru_wa, lru_ba, lru_wx, lru_bx, lru_lambda,
                 w_lru_out, mem_norm, w_mem_kv, w_mem_out, w_out, ffn2_norm, ffn2_w_gate, ffn2_w_up,
                 ffn2_w_down, final_norm, depth=None):
    f32 = np.float32
    x = np.asarray(x, f32)
    mem = np.asarray(mem, f32)
    if depth is None:
        depth = ffn1_norm.shape[0]
    ncores = 8
    ca = lambda a: np.ascontiguousarray(np.asarray(a, f32)[:depth])
    tab = _alibi_tables()
    tabF_A = tab[:, :, 0].copy()
    tabF_A[...] = NEG
    tabF_B = np.ascontiguousarray(tab[:, :, 0])
    qperm = np.concatenate([np.concatenate([np.arange(j * 64, j * 64 + 64), np.arange((4 + j) * 64, (4 + j) * 64 + 64)])
                            for j in range(4)])
    win = np.array(np.asarray(w_in, f32)[:depth], copy=True)
    win[:, :, :512] = win[:, :, qperm]
    win = np.ascontiguousarray(win)
    wbd = np.zeros((depth, 128, 16, 128), f32)
    for l in range(depth):
        for gi_, wsrc in enumerate((lru_wa[l], lru_wx[l])):
            for c in range(8):
                wbd[l, 0:64, gi_ * 8 + c, 0:64] = wsrc[2 * c]
                wbd[l, 64:128, gi_ * 8 + c, 64:128] = wsrc[2 * c + 1]
    sinks = np.ascontiguousarray(np.repeat(np.asarray(attn_sinks, f32)[:depth], 128, axis=1)[:, None, :])
    shared = {"w_in": win, "wbd": wbd,
              "f1_wg": ca(ffn1_w_gate), "f1_wu": ca(ffn1_w_up), "f1_wd": ca(ffn1_w_down),
              "f2_wg": ca(ffn2_w_gate), "f2_wu": ca(ffn2_w_up), "f2_wd": ca(ffn2_w_down),
              "biasT": tab, "sinks": sinks, "w_attn": ca(w_attn_out), "w_lru": ca(w_lru_out),
              "w_memkv": ca(w_mem_kv), "w_memo": ca(w_mem_out), "w_out": ca(w_out)}
    in_maps = []
    for c in range(ncores):
        b, half = c // 2, c % 2
        v = np.zeros((128, depth * NV), f32)
        for l in range(depth):
            o = l * NV
            v[:, o + V_G1:o + V_G1 + 8] = _fm(ffn1_norm[l])
            v[:, o + V_GM:o + V_GM + 8] = _fm(mix_norm[l])
            v[:, o + V_G2:o + V_G2 + 8] = _fm(ffn2_norm[l])
            for j in range(4):
                v[:, o + V_CW + j * 8:o + V_CW + j * 8 + 8] = _fm(conv_w[l, j])
            v[:, o + V_CB:o + V_CB + 8] = _fm(conv_b[l])
            v[:, o + V_BA:o + V_BA + 8] = _fm(lru_ba[l])
            v[:, o + V_BX:o + V_BX + 8] = _fm(lru_bx[l])
            v[:, o + V_LAM:o + V_LAM + 8] = _fm(lru_lambda[l])
            v[:, o + V_GB:o + V_GB + 24] = _fm(gate_bias[l])
            v[:, o + V_GMEM:o + V_GMEM + 8] = _fm(mem_norm[l])
            v[:, o + V_GFIN:o + V_GFIN + 8] = _fm(final_norm)
            v[:, o + V_F0] = 1.0 if half == 0 else 0.0
            v[:, o + V_OMF0] = 0.0 if half == 0 else 1.0
            v[:, o + V_MASKA] = NEG if half == 0 else 0.0
        xs = _to_fm(x[b, half * NT:(half + 1) * NT, :])
        if half == 0:
            xh4 = np.zeros((128, 8, 4), f32)
        else:
            xh4 = _to_fm(x[b, NT - 4:NT, :])
        m = dict(shared)
        m.update({"x_in": xs, "xh4_in": xh4, "vecs": v,
                  "memT": _to_fm(mem[b])})
        in_maps.append(m)
    return in_maps, depth


def kernel(x, mem, ffn1_norm, ffn1_w_gate, ffn1_w_up, ffn1_w_down, mix_norm, w_in, gate_bias,
           attn_sinks, w_attn_out, conv_w, conv_b, lru_wa, lru_ba, lru_wx, lru_bx, lru_lambda,
           w_lru_out, mem_norm, w_mem_kv, w_mem_out, w_out, ffn2_norm, ffn2_w_gate, ffn2_w_up,
           ffn2_w_down, final_norm):
    in_maps, depth = _prep_inputs(x, mem, ffn1_norm, ffn1_w_gate, ffn1_w_up, ffn1_w_down, mix_norm, w_in,
                                  gate_bias, attn_sinks, w_attn_out, conv_w, conv_b, lru_wa, lru_ba, lru_wx,
                                  lru_bx, lru_lambda, w_lru_out, mem_norm, w_mem_kv, w_mem_out, w_out,
                                  ffn2_norm, ffn2_w_gate, ffn2_w_up, ffn2_w_down, final_norm)
    ncores = 8
    res = run_bass_kernel_spmd(_prog(depth), in_maps, core_ids=list(range(ncores))).results
    B = np.asarray(x).shape[0]
    out = np.empty((B, 2 * NT, D), np.float32)
    for c in range(ncores):
        out[c // 2, (c % 2) * NT:(c % 2 + 1) * NT, :] = _from_fm(np.asarray(res[c]["y_out"], np.float32))
    return out
```
